# Optimizing a Trainium2 kernel written in Bass

```python
import math
import jax, jax.numpy as jnp
from jax import lax
import numpy as np

D_MODEL = 1024
BATCH = 16
SEQ = 2048
DEPTH = 1

HEAD_DIM = 64
ROPE_DIM = HEAD_DIM // 4
ROPE_THETA = 500000.0
Q_BLOCK = 128
NSA_HEADS = 8
NSA_GROUPS = 2
NSA_HPG = NSA_HEADS // NSA_GROUPS
CMP_LEN = 32
CMP_STRIDE = 16
CMP_HIDDEN = 256
SLC_LEN = 64
SLC_TOPK = 8
WINDOW = 512
FORCE_BONUS = 1.0e4
DIFF_HEADS = 4
DIFF_V_DIM = 2 * HEAD_DIM
FFN_HIDDEN = ((8 * D_MODEL // 3 + 255) // 256) * 256
DEEPNORM_ALPHA = (2 * DEPTH) ** 0.25
DEEPNORM_BETA = (8 * DEPTH) ** -0.25
NEG = -1.0e30
LN_EPS = 1e-5
RMS_EPS = 1e-5

NSA_Q = NSA_HEADS * HEAD_DIM
NSA_KV = NSA_GROUPS * HEAD_DIM
DIFF_QK = DIFF_HEADS * 2 * HEAD_DIM
DIFF_V = DIFF_HEADS * DIFF_V_DIM
IN_WIDTHS = (NSA_Q, NSA_KV, NSA_KV, NSA_KV, NSA_KV, NSA_KV, NSA_KV, 3 * NSA_HEADS,
             DIFF_QK, DIFF_QK, DIFF_V, 2 * D_MODEL)
IN_IS_VALUE = (False, False, True, False, True, False, True, False, False, False, True, False)

kernel_name = "hybrid_nsa_diffattn_gated_deepnorm"


def layer_norm(x, g, b):
    xf = x.astype(jnp.float32)
    mu = xf.mean(-1, keepdims=True)
    var = jnp.square(xf - mu).mean(-1, keepdims=True)
    return ((xf - mu) * lax.rsqrt(var + LN_EPS) * g + b).astype(x.dtype)


def rope_partial(t, pos):
    half = ROPE_DIM // 2
    inv_freq = ROPE_THETA ** (-jnp.arange(half, dtype=jnp.float32) * 2.0 / ROPE_DIM)
    ang = pos.astype(jnp.float32)[:, None] * inv_freq[None, :]
    shape = (1, pos.shape[0]) + (1,) * (t.ndim - 3) + (half,)
    cos = jnp.cos(ang).reshape(shape).astype(t.dtype)
    sin = jnp.sin(ang).reshape(shape).astype(t.dtype)
    r1, r2, rest = t[..., :half], t[..., half:ROPE_DIM], t[..., ROPE_DIM:]
    return jnp.concatenate([r1 * cos - r2 * sin, r1 * sin + r2 * cos, rest], axis=-1)


def compress_blocks(kv, pe, w1, b1, w2):
    B, S, G, Dh = kv.shape
    chunks = kv.reshape(B, S // CMP_STRIDE, CMP_STRIDE, G, Dh)
    blocks = jnp.concatenate([chunks[:, :-1], chunks[:, 1:]], axis=2)
    blocks = blocks + pe[None, None, :, None, :]
    nc = blocks.shape[1]
    flat = blocks.transpose(0, 1, 3, 2, 4).reshape(B, nc, G, CMP_LEN * Dh)
    return jax.nn.gelu(flat @ w1 + b1) @ w2


def compressed_attention(q, k_cmp, v_cmp):
    S, Dh = q.shape[1], q.shape[-1]
    nc = k_cmp.shape[1]
    s = jnp.einsum('bsghd,bcgd->bghsc', q, k_cmp).astype(jnp.float32) * (Dh ** -0.5)
    t = jnp.arange(S)
    c_end = jnp.arange(nc) * CMP_STRIDE + CMP_LEN - 1
    mask = c_end[None, :] <= t[:, None]
    s = jnp.where(mask, s, NEG)
    p = jax.nn.softmax(s, axis=-1)
    p = jnp.where((t >= CMP_LEN - 1)[:, None], p, 0.0)
    out = jnp.einsum('bghsc,bcgd->bsghd', p.astype(v_cmp.dtype), v_cmp)
    return out, p


def select_blocks(probs, S):
    nc = probs.shape[-1]
    nsb = S // SLC_LEN
    k = min(SLC_TOPK, nsb)
    c_start = jnp.arange(nc) * CMP_STRIDE
    s_start = jnp.arange(nsb) * SLC_LEN
    overlap = jnp.clip(jnp.minimum(c_start[:, None] + CMP_LEN, s_start[None, :] + SLC_LEN)
                       - jnp.maximum(c_start[:, None], s_start[None, :]), 0, None)
    overlap = overlap.astype(jnp.float32) / CMP_LEN
    p_slc = jnp.einsum('bghsc,cj->bgsj', probs, overlap)
    t_blk = jnp.arange(S)[:, None] // SLC_LEN
    j = jnp.arange(nsb)[None, :]
    valid = j <= t_blk
    forced = (j == 0) | (j == t_blk) | (j == t_blk - 1)
    priority = jnp.where(valid, p_slc + jnp.where(forced, FORCE_BONUS, 0.0), -1.0)
    _, idx = lax.top_k(priority, k)
    return idx


def selected_attention(q, k_slc, v_slc, idx):
    B, S, G, HPG, Dh = q.shape
    nsb = S // SLC_LEN
    kk = idx.shape[-1]
    n_blk = S // Q_BLOCK
    kb = k_slc.reshape(B, nsb, SLC_LEN, G, Dh).transpose(0, 3, 1, 2, 4)
    vb = v_slc.reshape(B, nsb, SLC_LEN, G, Dh).transpose(0, 3, 1, 2, 4)
    qb = q.reshape(B, n_blk, Q_BLOCK, G, HPG, Dh).transpose(1, 0, 3, 4, 2, 5)
    ib = idx.reshape(B, G, n_blk, Q_BLOCK, kk).transpose(2, 0, 1, 3, 4)
    gather = jax.vmap(jax.vmap(lambda blocks, ids: blocks[ids]))
    offs = jnp.arange(SLC_LEN)

    def one(args):
        qi, ii, n = args
        kg = gather(kb, ii).reshape(B, G, Q_BLOCK, kk * SLC_LEN, Dh)
        vg = gather(vb, ii).reshape(B, G, Q_BLOCK, kk * SLC_LEN, Dh)
        t = n * Q_BLOCK + jnp.arange(Q_BLOCK)
        kpos = (ii[..., None] * SLC_LEN + offs).reshape(B, G, Q_BLOCK, kk * SLC_LEN)
        mask = kpos <= t[None, None, :, None]
        s = jnp.einsum('bghqd,bgqkd->bghqk', qi, kg).astype(jnp.float32) * (Dh ** -0.5)
        s = jnp.where(mask[:, :, None], s, NEG)
        p = jax.nn.softmax(s, axis=-1).astype(vg.dtype)
        return jnp.einsum('bghqk,bgqkd->bghqd', p, vg)

    out = lax.map(one, (qb, ib, jnp.arange(n_blk)))
    return out.transpose(1, 0, 4, 2, 3, 5).reshape(B, S, G, HPG, Dh)


def window_attention(q, k, v):
    B, S, G, HPG, Dh = q.shape
    n_blk = S // Q_BLOCK
    span = WINDOW + Q_BLOCK
    kp = jnp.pad(k, ((0, 0), (WINDOW, 0), (0, 0), (0, 0)))
    vp = jnp.pad(v, ((0, 0), (WINDOW, 0), (0, 0), (0, 0)))

    def one(n):
        start = n * Q_BLOCK
        qi = lax.dynamic_slice_in_dim(q, start, Q_BLOCK, axis=1)
        ki = lax.dynamic_slice_in_dim(kp, start, span, axis=1)
        vi = lax.dynamic_slice_in_dim(vp, start, span, axis=1)
        t = start + jnp.arange(Q_BLOCK)
        s_pos = start - WINDOW + jnp.arange(span)
        dist = t[:, None] - s_pos[None, :]
        mask = (dist >= 0) & (dist < WINDOW) & (s_pos[None, :] >= 0)
        s = jnp.einsum('bqghd,bkgd->bghqk', qi, ki).astype(jnp.float32) * (Dh ** -0.5)
        s = jnp.where(mask, s, NEG)
        p = jax.nn.softmax(s, axis=-1).astype(vi.dtype)
        return jnp.einsum('bghqk,bkgd->bqghd', p, vi)

    out = lax.map(one, jnp.arange(n_blk))
    return out.transpose(1, 0, 2, 3, 4, 5).reshape(B, S, G, HPG, Dh)


def diff_attention(q, k, v, lam_params, lambda_init, norm_g):
    B, S, H, _, Dh = q.shape
    n_blk = S // Q_BLOCK
    lp = lam_params.astype(jnp.float32)
    lam = jnp.exp(jnp.sum(lp[0] * lp[1])) - jnp.exp(jnp.sum(lp[2] * lp[3])) + lambda_init
    k_pos = jnp.arange(S)

    def one(n):
        start = n * Q_BLOCK
        qi = lax.dynamic_slice_in_dim(q, start, Q_BLOCK, axis=1)
        s = jnp.einsum('bqhcd,bkhcd->bhcqk', qi, k).astype(jnp.float32) * (Dh ** -0.5)
        t = start + jnp.arange(Q_BLOCK)
        s = jnp.where(k_pos[None, :] <= t[:, None], s, NEG)
        p = jax.nn.softmax(s, axis=-1)
        a = p[:, :, 0] - lam * p[:, :, 1]
        return jnp.einsum('bhqk,bkhe->bqhe', a.astype(v.dtype), v)

    out = lax.map(one, jnp.arange(n_blk))
    out = out.transpose(1, 0, 2, 3, 4).reshape(B, S, H, 2 * Dh)
    of = out.astype(jnp.float32)
    of = of * lax.rsqrt(jnp.mean(of * of, axis=-1, keepdims=True) + RMS_EPS) * norm_g
    return (of * (1.0 - lambda_init)).astype(v.dtype)


def hybrid_mixer(x, w_in, cmp_pe, cmp_w1, cmp_b1, cmp_w2, diff_lambda, diff_norm_g,
                 w_branch_a, w_branch_b, w_o, lambda_init):
    B, S, D = x.shape
    pos = jnp.arange(S)
    proj = x @ w_in
    split_at = [int(c) for c in np.cumsum(IN_WIDTHS)[:-1]]
    (q_nsa, k_c, v_c, k_s, v_s, k_w, v_w, g_nsa,
     q_df, k_df, v_df, g_merge) = jnp.split(proj, split_at, axis=-1)

    q_nsa = q_nsa.reshape(B, S, NSA_GROUPS, NSA_HPG, HEAD_DIM)
    q_rot = rope_partial(q_nsa, pos)
    kv_shape = (B, S, NSA_GROUPS, HEAD_DIM)
    k_cmp = compress_blocks(k_c.reshape(kv_shape), cmp_pe[0], cmp_w1[0], cmp_b1[0], cmp_w2[0])
    v_cmp = compress_blocks(v_c.reshape(kv_shape), cmp_pe[1], cmp_w1[1], cmp_b1[1], cmp_w2[1])
    o_cmp, probs = compressed_attention(q_nsa, k_cmp, v_cmp)
    idx = select_blocks(probs, S)
    o_slc = selected_attention(q_rot, rope_partial(k_s.reshape(kv_shape), pos), v_s.reshape(kv_shape), idx)
    o_win = window_attention(q_rot, rope_partial(k_w.reshape(kv_shape), pos), v_w.reshape(kv_shape))
    gates = jax.nn.sigmoid(g_nsa.reshape(B, S, NSA_GROUPS, NSA_HPG, 3))
    y_a = (gates[..., 0:1] * o_cmp + gates[..., 1:2] * o_slc + gates[..., 2:3] * o_win).reshape(B, S, NSA_Q)

    q_df = rope_partial(q_df.reshape(B, S, DIFF_HEADS, 2, HEAD_DIM), pos)
    k_df = rope_partial(k_df.reshape(B, S, DIFF_HEADS, 2, HEAD_DIM), pos)
    v_df = v_df.reshape(B, S, DIFF_HEADS, DIFF_V_DIM)
    y_b = diff_attention(q_df, k_df, v_df, diff_lambda, lambda_init, diff_norm_g).reshape(B, S, DIFF_V)

    gm = jax.nn.sigmoid(g_merge).reshape(B, S, 2, D)
    merged = gm[:, :, 0] * (y_a @ w_branch_a) + gm[:, :, 1] * (y_b @ w_branch_b)
    return merged @ w_o


def swiglu(x, w_gate_up, w_down):
    g, u = jnp.split(x @ w_gate_up, 2, axis=-1)
    return (jax.nn.silu(g) * u) @ w_down


def setup_inputs(seed: int = 0) -> dict:
    key = jax.random.key(seed)
    ks = jax.random.split(key, 20)
    D = D_MODEL
    beta = DEEPNORM_BETA

    def nrm(k, shape, scale):
        return jax.random.normal(k, shape, jnp.float32) * scale

    x = nrm(ks[0], (BATCH, SEQ, D), 1.0)
    in_keys = jax.random.split(ks[1], len(IN_WIDTHS))
    w_in = jnp.concatenate(
        [nrm(kk, (DEPTH, D, w), D ** -0.5 * (beta if is_v else 1.0))
         for kk, w, is_v in zip(in_keys, IN_WIDTHS, IN_IS_VALUE)], axis=-1)
    cmp_pe = nrm(ks[2], (DEPTH, 2, CMP_LEN, HEAD_DIM), 0.1)
    cmp_w1 = nrm(ks[3], (DEPTH, 2, CMP_LEN * HEAD_DIM, CMP_HIDDEN), (CMP_LEN * HEAD_DIM) ** -0.5)
    cmp_b1 = nrm(ks[4], (DEPTH, 2, CMP_HIDDEN), 0.01)
    cmp_w2 = nrm(ks[5], (DEPTH, 2, CMP_HIDDEN, HEAD_DIM), CMP_HIDDEN ** -0.5)
    diff_lambda = nrm(ks[6], (DEPTH, 4, HEAD_DIM), 0.1)
    diff_norm_g = 1.0 + nrm(ks[7], (DEPTH, DIFF_V_DIM), 0.01)
    w_branch_a = nrm(ks[8], (DEPTH, NSA_Q, D), NSA_Q ** -0.5 * beta)
    w_branch_b = nrm(ks[9], (DEPTH, DIFF_V, D), DIFF_V ** -0.5 * beta)
    w_o = nrm(ks[10], (DEPTH, D, D), D ** -0.5 * beta)
    ln1_g = 1.0 + nrm(ks[11], (DEPTH, D), 0.01)
    ln1_b = nrm(ks[12], (DEPTH, D), 0.01)
    w_gate_up = nrm(ks[13], (DEPTH, D, 2 * FFN_HIDDEN), D ** -0.5 * beta)
    w_down = nrm(ks[14], (DEPTH, FFN_HIDDEN, D), FFN_HIDDEN ** -0.5 * beta)
    ln2_g = 1.0 + nrm(ks[15], (DEPTH, D), 0.01)
    ln2_b = nrm(ks[16], (DEPTH, D), 0.01)
    return {"x": x, "w_in": w_in, "cmp_pe": cmp_pe, "cmp_w1": cmp_w1, "cmp_b1": cmp_b1,
            "cmp_w2": cmp_w2, "diff_lambda": diff_lambda, "diff_norm_g": diff_norm_g,
            "w_branch_a": w_branch_a, "w_branch_b": w_branch_b, "w_o": w_o,
            "ln1_g": ln1_g, "ln1_b": ln1_b, "w_gate_up": w_gate_up, "w_down": w_down,
            "ln2_g": ln2_g, "ln2_b": ln2_b}


def reference(x, w_in, cmp_pe, cmp_w1, cmp_b1, cmp_w2, diff_lambda, diff_norm_g,
              w_branch_a, w_branch_b, w_o, ln1_g, ln1_b, w_gate_up, w_down, ln2_g, ln2_b):
    h = x
    for l in range(DEPTH):
        lambda_init = 0.8 - 0.6 * math.exp(-0.3 * l)
        mix = hybrid_mixer(h, w_in[l], cmp_pe[l], cmp_w1[l], cmp_b1[l], cmp_w2[l],
                           diff_lambda[l], diff_norm_g[l], w_branch_a[l], w_branch_b[l],
                           w_o[l], lambda_init)
        h = layer_norm(DEEPNORM_ALPHA * h + mix, ln1_g[l], ln1_b[l])
        h = layer_norm(DEEPNORM_ALPHA * h + swiglu(h, w_gate_up[l], w_down[l]), ln2_g[l], ln2_b[l])
    return h
```

```python
import math
from contextlib import ExitStack

import numpy as np
import concourse.bass as bass
import concourse.mybir as mybir
from concourse.bass_utils import run_bass_kernel_spmd

F32 = mybir.dt.float32
BF16 = mybir.dt.bfloat16
AF = mybir.ActivationFunctionType
ALU = mybir.AluOpType
AX = mybir.AxisListType

SEQ = 2048
DM = 1024
NTT = 16
NQ = 4
NSEQ = 2
NCORES = 8
BIG = -30000.0
ALPHA = 2.0 ** 0.25
LAMBDA_INIT = 0.8 - 0.6 * math.exp(0.0)
LN_EPS = 1e-5
RMS_EPS = 1e-5
FFN_H = 2816
NHC = 22
EPOCH = 12000


class Sched:
    def __init__(self, nc, es):
        self.nc = nc
        self.es = es
        self.eng = {"pe": nc.tensor, "act": nc.scalar, "dve": nc.vector,
                    "pool": nc.gpsimd, "sp": nc.sync}
        self.sems = {}
        self.cnt = {}
        self.epoch = {e: 0 for e in self.eng}
        self.waited = {}
        self.last_w = {}
        self.readers = {}
        self.n_wait = 0
        self.n_ins = 0

    def _sem(self, key):
        if key not in self.sems:
            self.sems[key] = self.es.enter_context(
                self.nc.semaphore("s_" + "_".join(str(k) for k in key)))
            self.cnt[key] = 0
        return self.sems[key]

    def _engkey(self, e):
        key = (e, self.epoch[e])
        self._sem(key)
        if self.cnt[key] >= EPOCH:
            self.epoch[e] += 1
            key = (e, self.epoch[e])
            self._sem(key)
        return key

    def _deps(self, reads, writes):
        deps = {}

        def add(tok):
            if tok is None:
                return
            k, v = tok
            if deps.get(k, 0) < v:
                deps[k] = v
        for k in reads:
            add(self.last_w.get(k))
        for k in writes:
            add(self.last_w.get(k))
            for t in self.readers.get(k, ()):
                add(t)
        return deps

    def _emit_waits(self, e, deps, skip_self=False):
        for k, v in deps.items():
            if skip_self and k[0] == e:
                continue
            if self.waited.get((e, k), 0) >= v:
                continue
            self.eng[e].wait_ge(self.sems[k], v)
            self.waited[(e, k)] = v
            self.n_wait += 1

    def _record(self, tok, reads, writes):
        for k in writes:
            self.last_w[k] = tok
            self.readers[k] = []
        for k in reads:
            self.readers.setdefault(k, []).append(tok)

    def op(self, e, fn, reads=(), writes=()):
        deps = self._deps(reads, writes)
        self._emit_waits(e, deps, skip_self=(e == "pe"))
        key = self._engkey(e)
        ins = fn()
        self.cnt[key] += 1
        ins.then_inc(self.sems[key], 1)
        self.n_ins += 1
        self._record((key, self.cnt[key]), reads, writes)

    def group(self, e, fns, reads=(), writes=()):
        deps = self._deps(reads, writes)
        self._emit_waits(e, deps, skip_self=(e == "pe"))
        key = self._engkey(e)
        ins = None
        for fn in fns:
            ins = fn()
            self.n_ins += 1
        self.cnt[key] += 1
        ins.then_inc(self.sems[key], 1)
        self._record((key, self.cnt[key]), reads, writes)

    def dma(self, q, out, in_, reads=(), writes=(), sem="d0", **kw):
        deps = self._deps(reads, writes)
        self._emit_waits(q, deps)
        key = ("dma", sem)
        self._sem(key)
        if q == "pool":
            kw.setdefault("max_dma_last_dim", 2048)
        ins = self.eng[q].dma_start(out=out, in_=in_, **kw)
        self.cnt[key] += 16
        ins.then_inc(self.sems[key], 16)
        self.n_ins += 1
        self._record((key, self.cnt[key]), reads, writes)

    def barrier(self, engines=("pe", "act", "dve", "pool", "sp")):
        deps = {k: v for k, v in self.cnt.items() if v > 0}
        for e in engines:
            self._emit_waits(e, deps)
        self.last_w = {}
        self.readers = {}


class Region:
    def __init__(self, ranges):
        self.ranges = [[a * 1024, b * 1024] for a, b in ranges]

    def alloc(self, nbytes):
        for r in self.ranges:
            off = (r[0] + 63) // 64 * 64
            if off + nbytes <= r[1]:
                r[0] = off + nbytes
                return off
        raise RuntimeError(f"region full: need {nbytes}, ranges {self.ranges}")

    def __enter__(self):
        return self

    def __exit__(self, *a):
        return False

    def close(self):
        pass


ARENA_BYTES = 203 * 1024


OFF = {}
_o = 0
for _n, _w in (("q_nsa", 512), ("k_c", 128), ("v_c", 128), ("k_s", 128), ("v_s", 128), ("k_w", 128),
               ("v_w", 128), ("g_nsa", 24), ("q_df", 512), ("k_df", 512), ("v_df", 512), ("g_m", 2048)):
    OFF[_n] = _o
    _o += _w


def _swap_cols(base, n):
    idx = []
    for c in range(n):
        h, d = divmod(c, 64)
        if d < 8:
            d2 = d + 8
        elif d < 16:
            d2 = d - 8
        else:
            d2 = d
        idx.append(base + h * 64 + d2)
    return idx


def _kmajor(w):
    K, N = w.shape
    return np.ascontiguousarray(w.reshape(K // 128, 128, N).transpose(1, 0, 2))


def prep_shared(inp):
    w_in = np.asarray(inp["w_in"], np.float32)[0]
    out = {}
    cols = []
    cols += list(range(OFF["q_nsa"], OFF["q_nsa"] + 512))
    cols += _swap_cols(OFF["q_nsa"], 512)
    cols += list(range(OFF["k_s"], OFF["k_s"] + 128))
    cols += _swap_cols(OFF["k_s"], 128)
    cols += list(range(OFF["k_w"], OFF["k_w"] + 128))
    cols += _swap_cols(OFF["k_w"], 128)
    cols += list(range(OFF["k_c"], OFF["k_c"] + 128))
    cols += list(range(OFF["v_c"], OFF["v_c"] + 128))
    out["wn"] = _kmajor(w_in[:, cols])
    cols = list(range(OFF["v_s"], OFF["v_s"] + 128)) + list(range(OFF["v_w"], OFF["v_w"] + 128))
    out["wnv"] = _kmajor(w_in[:, cols])
    out["wg"] = _kmajor(w_in[:, OFF["g_nsa"]:OFF["g_nsa"] + 24])
    cols = []
    cols += list(range(OFF["q_df"], OFF["q_df"] + 512))
    cols += _swap_cols(OFF["q_df"], 512)
    cols += list(range(OFF["k_df"], OFF["k_df"] + 512))
    cols += _swap_cols(OFF["k_df"], 512)
    out["wd"] = _kmajor(w_in[:, cols])
    out["wdv"] = _kmajor(w_in[:, OFF["v_df"]:OFF["v_df"] + 512])
    out["wgm"] = _kmajor(w_in[:, OFF["g_m"]:OFF["g_m"] + 2048])
    out["wa"] = _kmajor(np.asarray(inp["w_branch_a"], np.float32)[0])
    out["wb"] = _kmajor(np.asarray(inp["w_branch_b"], np.float32)[0])
    out["wo"] = _kmajor(np.asarray(inp["w_o"], np.float32)[0])
    wgu = np.asarray(inp["w_gate_up"], np.float32)[0]
    g = wgu[:, :FFN_H].reshape(DM, NHC, 128)
    u = wgu[:, FFN_H:].reshape(DM, NHC, 128)
    gu = np.concatenate([g, u], axis=2).reshape(DM, NHC * 256)
    out["wgu"] = _kmajor(gu)
    out["wdn"] = _kmajor(np.asarray(inp["w_down"], np.float32)[0])
    w1 = np.asarray(inp["cmp_w1"], np.float32)[0]
    w1r = w1.reshape(2, 32, 64, 256).transpose(0, 2, 1, 3)
    out["w1"] = np.ascontiguousarray(np.concatenate([w1r, w1r], axis=1))
    w2 = np.asarray(inp["cmp_w2"], np.float32)[0]
    w2k = np.concatenate([w2[0], w2[0]], axis=1)
    out["w2k"] = _kmajor(w2k)
    out["w2v"] = _kmajor(w2[1])
    b1 = np.asarray(inp["cmp_b1"], np.float32)[0]
    out["b1"] = np.ascontiguousarray(b1.reshape(2, 2, 128).transpose(2, 0, 1).reshape(128, 4))
    pe = np.asarray(inp["cmp_pe"], np.float32)[0]
    pet = pe.transpose(2, 0, 1)
    out["pet"] = np.ascontiguousarray(np.concatenate([pet, pet], axis=0).reshape(128, 64))
    dl = np.asarray(inp["diff_lambda"], np.float32)[0].reshape(1, 256)
    out["dl"] = np.ascontiguousarray(np.broadcast_to(dl, (128, 256)))
    out["ng"] = np.ascontiguousarray(np.asarray(inp["diff_norm_g"], np.float32)[0].reshape(128, 1))
    for n in ("ln1_g", "ln1_b", "ln2_g", "ln2_b"):
        v = np.asarray(inp[n], np.float32)[0].reshape(1, DM)
        out[n] = np.ascontiguousarray(np.broadcast_to(v, (128, DM)))
    out.update(make_consts())
    return out


_CONSTS = None


def make_consts():
    global _CONSTS
    if _CONSTS is not None:
        return _CONSTS
    c = {}
    c["ident"] = np.eye(128, dtype=np.float32)
    c["ones"] = np.ones((128, 128), np.float32)
    half = 8
    inv = (500000.0 ** (-(np.arange(half, dtype=np.float32) * 2.0 / 16.0))).astype(np.float32)
    ang = np.arange(SEQ, dtype=np.float32)[:, None] * inv[None, :]
    cos = np.cos(ang).astype(np.float32).T
    sin = np.sin(ang).astype(np.float32).T
    rc = np.ones((128, SEQ), np.float32)
    rs = np.zeros((128, SEQ), np.float32)
    for base in (0, 64):
        rc[base:base + 8] = cos
        rc[base + 8:base + 16] = cos
        rs[base:base + 8] = -sin
        rs[base + 8:base + 16] = sin
    c["ropec"] = rc
    c["ropes"] = rs
    k = np.arange(128)[:, None]
    q = np.arange(128)[None, :]
    tri = np.zeros((128, 256), np.float32)
    tri[:, 0:128] = np.where(k <= q, 0.0, BIG)
    tri[:, 128:256] = np.where(q < k, 0.0, BIG)
    c["tri"] = tri
    cm = np.full((128, SEQ), BIG, np.float32)
    cc = np.arange(127)[:, None]
    t = np.arange(SEQ)[None, :]
    cm[:127] = np.where(cc * 16 + 31 <= t, 0.0, BIG)
    c["cmask"] = cm
    e = np.zeros((32, SEQ), np.float32)
    e[np.arange(SEQ) // 64, np.arange(SEQ)] = 1.0
    c["efull"] = e
    ov = np.zeros((128, 33), np.float32)
    cs = np.arange(127)[:, None] * 16
    ss = np.arange(32)[None, :] * 64
    ovl = np.clip(np.minimum(cs + 32, ss + 64) - np.maximum(cs, ss), 0, None).astype(np.float32) / 32.0
    ov[:127, :32] = ovl
    ov[:127, 32] = 1.0
    c["ovaug"] = ov
    valid = np.zeros((128, 16, 32), np.float32)
    addc = np.zeros((128, 16, 32), np.float32)
    for n in range(16):
        tt = n * 128 + np.arange(128)
        tb = (tt // 64)[:, None]
        j = np.arange(32)[None, :]
        v = j <= tb
        f = (j == 0) | (j == tb) | (j == tb - 1)
        valid[:, n, :] = v
        addc[:, n, :] = np.where(v, np.where(f, 1.0e4, 0.0), -1.0)
    c["selvalid"] = valid.reshape(128, 512)
    c["seladdc"] = addc.reshape(128, 512)
    sg = np.zeros((128, 24, 128), np.float32)
    for i in range(24):
        sg[i, i, :] = 1.0
    c["selg"] = sg.reshape(128, 24 * 128)
    _CONSTS = c
    return c


def make_xT(x_core):
    xc = np.asarray(x_core, np.float32).reshape(NSEQ, SEQ, DM // 128, 128)
    return np.ascontiguousarray(xc.transpose(0, 3, 2, 1))


IN_SHAPES = {
    "x": [NSEQ * SEQ, DM], "xT": [NSEQ, 128, DM // 128, SEQ],
    "wn": [128, 8, 14 * 128], "wnv": [128, 8, 256], "wg": [128, 8, 24],
    "wd": [128, 8, 16 * 128], "wdv": [128, 8, 512], "wgm": [128, 8, 2048],
    "wa": [128, 4, 1024], "wb": [128, 4, 1024], "wo": [128, 8, 1024],
    "wgu": [128, 8, NHC * 256], "wdn": [128, NHC, 1024],
    "w1": [2, 128, 32, 256], "w2k": [128, 2, 128], "w2v": [128, 2, 64], "b1": [128, 4], "pet": [128, 64],
    "dl": [128, 256], "ng": [128, 1],
    "ln1_g": [128, DM], "ln1_b": [128, DM], "ln2_g": [128, DM], "ln2_b": [128, DM],
    "ident": [128, 128], "ones": [128, 128], "ropec": [128, SEQ], "ropes": [128, SEQ], "tri": [128, 256],
    "cmask": [128, SEQ], "efull": [32, SEQ], "ovaug": [128, 33], "selvalid": [128, 512],
    "seladdc": [128, 512], "selg": [128, 24 * 128],
}


def build_nc(debug=False, nseq=NSEQ):
    nc = bass.Bass("TRN2", target_bir_lowering=False)
    D = {n: nc.dram_tensor(n, shp, F32, kind="ExternalInput").ap() for n, shp in IN_SHAPES.items()}
    out_d = nc.dram_tensor("out", [NSEQ * SEQ, DM], F32, kind="ExternalOutput").ap()
    dbg = {}
    if debug:
        for n, shp in (("d_ya", [128, 4 * SEQ]), ("d_yb", [128, 4 * SEQ]), ("d_mt", [128, 8 * SEQ]),
                       ("d_z2", [128, NTT * DM]), ("d_t0", [128, SEQ]), ("d_qu0", [128, SEQ]),
                       ("d_ke0", [128, SEQ]), ("d_kcmp", [128, 128]), ("d_vcmp", [128, 128]),
                       ("d_t0b", [128, SEQ])):
            dbg[n] = nc.dram_tensor(n, shp, F32, kind="ExternalOutput").ap()

    with ExitStack() as es:
        S = Sched(nc, es)
        uid = [0]

        arena = es.enter_context(nc.sbuf_tensor("arena", [128, ARENA_BYTES // 2], BF16))

        def sbuf(region, shape, dt, name=None):
            nel = 1
            for d_ in shape[1:]:
                nel *= d_
            esz = 4 if dt == F32 else 2
            off = region.alloc(nel * esz)
            v = arena[0:shape[0], off // 2: off // 2 + nel * esz // 2]
            if dt == F32:
                v = v.bitcast(F32)
            if len(shape) == 3:
                v = v.rearrange("p (a b) -> p a b", a=shape[1])
            elif len(shape) == 4:
                v = v.rearrange("p (a b c) -> p a b c", a=shape[1], b=shape[2])
            return v

        GREG = Region([(0, 4)])

        ps = es.enter_context(nc.psum_tensor("ps", [128, 8, 512], F32))

        def pb(i):
            return ps[:, i, :]

        def pbf(i):
            return ps[:, i, :].bitcast(BF16)

        def pk(i):
            return ("ps", i)

        rot_state = {"banks": [0, 1, 2, 3], "i": 0, "pairs": [], "pi": 0}

        def rotpair():
            p = rot_state["pairs"][rot_state["pi"] % len(rot_state["pairs"])]
            rot_state["pi"] += 1
            return p

        def rot():
            if not rot_state["banks"]:
                return rotpair()
            b = rot_state["banks"][rot_state["i"] % len(rot_state["banks"])]
            rot_state["i"] += 1
            return b

        def set_rot(banks, pairs=()):
            rot_state["banks"] = list(banks)
            rot_state["i"] = 0
            rot_state["pairs"] = list(pairs)
            rot_state["pi"] = 0

        def V(fn, r=(), w=()):
            S.op("dve", fn, r, w)

        def A(fn, r=(), w=()):
            S.op("act", fn, r, w)

        def P(fn, r=(), w=()):
            S.op("pool", fn, r, w)

        def MM(fns, r=(), w=()):
            S.group("pe", fns, r, w)

        def mm(out, lhsT, rhs, start, stop):
            return lambda: nc.tensor.matmul(out, lhsT=lhsT, rhs=rhs, start=start, stop=stop)

        dcount = [0]

        def load(stack, name, shape, dt, src, q=None, key=None):
            t = sbuf(stack, shape, dt, name)
            dcount[0] += 1
            qq = q or ("pool" if dt == BF16 else "sp")
            S.dma(qq, t[:], src, writes=[key or name], sem=f"ld_{name}")
            return t

        def dump(name, ap_sb, key, shape, is_bf=True):
            if not debug or name == "d_mt":
                return
            S.barrier()
            stg = sbuf(Region([(195, 203)]), [128, 2048], F32, "dbgstage")
            ncol = shape[1]
            for c0 in range(0, ncol, 2048):
                c1 = min(ncol, c0 + 2048)
                V(lambda: nc.vector.tensor_copy(out=stg[0:shape[0], 0:c1 - c0], in_=ap_sb[:, c0:c1]), w=["dbgstg"])
                S.dma("sp", dbg[name][:, c0:c1], stg[0:shape[0], 0:c1 - c0], reads=["dbgstg"], sem="dbg")
            S.barrier()

        ident = load(GREG, "ident", [128, 128], BF16, D["ident"])
        ones = load(GREG, "ones", [128, 128], BF16, D["ones"])
        tri = load(GREG, "tri", [128, 256], BF16, D["tri"])
        eps_ln = sbuf(GREG, [128, 1], F32, "eps_ln")
        V(lambda: nc.vector.memset(eps_ln[:], LN_EPS), w=["eps_ln"])
        tiny_c = sbuf(GREG, [128, 1], F32, "tiny_c")
        V(lambda: nc.vector.memset(tiny_c[:], 1e-18), w=["tiny_c"])
        zero_c = sbuf(GREG, [128, 1], F32, "zero_c")
        V(lambda: nc.vector.memset(zero_c[:], 0.0), w=["tiny_c"])
        eps_rms = sbuf(GREG, [128, 1], F32, "eps_rms")
        V(lambda: nc.vector.memset(eps_rms[:], RMS_EPS), w=["eps_rms"])

        neglam = sbuf(GREG, [128, 1], F32, "neglam")
        gcol = sbuf(GREG, [128, 1], F32, "gcol")
        with Region([(153, 195)]) as st:
            dl = load(st, "dl", [128, 256], F32, D["dl"])
            ng = load(st, "ng", [128, 1], F32, D["ng"])
            tmp = sbuf(st, [128, 128], F32, "dltmp")
            s12 = sbuf(st, [128, 2], F32, "s12")
            e12 = sbuf(st, [128, 2], F32, "e12")
            V(lambda: nc.vector.tensor_tensor(out=tmp[:, 0:64], in0=dl[:, 0:64], in1=dl[:, 64:128], op=ALU.mult),
              r=["dl"], w=["dltmp"])
            V(lambda: nc.vector.tensor_tensor(out=tmp[:, 64:128], in0=dl[:, 128:192], in1=dl[:, 192:256],
                                              op=ALU.mult), r=["dl"], w=["dltmp"])
            V(lambda: nc.vector.reduce_sum(out=s12[:, 0:1], in_=tmp[:, 0:64], axis=AX.X), r=["dltmp"], w=["s12"])
            V(lambda: nc.vector.reduce_sum(out=s12[:, 1:2], in_=tmp[:, 64:128], axis=AX.X), r=["dltmp"], w=["s12"])
            A(lambda: nc.scalar.activation(out=e12[:], in_=s12[:], func=AF.Exp), r=["s12"], w=["e12"])
            V(lambda: nc.vector.tensor_tensor(out=neglam[:], in0=e12[:, 1:2], in1=e12[:, 0:1], op=ALU.subtract),
              r=["e12"], w=["neglam"])
            V(lambda: nc.vector.tensor_scalar(out=neglam[:], in0=neglam[:], scalar1=-LAMBDA_INIT, scalar2=None,
                                              op0=ALU.add), r=["neglam"], w=["neglam"])
            V(lambda: nc.vector.tensor_scalar(out=gcol[:], in0=ng[:], scalar1=1.0 - LAMBDA_INIT, scalar2=None,
                                              op0=ALU.mult), r=["ng"], w=["gcol"])
            S.barrier()

        class Pipe:
            def __init__(self, la):
                self.dq = []
                self.la = la

            def npv(self):
                return sum(1 for k, _ in self.dq if k == "pv")

            def push(self, qk, pv):
                qk()
                self.dq.append(("pv", pv))
                while self.npv() > self.la:
                    while True:
                        k, fn = self.dq.pop(0)
                        fn()
                        if k == "pv":
                            break

            def post(self, fn):
                self.dq.append(("post", fn))

            def drain(self):
                while self.dq:
                    k, fn = self.dq.pop(0)
                    fn()

        pipe = Pipe(1)
        filler = [0]

        def attn_pipe(blocks, kT, qT, vA, rkeys, acc, PT, pti, nrow=128, den_acc=None, LA=1):
            nb = len(blocks)
            steps = [list(range(i, min(i + 2, nb))) for i in range(0, nb, 2)]
            slots = {}
            sbank = {}

            def emit_qk(si):
                st_ = steps[si]
                bp = rotpair()
                fns = []
                for j, bi in enumerate(st_):
                    kb, c0, c1, mask, m0 = blocks[bi]
                    fns.append(mm(pb(bp + j)[0:nrow, c0:c1], kT(kb), qT(c0, c1), True, mask is None))
                    if mask == "causal":
                        fns.append(mm(pb(bp + j)[0:nrow, m0:m0 + 128], ident[:, :], tri[:, 0:128], False, True))
                    elif mask == "anti":
                        fns.append(mm(pb(bp + j)[0:nrow, m0:m0 + 128], ident[:, :], tri[:, 128:256], False, True))
                bkeys = [pk(bp + j) for j in range(len(st_))]
                MM(fns, r=list(rkeys), w=bkeys)
                i = pti[0] % len(PT)
                pti[0] += 1
                pt = PT[i]
                rngs = [blocks[bi][1:3] for bi in st_]
                if len(st_) == 2 and rngs[0] == rngs[1]:
                    c0, c1 = rngs[0]
                    A(lambda: nc.scalar.activation(out=pt[0:nrow, 0:2, c0:c1], in_=ps[0:nrow, bp:bp + 2, c0:c1],
                                                   func=AF.Exp, scale=0.125), w=bkeys + [("PT", i)])
                else:
                    for j, (c0, c1) in enumerate(rngs):
                        A(lambda: nc.scalar.activation(out=pt[0:nrow, j, c0:c1], in_=pb(bp + j)[0:nrow, c0:c1],
                                                       func=AF.Exp, scale=0.125), w=[pk(bp + j), ("PT", i)])
                slots[si] = i
                sbank[si] = bp

            def emit_pv(si):
                i = slots.pop(si)
                pt = PT[i]
                fns = []
                for j, bi in enumerate(steps[si]):
                    kb, c0, c1, mask, m0 = blocks[bi]
                    fns.append(mm(pb(acc)[:, c0:c1], vA(kb), pt[0:nrow, j, c0:c1], bi == 0, bi == nb - 1))
                MM(fns, r=[("PT", i)] + list(rkeys), w=[pk(acc)])
                if filler[0] and den_acc is None:
                    fb = sbank[si]
                    MM([mm(pb(fb)[:, :], ident[:, :], pt[:, 0, :], True, True) for _ in range(filler[0])],
                       r=[("PT", i), "ident"], w=[pk(fb)])
                if den_acc is not None:
                    fns = []
                    for j, bi in enumerate(steps[si]):
                        kb, c0, c1, mask, m0 = blocks[bi]
                        fns.append(mm(pb(den_acc)[:, c0:c1], ones[0:nrow, :], pt[0:nrow, j, c0:c1],
                                      bi == 0, bi == nb - 1))
                    MM(fns, r=[("PT", i), "ones"], w=[pk(den_acc)])

            for si in range(len(steps)):
                pipe.push(lambda si=si: emit_qk(si), lambda si=si: emit_pv(si))

        wb_i = [0]

        for s in range(nseq):
            x_s = D["x"][s * SEQ:(s + 1) * SEQ, :]
            out_s = out_d[s * SEQ:(s + 1) * SEQ, :]
            seq_stack = Region([(4, 36)])
            set_rot([0, 1, 2, 3, 4, 5, 6, 7])
            xT = sbuf(seq_stack, [128, 8, SEQ], BF16, "xT")
            xT_src = D["xT"][s]

            def xT_issue(tg_):
                S.dma("pool", xT[:, :, tg_ * 512:(tg_ + 1) * 512], xT_src[:, :, tg_ * 512:(tg_ + 1) * 512],
                      writes=[("xT", tg_ * 4 + i_) for i_ in range(4)], sem=f"xT{tg_}")

            if s == 0:
                xT_issue(0)

            def xkeys(tg):
                return [("xT", tg * 4 + i) for i in range(4)]

            def proj_bank(wt, wkey, tg):
                bk = rot()
                MM([mm(pb(bk), wt[:, k, :], xT[:, k, tg * 512:(tg + 1) * 512], k == 0, k == 7) for k in range(8)],
                   r=[wkey] + xkeys(tg), w=[pk(bk)])
                return bk

            ya_stack = Region([(36, 68)])
            YA = sbuf(ya_stack, [128, 4, SEQ], BF16, "YA")
            YB = sbuf(ya_stack, [128, 4, SEQ], BF16, "YB")

            nsa = Region([(68, 154)])
            Tq = [sbuf(nsa, [128, SEQ], BF16, f"Tq{h}") for h in range(8)]
            QU = [sbuf(nsa, [128, SEQ], BF16, f"QU{j}") for j in range(4)]
            KE = [sbuf(nsa, [128, SEQ], BF16, f"KE{g}") for g in range(2)]
            KW = [sbuf(nsa, [128, SEQ], BF16, f"KW{g}") for g in range(2)]
            VS = sbuf(nsa, [128, NTT, 2, 128], BF16, "VS")
            VW = sbuf(nsa, [128, NTT, 2, 128], BF16, "VW")
            GT = sbuf(nsa, [128, SEQ], BF16, "GT")
            kcmpL = [sbuf(nsa, [128, 128], BF16, f"kcmpL{g}") for g in range(2)]
            kcmpH = [sbuf(nsa, [128, 128], BF16, f"kcmpH{g}") for g in range(2)]
            vcmpA = [sbuf(nsa, [128, 128], BF16, f"vcmpA{g}") for g in range(2)]
            for g in range(2):
                S.dma("pool", KE[g][64:96, :], D["efull"], writes=[("KE", g)], sem=f"ke{g}")
                V(lambda: nc.vector.memset(vcmpA[g][:], 1.0), w=[("vcmpA", g)])
            for h_ in range(8):
                V(lambda: nc.vector.memset(Tq[h_][64:96, :], 0.0), w=[("TqN", h_, q_) for q_ in range(NQ)])
            V(lambda: nc.vector.memset(GT[:], 0.0), w=["GT"])
            for g in range(2):
                V(lambda: nc.vector.memset(KW[g][64:96, :], 0.0), w=[("KW", g)])
                V(lambda: nc.vector.memset(kcmpL[g][:], 0.0), w=[("kcmpT", g)])
                V(lambda: nc.vector.memset(kcmpH[g][:], 0.0), w=[("kcmpT", g)])
            V(lambda: nc.vector.memset(VS[:], 1.0), w=["VS"])
            V(lambda: nc.vector.memset(VW[:], 1.0), w=["VW"])

            set_rot([0, 1, 2, 3, 4, 5, 6, 7])
            with Region([(36, 68), (154, 195)]) as st:
                ropec = load(st, "ropec", [128, SEQ], F32, D["ropec"])
                ropes = load(st, "ropes", [128, SEQ], F32, D["ropes"])
                w1h = [sbuf(st, [128, 32, 128], BF16, f"w1h{i}") for i in range(2)]

                def w1_issue(p_):
                    kv_, hc_ = divmod(p_, 2)
                    S.dma("pool", w1h[p_ % 2][:], D["w1"][kv_][:, :, hc_ * 128:(hc_ + 1) * 128],
                          writes=[("w1h", p_ % 2)], sem=f"w1h{p_ % 2}")
                kcT = sbuf(st, [128, SEQ + 32], BF16, "kcT")
                vcT = sbuf(st, [128, SEQ + 32], BF16, "vcT")
                S.dma("pool", kcT[:, SEQ:SEQ + 32], D["pet"][:, 0:32], writes=["kcT_pe"], sem="pet0")
                S.dma("pool", vcT[:, SEQ:SEQ + 32], D["pet"][:, 32:64], writes=["vcT_pe"], sem="pet1")
                wbufs = [sbuf(st, [128, 8, 128], BF16, f"wbuf{i}") for i in range(4)]
                tmpA = [sbuf(st, [128, 512], F32, f"tmpA{i}") for i in range(2)]
                tmpB = [sbuf(st, [128, 512], F32, f"tmpB{i}") for i in range(2)]
                tcount = [0]

                def wload(src_ap):
                    i = wb_i[0] % 4
                    wb_i[0] += 1
                    S.dma("pool", wbufs[i][:], src_ap, writes=[("wbuf", i)], sem=f"wbuf{i}")
                    return wbufs[i], ("wbuf", i)

                def rope_evac(bA, bB, tg, dests, plain=None):
                    i = tcount[0] % 2
                    tcount[0] += 1
                    cs = slice(tg * 512, (tg + 1) * 512)
                    tA, tB = tmpA[i], tmpB[i]
                    if plain is not None:
                        dt_, dk = plain
                        A(lambda: nc.scalar.copy(out=dt_[:, cs], in_=pb(bA)), w=[pk(bA), dk])
                    V(lambda: nc.vector.tensor_tensor(out=tA[:], in0=pb(bA), in1=ropec[:, cs], op=ALU.mult),
                      r=["ropec"], w=[pk(bA), ("tmpA", i)])
                    V(lambda: nc.vector.tensor_tensor(out=tB[:], in0=pb(bB), in1=ropes[:, cs], op=ALU.mult),
                      r=["ropes"], w=[pk(bB), ("tmpB", i)])
                    for (r0, dtile, dkey, d0, nrow) in dests:
                        V(lambda: nc.vector.tensor_tensor(out=dtile[d0:d0 + nrow, cs], in0=tA[r0:r0 + nrow, :],
                                                          in1=tB[r0:r0 + nrow, :], op=ALU.add),
                          r=[("tmpA", i), ("tmpB", i)], w=[dkey])

                wn = D["wn"]
                for j in range(4):
                    wA, kA = wload(wn[:, :, j * 128:(j + 1) * 128])
                    wB, kB = wload(wn[:, :, (4 + j) * 128:(5 + j) * 128])
                    if j == 0 and s == 0:
                        for tg_ in range(1, NQ):
                            xT_issue(tg_)
                    for tg in range(NQ):
                        bA = proj_bank(wA, kA, tg)
                        bB = proj_bank(wB, kB, tg)
                        rope_evac(bA, bB, tg,
                                  [(0, Tq[2 * j], ("Tq", 2 * j, tg), 0, 64),
                                   (64, Tq[2 * j + 1], ("Tq", 2 * j + 1, tg), 0, 64)],
                                  plain=(QU[j], ("QU", j, tg)))
                    if j == 1:
                        w1_issue(0)
                        w1_issue(1)
                for (t0, dst, nm) in ((8, KE, "KE"), (10, KW, "KW")):
                    wA, kA = wload(wn[:, :, t0 * 128:(t0 + 1) * 128])
                    wB, kB = wload(wn[:, :, (t0 + 1) * 128:(t0 + 2) * 128])
                    for tg in range(NQ):
                        bA = proj_bank(wA, kA, tg)
                        bB = proj_bank(wB, kB, tg)
                        rope_evac(bA, bB, tg, [(0, dst[0], (nm, 0), 0, 64), (64, dst[1], (nm, 1), 0, 64)])
                for (t0, dst, nm) in ((12, kcT, "kcT"), (13, vcT, "vcT")):
                    wA, kA = wload(wn[:, :, t0 * 128:(t0 + 1) * 128])
                    for tg in range(NQ):
                        bA = proj_bank(wA, kA, tg)
                        if tg % 2 == 0:
                            A(lambda: nc.scalar.copy(out=dst[:, tg * 512:(tg + 1) * 512], in_=pb(bA)),
                              w=[pk(bA), nm])
                        else:
                            V(lambda: nc.vector.tensor_copy(out=dst[:, tg * 512:(tg + 1) * 512], in_=pb(bA)),
                              w=[pk(bA), nm])
                wgt = load(st, "wg", [128, 8, 24], BF16, D["wg"])
                for tg in range(NQ):
                    bk = rot()
                    MM([mm(pb(bk)[0:24, :], wgt[:, k, :], xT[:, k, tg * 512:(tg + 1) * 512], k == 0, k == 7)
                        for k in range(8)], r=["wg"] + xkeys(tg), w=[pk(bk)])
                    A(lambda: nc.scalar.activation(out=GT[0:24, tg * 512:(tg + 1) * 512], in_=pb(bk)[0:24, :],
                                                   func=AF.Sigmoid), w=[pk(bk), "GT"])
                wnv = load(st, "wnv", [128, 8, 256], BF16, D["wnv"])
                for tt in range(NTT):
                    bk = rot()
                    MM([mm(pb(bk)[:, 0:256], xT[:, k, tt * 128:(tt + 1) * 128], wnv[:, k, :], k == 0, k == 7)
                        for k in range(8)], r=["wnv", ("xT", tt)], w=[pk(bk)])
                    srcs = pb(bk)[:, 0:128].rearrange("p (g d) -> p g d", g=2)
                    srcw = pb(bk)[:, 128:256].rearrange("p (g d) -> p g d", g=2)
                    V(lambda: nc.vector.tensor_copy(out=VS[:, tt, :, 0:64], in_=srcs), w=[pk(bk), "VS"])
                    A(lambda: nc.scalar.copy(out=VW[:, tt, :, 0:64], in_=srcw), w=[pk(bk), "VW"])

                w2k = load(st, "w2k", [128, 2, 128], BF16, D["w2k"])
                w2v = load(st, "w2v", [128, 2, 64], BF16, D["w2v"])
                b1 = load(st, "b1", [128, 4], F32, D["b1"])
                hb = [sbuf(st, [128, 1], F32, f"hb{i}") for i in range(2)]
                xh = [sbuf(st, [128, 128], F32, f"xh{i}") for i in range(2)]
                x2 = [sbuf(st, [128, 128], F32, f"x2{i}") for i in range(2)]
                sgm = [sbuf(st, [128, 128], F32, f"sgm{i}") for i in range(2)]
                gT = sbuf(st, [128, 2, 2, 128], BF16, "gT")
                it = [0]
                for kv, srcT, srck in ((0, kcT, "kcT"), (1, vcT, "vcT")):
                    for hc in range(2):
                        p_ = kv * 2 + hc
                        wb_ = w1h[p_ % 2]
                        for g in range(2):
                            r0 = g * 64
                            i = it[0] % 2
                            it[0] += 1
                            bk = rot()
                            MM([mm(pb(bk)[:, 0:129], wb_[r0:r0 + 64, l, :],
                                   srcT[r0:r0 + 64, l:l + 16 * 128 + 1:16], l == 0, l == 31)
                                for l in range(32)], r=[("w1h", p_ % 2), srck, srck + "_pe"], w=[pk(bk)])
                            V(lambda: nc.vector.tensor_tensor(out=hb[i][:], in0=pb(bk)[:, 128:129],
                                                              in1=b1[:, kv * 2 + hc:kv * 2 + hc + 1], op=ALU.add),
                              r=["b1"], w=[pk(bk), ("hb", i)])
                            A(lambda: nc.scalar.activation(out=xh[i][:, 0:127], in_=pb(bk)[:, 0:127],
                                                           func=AF.Identity, bias=hb[i][:]),
                              r=[("hb", i)], w=[pk(bk), ("xh", i)])
                            V(lambda: nc.vector.tensor_tensor(out=x2[i][:, 0:127], in0=xh[i][:, 0:127],
                                                              in1=xh[i][:, 0:127], op=ALU.mult),
                              r=[("xh", i)], w=[("x2", i)])
                            V(lambda: nc.vector.tensor_scalar(out=x2[i][:, 0:127], in0=x2[i][:, 0:127],
                                                              scalar1=0.044715, scalar2=1.0, op0=ALU.mult,
                                                              op1=ALU.add), w=[("x2", i)])
                            V(lambda: nc.vector.tensor_tensor(out=x2[i][:, 0:127], in0=x2[i][:, 0:127],
                                                              in1=xh[i][:, 0:127], op=ALU.mult),
                              r=[("xh", i)], w=[("x2", i)])
                            A(lambda: nc.scalar.activation(out=sgm[i][:, 0:127], in_=x2[i][:, 0:127], func=AF.Sigmoid,
                                                           scale=2.0 * math.sqrt(2.0 / math.pi)),
                              r=[("x2", i)], w=[("sgm", i)])
                            V(lambda: nc.vector.tensor_tensor(out=gT[:, g, hc, 0:127], in0=xh[i][:, 0:127],
                                                              in1=sgm[i][:, 0:127], op=ALU.mult),
                              r=[("xh", i), ("sgm", i)], w=[("gT", g, hc)])
                        if p_ + 2 < 4:
                            w1_issue(p_ + 2)
                    for g in range(2):
                        bk = rot()
                        gk = [("gT", g, 0), ("gT", g, 1)]
                        if kv == 0:
                            MM([mm(pb(bk)[:, 0:127], w2k[:, hc, :], gT[:, g, hc, 0:127], hc == 0, hc == 1)
                                for hc in range(2)], r=["w2k"] + gk, w=[pk(bk)])
                            V(lambda: nc.vector.tensor_copy(out=kcmpL[g][0:64, 0:127], in_=pb(bk)[0:64, 0:127]),
                              w=[pk(bk), ("kcmpT", g)])
                            V(lambda: nc.vector.tensor_copy(out=kcmpH[g][64:128, 0:127], in_=pb(bk)[64:128, 0:127]),
                              w=[pk(bk), ("kcmpT", g)])
                        else:
                            MM([mm(pb(bk)[0:127, 0:64], gT[:, g, hc, 0:127], w2v[:, hc, :], hc == 0, hc == 1)
                                for hc in range(2)], r=["w2v"] + gk, w=[pk(bk)])
                            V(lambda: nc.vector.tensor_copy(out=vcmpA[g][0:127, 0:64], in_=pb(bk)[0:127, 0:64]),
                              w=[pk(bk), ("vcmpA", g)])
                if s == 0:
                    dump("d_t0", Tq[0][:], ("Tq", 0, 0), [128, SEQ])
                    dump("d_qu0", QU[0][:], ("QU", 0, 0), [128, SEQ])
                    dump("d_ke0", KE[0][:], ("KE", 0), [128, SEQ])
                    dump("d_kcmp", kcmpL[0][:], ("kcmpT", 0), [128, 128])
                    dump("d_vcmp", vcmpA[0][:], ("vcmpA", 0), [128, 128])
                S.barrier()

            set_rot([], pairs=[0, 2])
            ACC = [4, 5, 6, 7]
            ACCP = [4, 6]
            pipe.la = 2
            with Region([(52, 68), (154, 195)]) as st:
                cmask = load(st, "cmask", [128, SEQ], BF16, D["cmask"])
                ovaug = load(st, "ovaug", [128, 33], BF16, D["ovaug"])
                selvalid = load(st, "selvalid", [128, 512], F32, D["selvalid"])
                seladdc = load(st, "seladdc", [128, 512], F32, D["seladdc"])
                selg = load(st, "selg", [128, 24 * 128], BF16, D["selg"])
                PT = [sbuf(st, [128, 2, 512], BF16, f"PT{i}") for i in range(4)]
                PTc = sbuf(st, [128, 4, 512], BF16, "PTc")
                GB = sbuf(st, [64, 12, 512], BF16, "GB")
                yacc = sbuf(st, [64, 4, 512], F32, "yacc")
                rr_all = sbuf(st, [128, 4, 512], F32, "rr_all")
                rr = [rr_all[:, 2 * i:2 * i + 2, :] for i in range(2)]
                den16 = sbuf(st, [128, 16], F32, "den16")
                pn = sbuf(Region([(195, 203)]), [128, 512], F32, "pn")
                pri = sbuf(st, [128, 128], F32, "pri")
                top8 = sbuf(st, [128, 32], F32, "top8")
                nsel = sbuf(st, [128, 128], BF16, "nsel")
                pti = [0]
                acci = [0]
                nrm = [0]

                def attn(blocks, kT, qT, vA, rkeys, acc, nrow=128, den_acc=None):
                    attn_pipe(blocks, kT, qT, vA, rkeys, acc, PT, pti, nrow, den_acc)

                def combine_cmp(acc, h, Q, hh):
                    i = nrm[0] % 2
                    nrm[0] += 1
                    A(lambda: nc.scalar.activation(out=rr[i][64:128, 0, :], in_=pb(acc)[64:128, :], func=AF.Ln,
                                                   bias=tiny_c[64:128, :]), r=["tiny_c"], w=[pk(acc), ("rr", i)])
                    A(lambda: nc.scalar.activation(out=rr[i][64:128, 0, :], in_=rr[i][64:128, 0, :], func=AF.Exp,
                                                   scale=-1.0), w=[("rr", i)])
                    V(lambda: nc.vector.tensor_tensor(out=rr[i][0:64, 0, :], in0=pb(acc)[0:64, :],
                                                      in1=rr[i][64:128, 0, :], op=ALU.mult),
                      w=[pk(acc), ("rr", i)])
                    V(lambda: nc.vector.tensor_tensor(out=yacc[:, hh, :], in0=rr[i][0:64, 0, :], in1=GB[:, hh * 3, :],
                                                      op=ALU.mult), r=[("rr", i), ("GB", hh)], w=[("yacc", hh)])

                def combine2(accp, h, Q, hh):
                    i = nrm[0] % 2
                    nrm[0] += 1
                    cs = slice(Q * 512, (Q + 1) * 512)
                    A(lambda: nc.scalar.activation(out=rr[i][64:128, :, :], in_=ps[64:128, accp:accp + 2, :],
                                                   func=AF.Ln, bias=zero_c[64:128, :]),
                      r=["tiny_c"], w=[pk(accp), pk(accp + 1), ("rr", i)])
                    A(lambda: nc.scalar.activation(out=rr[i][64:128, :, :], in_=rr[i][64:128, :, :], func=AF.Exp,
                                                   scale=-1.0), w=[("rr", i)])
                    V(lambda: nc.vector.tensor_tensor(out=rr[i][0:64, :, :], in0=ps[0:64, accp:accp + 2, :],
                                                      in1=rr[i][64:128, :, :], op=ALU.mult),
                      w=[pk(accp), pk(accp + 1), ("rr", i)])
                    V(lambda: nc.vector.tensor_tensor(out=rr[i][0:64, :, :], in0=rr[i][0:64, :, :],
                                                      in1=GB[:, hh * 3 + 1:hh * 3 + 3, :], op=ALU.mult),
                      r=[("GB", hh)], w=[("rr", i)])
                    V(lambda: nc.vector.tensor_tensor(out=rr[i][0:64, 0, :], in0=rr[i][0:64, 0, :],
                                                      in1=rr[i][0:64, 1, :], op=ALU.add), w=[("rr", i)])
                    r0 = (h % 2) * 64
                    V(lambda: nc.vector.tensor_tensor(out=YA[r0:r0 + 64, h // 2, cs], in0=yacc[:, hh, :],
                                                      in1=rr[i][0:64, 0, :], op=ALU.add),
                      r=[("rr", i), ("yacc", hh)], w=[("YA", h // 2)])

                for g in range(2):
                    for Q in range(NQ):
                        cs = slice(Q * 512, (Q + 1) * 512)
                        for c12 in range(12):
                            gbk = c12 % 4
                            c = g * 12 + c12
                            MM([mm(pb(gbk)[:, :], selg[:, c * 128:(c + 1) * 128], GT[:, cs], True, True)],
                               r=["selg", "GT"], w=[pk(gbk)])
                            A(lambda: nc.scalar.copy(out=GB[:, c12, :], in_=pb(gbk)[0:64, :]),
                              w=[pk(gbk), ("GB", c12 // 3)])
                        bps = [rotpair(), rotpair()]
                        for hh in range(4):
                            h = g * 4 + hh
                            j, r0 = h // 2, (h % 2) * 64
                            sbk = bps[hh // 2] + hh % 2
                            kc_ = kcmpL[g] if r0 == 0 else kcmpH[g]
                            MM([mm(pb(sbk)[0:127, :], kc_[:, 0:127], QU[j][:, cs], True, False),
                                mm(pb(sbk)[0:127, :], ident[0:127, 0:127], cmask[0:127, cs], False, True)],
                               r=[("kcmpT", g), ("QU", j, Q), "cmask", "ident"], w=[pk(sbk)])
                        for pp in range(2):
                            bp = bps[pp]
                            A(lambda: nc.scalar.activation(out=PTc[0:127, 2 * pp:2 * pp + 2, :],
                                                           in_=ps[0:127, bp:bp + 2, :], func=AF.Exp, scale=0.125),
                              w=[pk(bp), pk(bp + 1), ("PTc", pp)])
                        for hh in range(4):
                            MM([mm(pb(4 + hh)[:, :], vcmpA[g][0:127, :], PTc[0:127, hh, :], True, True)],
                               r=[("PTc", hh // 2), ("vcmpA", g)], w=[pk(4 + hh)])
                        rkeys4 = [("rr", 0), ("rr", 1)]
                        akeys4 = [pk(4 + hh) for hh in range(4)]
                        A(lambda: nc.scalar.activation(out=rr_all[64:128, :, :], in_=ps[64:128, 4:8, :], func=AF.Ln,
                                                       bias=tiny_c[64:128, :]), r=["tiny_c"], w=akeys4 + rkeys4)
                        A(lambda: nc.scalar.activation(out=rr_all[64:128, :, :], in_=rr_all[64:128, :, :],
                                                       func=AF.Exp, scale=-1.0), w=rkeys4)
                        rp = rotpair()
                        fns = []
                        for n in range(4):
                            for hh in range(4):
                                c16 = n * 4 + hh
                                fns.append(mm(pb(rp)[:, c16 * 32:(c16 + 1) * 32], PTc[0:127, hh, n * 128:(n + 1) * 128],
                                              ovaug[0:127, 0:32], True, True))
                                fns.append(mm(pb(rp + 1)[:, c16:c16 + 1], PTc[0:127, hh, n * 128:(n + 1) * 128],
                                              ovaug[0:127, 32:33], True, True))
                        MM(fns, r=[("PTc", 0), ("PTc", 1), "ovaug"], w=[pk(rp), pk(rp + 1)])
                        V(lambda: nc.vector.tensor_scalar(out=den16[:], in0=pb(rp + 1)[:, 0:16], scalar1=1e-30,
                                                          scalar2=None, op0=ALU.max), w=[pk(rp + 1), "den16"])
                        V(lambda: nc.vector.reciprocal(out=den16[:], in_=den16[:]), w=["den16"])
                        V(lambda: nc.vector.tensor_tensor(
                            out=pn[:].rearrange("p (c j) -> p c j", j=32),
                            in0=pb(rp)[:, :].rearrange("p (c j) -> p c j", j=32),
                            in1=den16[:].unsqueeze(2).broadcast_to([128, 16, 32]), op=ALU.mult),
                          r=["den16"], w=[pk(rp), "pn"])
                        V(lambda: nc.vector.tensor_reduce(out=pri[:].rearrange("p (n j) -> p n j", n=4),
                                                          in_=pn[:].rearrange("p (n h j) -> p n j h", n=4, h=4),
                                                          axis=AX.X, op=ALU.add), r=["pn"], w=["pri"])
                        V(lambda: nc.vector.tensor_tensor(out=pri[:], in0=pri[:],
                                                          in1=selvalid[:, Q * 128:(Q + 1) * 128], op=ALU.mult),
                          r=["selvalid"], w=["pri"])
                        V(lambda: nc.vector.tensor_tensor(out=pri[:], in0=pri[:],
                                                          in1=seladdc[:, Q * 128:(Q + 1) * 128], op=ALU.add),
                          r=["seladdc"], w=["pri"])
                        for n in range(4):
                            V(lambda: nc.vector.max(out=top8[:, n * 8:(n + 1) * 8], in_=pri[:, n * 32:(n + 1) * 32]),
                              r=["pri"], w=["top8"])
                        V(lambda: nc.vector.tensor_tensor(
                            out=pn[:, 0:128].rearrange("p (n j) -> p n j", n=4),
                            in0=pri[:].rearrange("p (n j) -> p n j", n=4),
                            in1=top8[:].rearrange("p (n e) -> p n e", n=4)[:, :, 7:8].broadcast_to([128, 4, 32]),
                            op=ALU.is_lt), r=["pri", "top8"], w=["pn"])
                        V(lambda: nc.vector.tensor_scalar(out=nsel[:], in0=pn[:, 0:128], scalar1=BIG, scalar2=None,
                                                          op0=ALU.mult), r=["pn"], w=["nsel"])
                        V(lambda: nc.vector.tensor_tensor(out=rr_all[0:64, :, :], in0=ps[0:64, 4:8, :],
                                                          in1=rr_all[64:128, :, :], op=ALU.mult), w=akeys4 + rkeys4)
                        V(lambda: nc.vector.tensor_tensor(out=yacc[:, :, :], in0=rr_all[0:64, :, :],
                                                          in1=GB[:, 0:12:3, :], op=ALU.mult),
                          r=rkeys4 + [("GB", hh) for hh in range(4)], w=[("yacc", hh) for hh in range(4)])

                        def sel_finish(g=g, Q=Q, cs=cs):
                            tb = rot()
                            MM([(lambda n=n: nc.tensor.transpose(out=pbf(tb)[0:32, n * 128:(n + 1) * 128],
                                                                 in_=nsel[:, n * 32:(n + 1) * 32], identity=ident[:]))
                                for n in range(4)], r=["nsel", "ident"], w=[pk(tb)])
                            for hh_ in range(4):
                                h_ = g * 4 + hh_
                                dst = Tq[h_][64:96, cs]
                                if hh_ % 2 == 0:
                                    V(lambda: nc.vector.tensor_copy(out=dst, in_=pbf(tb)[0:32, 0:512]),
                                      w=[pk(tb), ("TqN", h_, Q)])
                                else:
                                    A(lambda: nc.scalar.copy(out=dst, in_=pbf(tb)[0:32, 0:512]),
                                      w=[pk(tb), ("TqN", h_, Q)])

                        for hh in range(4):
                            h = g * 4 + hh
                            accp = ACCP[acci[0] % 2]
                            acci[0] += 1
                            blocks = [(4 * Q + o, 128 * o, 512, "causal", 128 * o) for o in range(4)]
                            if Q > 0:
                                blocks += [(4 * Q - 4 + o, 0, 128 * (o + 1), "anti", 128 * o) for o in range(4)]
                            attn(blocks,
                                 lambda kb, g=g: KW[g][0:96, kb * 128:(kb + 1) * 128],
                                 lambda c0, c1, h=h, Q=Q: Tq[h][0:96, Q * 512 + c0:Q * 512 + c1],
                                 lambda kb, g=g: VW[:, kb, g, :],
                                 [("KW", g), ("Tq", h, Q), "VW", "ident", "tri"], accp + 1)
                            if hh == 0:
                                sel_finish()
                            blocks = [(kb, 0, 512, None, 0) for kb in range(4 * Q)]
                            blocks += [(4 * Q + o, 128 * o, 512, "causal", 128 * o) for o in range(4)]
                            attn(blocks,
                                 lambda kb, g=g: KE[g][0:96, kb * 128:(kb + 1) * 128],
                                 lambda c0, c1, h=h, Q=Q: Tq[h][0:96, Q * 512 + c0:Q * 512 + c1],
                                 lambda kb, g=g: VS[:, kb, g, :],
                                 [("KE", g), ("Tq", h, Q), ("TqN", h, Q), "VS", "ident", "tri"], accp)
                            pipe.post(lambda accp=accp, h=h, Q=Q, hh=hh: combine2(accp, h, Q, hh))
                        pipe.drain()
                if s == 0:
                    dump("d_t0b", Tq[0][:], ("Tq", 0, 0), [128, SEQ])
                S.barrier()
            nsa.close()

            set_rot([0, 1, 2, 3, 4, 5, 6, 7])
            with Region([(68, 132)]) as dst_:
                QD = sbuf(dst_, [128, 4, SEQ], BF16, "QD")
                KD = sbuf(dst_, [128, 4, SEQ], BF16, "KD")
                KD1 = sbuf(dst_, [128, 4, SEQ], BF16, "KD1")
                V(lambda: nc.vector.memset(KD[64:128, :, :], 0.0), w=[("KD", j) for j in range(4)])
                V(lambda: nc.vector.memset(KD1[0:64, :, :], 0.0), w=[("KD", j) for j in range(4)])
                VD = sbuf(dst_, [128, NTT, 512], BF16, "VD")
                with Region([(132, 195)]) as st:
                    ropec = load(st, "ropec", [128, SEQ], F32, D["ropec"])
                    ropes = load(st, "ropes", [128, SEQ], F32, D["ropes"])
                    wbufs = [sbuf(st, [128, 8, 128], BF16, f"wbuf{i}") for i in range(4)]
                    tmpA = [sbuf(st, [128, 512], F32, f"tmpA{i}") for i in range(2)]
                    tmpB = [sbuf(st, [128, 512], F32, f"tmpB{i}") for i in range(2)]
                    wd = D["wd"]
                    tci = [0]
                    for (t0, dstt, nm) in ((0, QD, "QD"), (8, KD, "KD")):
                        for j in range(4):
                            i0 = wb_i[0] % 4
                            wb_i[0] += 1
                            i1 = wb_i[0] % 4
                            wb_i[0] += 1
                            S.dma("pool", wbufs[i0][:], wd[:, :, (t0 + j) * 128:(t0 + j + 1) * 128],
                                  writes=[("wbuf", i0)], sem=f"wbuf{i0}")
                            S.dma("pool", wbufs[i1][:], wd[:, :, (t0 + 4 + j) * 128:(t0 + 5 + j) * 128],
                                  writes=[("wbuf", i1)], sem=f"wbuf{i1}")
                            for tg in range(NQ):
                                bA = proj_bank(wbufs[i0], ("wbuf", i0), tg)
                                bB = proj_bank(wbufs[i1], ("wbuf", i1), tg)
                                i = tci[0] % 2
                                tci[0] += 1
                                cs = slice(tg * 512, (tg + 1) * 512)
                                V(lambda: nc.vector.tensor_tensor(out=tmpA[i][:], in0=pb(bA), in1=ropec[:, cs],
                                                                  op=ALU.mult), r=["ropec"], w=[pk(bA), ("tmpA", i)])
                                V(lambda: nc.vector.tensor_tensor(out=tmpB[i][:], in0=pb(bB), in1=ropes[:, cs],
                                                                  op=ALU.mult), r=["ropes"], w=[pk(bB), ("tmpB", i)])
                                if nm == "QD":
                                    V(lambda: nc.vector.tensor_tensor(out=dstt[:, j, cs], in0=tmpA[i][:],
                                                                      in1=tmpB[i][:], op=ALU.add),
                                      r=[("tmpA", i), ("tmpB", i)], w=[(nm, j)])
                                else:
                                    V(lambda: nc.vector.tensor_tensor(out=KD[0:64, j, cs], in0=tmpA[i][0:64, :],
                                                                      in1=tmpB[i][0:64, :], op=ALU.add),
                                      r=[("tmpA", i), ("tmpB", i)], w=[(nm, j)])
                                    V(lambda: nc.vector.tensor_tensor(out=KD1[64:128, j, cs], in0=tmpA[i][64:128, :],
                                                                      in1=tmpB[i][64:128, :], op=ALU.add),
                                      r=[("tmpA", i), ("tmpB", i)], w=[(nm, j)])
                    wdv = load(st, "wdv", [128, 8, 512], BF16, D["wdv"])
                    for tt in range(NTT):
                        bk = rot()
                        MM([mm(pb(bk), xT[:, k, tt * 128:(tt + 1) * 128], wdv[:, k, :], k == 0, k == 7)
                            for k in range(8)], r=["wdv", ("xT", tt)], w=[pk(bk)])
                        if tt % 2 == 0:
                            V(lambda: nc.vector.tensor_copy(out=VD[:, tt, :], in_=pb(bk)), w=[pk(bk), "VD"])
                        else:
                            A(lambda: nc.scalar.copy(out=VD[:, tt, :], in_=pb(bk)), w=[pk(bk), "VD"])
                    S.barrier()
                set_rot([], pairs=[0, 2])
                pipe.la = 2
                with Region([(132, 195)]) as st:
                    PT = [sbuf(st, [128, 2, 512], BF16, f"PT{i}") for i in range(4)]
                    a0 = sbuf(st, [128, 512], F32, "a0")
                    a1 = sbuf(st, [128, 512], F32, "a1")
                    r0t = sbuf(st, [128, 512], F32, "r0t")
                    of2 = [sbuf(st, [128, 512], F32, f"of{i}") for i in range(2)]
                    sq2 = [sbuf(st, [128, 512], BF16, f"sq{i}") for i in range(2)]
                    sd = sbuf(st, [128, 512], F32, "sd")
                    pti = [0]

                    def diff_post(c, oacc, dacc):
                        dstA = a0 if c == 0 else a1
                        A(lambda: nc.scalar.activation(out=r0t[:], in_=pb(dacc), func=AF.Ln),
                          w=[pk(dacc), "r0t"])
                        A(lambda: nc.scalar.activation(out=r0t[:], in_=r0t[:], func=AF.Exp, scale=-1.0),
                          w=["r0t"])
                        V(lambda: nc.vector.tensor_tensor(out=dstA[:], in0=pb(oacc), in1=r0t[:], op=ALU.mult),
                          r=["r0t"], w=[pk(oacc), ("a", c)])

                    def diff_final_a(pb_):
                        V(lambda: nc.vector.scalar_tensor_tensor(out=of2[pb_][:], in0=a1[:], scalar=neglam[:, 0:1],
                                                                 in1=a0[:], op0=ALU.mult, op1=ALU.add),
                          r=[("a", 0), ("a", 1), "neglam"], w=[("of", pb_)])
                        A(lambda: nc.scalar.activation(out=sq2[pb_][:], in_=of2[pb_][:], func=AF.Square),
                          r=[("of", pb_)], w=[("sq", pb_)])

                    def diff_final_b(pb_, H, cs):
                        mb = rot()
                        MM([mm(pb(mb), ones[:, :], sq2[pb_][:], True, True)], r=[("sq", pb_), "ones"], w=[pk(mb)])
                        A(lambda: nc.scalar.activation(out=sd[:], in_=pb(mb), func=AF.Ln, scale=1.0 / 128.0,
                                                       bias=eps_rms[:]), r=["eps_rms"], w=[pk(mb), "sd"])
                        A(lambda: nc.scalar.activation(out=sd[:], in_=sd[:], func=AF.Exp, scale=-0.5), w=["sd"])
                        V(lambda: nc.vector.scalar_tensor_tensor(out=YB[:, H, cs], in0=of2[pb_][:],
                                                                 scalar=gcol[:, 0:1], in1=sd[:], op0=ALU.mult,
                                                                 op1=ALU.mult),
                          r=[("of", pb_), "sd", "gcol"], w=[("YB", H)])

                    fin_i = [0]
                    fin_prev = [None]
                    for H in range(4):
                        for Q in range(NQ):
                            cs = slice(Q * 512, (Q + 1) * 512)
                            for c in range(2):
                                r0 = c * 64
                                oacc, dacc = (4, 5) if c == 0 else (6, 7)
                                blocks = [(kb, 0, 512, None, 0) for kb in range(4 * Q)]
                                blocks += [(4 * Q + o, 128 * o, 512, "causal", 128 * o) for o in range(4)]
                                attn_pipe(blocks,
                                          lambda kb, c=c, H=H: (KD if c == 0 else KD1)[:, H, kb * 128:(kb + 1) * 128],
                                          lambda c0, c1, H=H, Q=Q: QD[:, H, Q * 512 + c0:Q * 512 + c1],
                                          lambda kb, H=H: VD[:, kb, H * 128:(H + 1) * 128],
                                          [("KD", H), ("QD", H), "VD", "ident", "tri"], oacc, PT, pti, 128, dacc)
                                pipe.post(lambda c=c, oacc=oacc, dacc=dacc: diff_post(c, oacc, dacc))
                            pb_ = fin_i[0] % 2
                            fin_i[0] += 1
                            if fin_prev[0] is not None:
                                pipe.post(lambda a=fin_prev[0]: diff_final_b(*a))
                            pipe.post(lambda pb_=pb_: diff_final_a(pb_))
                            fin_prev[0] = (pb_, H, cs)
                            if H == 3 and Q == 2:
                                R6a = Region([(154, 178)])
                                wbufs6 = [sbuf(R6a, [128, 8, 128], BF16, f"w6_{i}") for i in range(4)]
                                WA = sbuf(R6a, [128, 4, 1024], BF16, "wa")
                                WB = sbuf(R6a, [128, 4, 1024], BF16, "wb")

                                def ld6a(m_):
                                    j0, j1 = (2 * m_) % 4, (2 * m_ + 1) % 4
                                    S.dma("pool", wbufs6[j0][:], D["wgm"][:, :, m_ * 128:(m_ + 1) * 128],
                                          writes=[("w6", j0)], sem=f"w6_{j0}")
                                    S.dma("pool", wbufs6[j1][:], D["wgm"][:, :, (8 + m_) * 128:(9 + m_) * 128],
                                          writes=[("w6", j1)], sem=f"w6_{j1}")
                                    S.dma("pool", WA[:, :, m_ * 128:(m_ + 1) * 128],
                                          D["wa"][:, :, m_ * 128:(m_ + 1) * 128], writes=[("wa", m_)], sem=f"wa{m_}")
                                    S.dma("pool", WB[:, :, m_ * 128:(m_ + 1) * 128],
                                          D["wb"][:, :, m_ * 128:(m_ + 1) * 128], writes=[("wb", m_)], sem=f"wb{m_}")

                                ld6a(0)
                    pipe.drain()
                    diff_final_b(*fin_prev[0])
                    S.barrier()
            if s == 0:
                dump("d_ya", YA[:].rearrange("p a b -> p (a b)"), ("YA", 0), [128, 4 * SEQ])
                dump("d_yb", YB[:].rearrange("p a b -> p (a b)"), ("YB", 0), [128, 4 * SEQ])
                S.barrier()

            set_rot([0, 1, 2, 3, 4, 5, 6, 7])
            mt_stack = Region([(68, 100)])
            MT = sbuf(mt_stack, [128, 8, SEQ], BF16, "MT")
            with Region([(100, 112)]) as st:
                wbufs = wbufs6
                sg0 = [sbuf(st, [128, 512], BF16, f"sg0{i}") for i in range(2)]
                sg1 = [sbuf(st, [128, 512], BF16, f"sg1{i}") for i in range(2)]
                t0_ = [sbuf(st, [128, 512], F32, f"t0{i}") for i in range(2)]
                t1_ = [sbuf(st, [128, 512], F32, f"t1{i}") for i in range(2)]
                ci = [0]
                for m in range(8):
                    i0, i1 = (2 * m) % 4, (2 * m + 1) % 4
                    if m + 1 < 8:
                        ld6a(m + 1)
                    if m == 4:
                        WO = sbuf(Region([(178, 194)]), [128, 8, 1024], BF16, "wo")
                        for hf in range(2):
                            S.dma("pool", WO[:, :, hf * 512:(hf + 1) * 512], D["wo"][:, :, hf * 512:(hf + 1) * 512],
                                  writes=[("wo", hf)], sem=f"wo{hf}")
                        ln1r = Region([(194, 202)])
                        lng1 = load(ln1r, "lng", [128, DM], F32, D["ln1_g"])
                        lnb1 = load(ln1r, "lnb", [128, DM], F32, D["ln1_b"])
                    for tg in range(NQ):
                        cs = slice(tg * 512, (tg + 1) * 512)
                        i = ci[0] % 2
                        ci[0] += 1
                        b0 = proj_bank(wbufs[i0], ("w6", i0), tg)
                        A(lambda: nc.scalar.activation(out=sg0[i][:], in_=pb(b0), func=AF.Sigmoid),
                          w=[pk(b0), ("sg0", i)])
                        b1_ = proj_bank(wbufs[i1], ("w6", i1), tg)
                        A(lambda: nc.scalar.activation(out=sg1[i][:], in_=pb(b1_), func=AF.Sigmoid),
                          w=[pk(b1_), ("sg1", i)])
                        ba = rot()
                        MM([mm(pb(ba), WA[:, kc, m * 128:(m + 1) * 128], YA[:, kc, cs], kc == 0, kc == 3)
                            for kc in range(4)], r=[("wa", m)] + [("YA", kc) for kc in range(4)], w=[pk(ba)])
                        V(lambda: nc.vector.tensor_tensor(out=t0_[i][:], in0=pb(ba), in1=sg0[i][:], op=ALU.mult),
                          r=[("sg0", i)], w=[pk(ba), ("t0", i)])
                        bb = rot()
                        MM([mm(pb(bb), WB[:, kc, m * 128:(m + 1) * 128], YB[:, kc, cs], kc == 0, kc == 3)
                            for kc in range(4)], r=[("wb", m)] + [("YB", kc) for kc in range(4)], w=[pk(bb)])
                        V(lambda: nc.vector.tensor_tensor(out=t1_[i][:], in0=pb(bb), in1=sg1[i][:], op=ALU.mult),
                          r=[("sg1", i)], w=[pk(bb), ("t1", i)])
                        V(lambda: nc.vector.tensor_tensor(out=MT[:, m, cs], in0=t0_[i][:], in1=t1_[i][:], op=ALU.add),
                          r=[("t0", i), ("t1", i)], w=[("MT", m)])
                S.barrier()
            ya_stack.close()
            seq_stack.close()
            if s == 0:
                dump("d_mt", MT[:].rearrange("p a b -> p (a b)"), ("MT", 0), [128, 8 * SEQ])
                S.barrier()

            ffn_stack = Region([(100, 164)])
            z2acc = sbuf(ffn_stack, [128, NTT, DM], F32, "z2acc")
            h1T = sbuf(Region([(4, 36)]), [128, 8, SEQ], BF16, "h1T")

            def layer_norm(zt, zkey, gam, bet, outt, okey, mv, rstd, st6):
                V(lambda: nc.vector.bn_stats(out=st6[:, 0, :], in_=zt[:, 0:512]), r=[zkey], w=["st6"])
                V(lambda: nc.vector.bn_stats(out=st6[:, 1, :], in_=zt[:, 512:1024]), r=[zkey], w=["st6"])
                V(lambda: nc.vector.bn_aggr(out=mv[:], in_=st6[:]), r=["st6"], w=["mv"])
                A(lambda: nc.scalar.activation(out=rstd[:], in_=mv[:, 1:2], func=AF.Sqrt, bias=eps_ln[:]),
                  r=["mv", "eps_ln"], w=["rstd"])
                V(lambda: nc.vector.reciprocal(out=rstd[:], in_=rstd[:]), w=["rstd"])
                V(lambda: nc.vector.tensor_scalar(out=outt[:], in0=zt[:], scalar1=mv[:, 0:1], scalar2=rstd[:, 0:1],
                                                  op0=ALU.subtract, op1=ALU.mult), r=[zkey, "mv", "rstd"], w=[okey])
                V(lambda: nc.vector.tensor_tensor(out=outt[:], in0=outt[:], in1=gam[:], op=ALU.mult),
                  r=["lng"], w=[okey])
                V(lambda: nc.vector.tensor_tensor(out=outt[:], in0=outt[:], in1=bet[:], op=ALU.add),
                  r=["lnb"], w=[okey])

            wgub = [sbuf(Region([(164 + 4 * i, 168 + 4 * i)]), [128, 8, 256], BF16, f"wgub{i}") for i in range(3)]
            with Region([(36, 68), (176, 178)]) as st:
                lng, lnb = lng1, lnb1
                for j_ in range(3):
                    S.dma("pool", wgub[j_][:], D["wgu"][:, :, j_ * 256:(j_ + 1) * 256], writes=[("wgub", j_)],
                          sem=f"wgub{j_}")
                xt = [sbuf(st, [128, DM], F32, f"xt{i}") for i in range(2)]
                zt = [sbuf(st, [128, DM], F32, f"zt{i}") for i in range(2)]
                ht = [sbuf(st, [128, DM], F32, f"ht{i}") for i in range(2)]
                hbf = [sbuf(st, [128, DM], BF16, f"hbf{i}") for i in range(2)]
                st6 = sbuf(st, [128, 2, 6], F32, "st6")
                mv = sbuf(st, [128, 2], F32, "mv")
                rstd = sbuf(st, [128, 1], F32, "rstd")
                pairs = [(0, 1), (2, 3), (4, 5)]

                def tail6b(t_):
                    i_ = t_ % 2
                    A(lambda: nc.scalar.mul(out=z2acc[:, t_, :], in_=ht[i_][:], mul=ALPHA), r=[("ht", i_)],
                      w=[("z2", t_)])
                    A(lambda: nc.scalar.copy(out=hbf[i_][:], in_=ht[i_][:]), r=[("ht", i_)], w=[("hbf", i_)])
                    bk_ = 6 + (t_ % 2)
                    MM([(lambda k=k: nc.tensor.transpose(out=pbf(bk_)[:, k * 128:(k + 1) * 128],
                                                          in_=hbf[i_][:, k * 128:(k + 1) * 128], identity=ident[:]))
                        for k in range(8)], r=[("hbf", i_), "ident"], w=[pk(bk_)])
                    A(lambda: nc.scalar.copy(out=h1T[:, :, t_ * 128:(t_ + 1) * 128],
                                             in_=pbf(bk_).rearrange("p (k t) -> p k t", k=8)),
                      w=[pk(bk_), ("h1T", t_)])
                for tt in range(NTT):
                    i = tt % 2
                    S.dma("sp", xt[i][:], x_s[tt * 128:(tt + 1) * 128, :], writes=[("xt", i)], sem=f"xt{i}")
                    b0, b1_ = pairs[tt % 3]
                    MM([mm(pb(b0), MT[:, kc, tt * 128:(tt + 1) * 128], WO[:, kc, 0:512], kc == 0, kc == 7)
                        for kc in range(8)], r=[("wo", 0)] + [("MT", m) for m in range(8)], w=[pk(b0)])
                    MM([mm(pb(b1_), MT[:, kc, tt * 128:(tt + 1) * 128], WO[:, kc, 512:1024], kc == 0, kc == 7)
                        for kc in range(8)], r=[("wo", 1)] + [("MT", m) for m in range(8)], w=[pk(b1_)])
                    V(lambda: nc.vector.scalar_tensor_tensor(out=zt[i][:].rearrange("p (a b) -> p a b", a=2),
                                                             in0=xt[i][:].rearrange("p (a b) -> p a b", a=2),
                                                             scalar=ALPHA, in1=ps[:, b0:b0 + 2, :],
                                                             op0=ALU.mult, op1=ALU.add),
                      r=[("xt", i)], w=[pk(b0), pk(b1_), ("zt", i)])
                    if tt > 0:
                        tail6b(tt - 1)
                    layer_norm(zt[i], ("zt", i), lng, lnb, ht[i], ("ht", i), mv, rstd, st6)
                tail6b(NTT - 1)
                S.barrier()
            mt_stack.close()
            if s == 0:
                dump("d_z2", z2acc[:].rearrange("p a b -> p (a b)"), ("z2", 0), [128, NTT * DM], is_bf=False)
                S.barrier()

            set_rot([0, 1, 2, 3, 4, 5])
            with Region([(36, 100), (176, 203)]) as st:
                lng = load(st, "lng2", [128, DM], F32, D["ln2_g"], key="lng")
                lnb = load(st, "lnb2", [128, DM], F32, D["ln2_b"], key="lnb")
                AT = sbuf(st, [128, 11, SEQ], BF16, "AT")
                WDN = sbuf(st, [128, 11, 1024], BF16, "WDN")
                sgt = [sbuf(st, [128, 512], BF16, f"sgt{i}") for i in range(2)]
                st6 = sbuf(st, [128, 2, 6], F32, "st6")
                mv = sbuf(st, [128, 2], F32, "mv")
                rstd = sbuf(st, [128, 1], F32, "rstd")
                gi = [0]
                si = [0]
                for half in range(2):
                    for jj in range(11):
                        j = half * 11 + jj
                        wi = gi[0] % 3
                        gi[0] += 1
                        if j >= 3:
                            S.dma("pool", wgub[wi][:], D["wgu"][:, :, j * 256:(j + 1) * 256], writes=[("wgub", wi)],
                                  sem=f"wgub{wi}")
                        if jj == 2:
                            for j2 in range(11):
                                S.dma("pool", WDN[:, j2, :], D["wdn"][:, half * 11 + j2, :],
                                      reads=[], writes=[("WDN", j2)], sem=f"wdn{j2}")
                        for tg in range(NQ):
                            cs = slice(tg * 512, (tg + 1) * 512)
                            hk = [("h1T", tg * 4 + q) for q in range(4)]
                            bg = rot()
                            MM([mm(pb(bg), wgub[wi][:, k, 0:128], h1T[:, k, cs], k == 0, k == 7) for k in range(8)],
                               r=[("wgub", wi)] + hk, w=[pk(bg)])
                            bu = rot()
                            MM([mm(pb(bu), wgub[wi][:, k, 128:256], h1T[:, k, cs], k == 0, k == 7) for k in range(8)],
                               r=[("wgub", wi)] + hk, w=[pk(bu)])
                            i = si[0] % 2
                            si[0] += 1
                            A(lambda: nc.scalar.activation(out=sgt[i][:], in_=pb(bg), func=AF.Silu),
                              w=[pk(bg), ("sgt", i)])
                            V(lambda: nc.vector.tensor_tensor(out=AT[:, jj, cs], in0=pb(bu), in1=sgt[i][:],
                                                              op=ALU.mult), r=[("sgt", i)], w=[pk(bu), ("AT", jj, tg)])
                    if half == 1 and s + 1 < nseq:
                        xTn = sbuf(Region([(4, 36)]), [128, 8, SEQ], BF16, "xTn")
                        for tg_ in range(NQ):
                            S.dma("pool", xTn[:, :, tg_ * 512:(tg_ + 1) * 512],
                                  D["xT"][s + 1][:, :, tg_ * 512:(tg_ + 1) * 512],
                                  writes=[("xTn", tg_)] + [("h1T", t_) for t_ in range(NTT)], sem=f"xT{tg_}")
                    for tt in range(NTT):
                        b0 = [0, 2, 4, 6][tt % 4]
                        ak = [("AT", jj, tt // 4) for jj in range(11)]
                        wk = [("WDN", jj) for jj in range(11)]
                        MM([mm(pb(b0), AT[:, jj, tt * 128:(tt + 1) * 128], WDN[:, jj, 0:512], jj == 0, jj == 10)
                            for jj in range(11)], r=ak + wk, w=[pk(b0)])
                        MM([mm(pb(b0 + 1), AT[:, jj, tt * 128:(tt + 1) * 128], WDN[:, jj, 512:1024], jj == 0, jj == 10)
                            for jj in range(11)], r=ak + wk, w=[pk(b0 + 1)])
                        V(lambda: nc.vector.tensor_tensor(out=z2acc[:, tt, :].rearrange("p (a b) -> p a b", a=2),
                                                          in0=ps[:, b0:b0 + 2, :],
                                                          in1=z2acc[:, tt, :].rearrange("p (a b) -> p a b", a=2),
                                                          op=ALU.add), w=[pk(b0), pk(b0 + 1), ("z2", tt)])
                        if half == 1:
                            i = tt % 2
                            layer_norm(z2acc[:, tt, :], ("z2", tt), lng, lnb, z2acc[:, tt, :], ("z2", tt), mv, rstd, st6)
                            S.dma("sp", out_s[tt * 128:(tt + 1) * 128, :], z2acc[:, tt, :], reads=[("z2", tt)],
                                  writes=[("outd", i)], sem=f"out{i}")
                S.barrier()
            ffn_stack.close()
        S.barrier(engines=("sp",))
        print("kernel build: n_ins", S.n_ins, "n_wait", S.n_wait, "sems", len(S.sems))
    return nc


_NC_CACHE = {}


def kernel(**inputs):
    shared = prep_shared(inputs)
    x = np.asarray(inputs["x"], np.float32)
    if "nc" not in _NC_CACHE:
        _NC_CACHE["nc"] = build_nc()
    nc = _NC_CACHE["nc"]
    in_maps = []
    for c in range(NCORES):
        m = dict(shared)
        m["x"] = np.ascontiguousarray(x[NSEQ * c:NSEQ * (c + 1)].reshape(NSEQ * SEQ, DM))
        m["xT"] = make_xT(x[NSEQ * c:NSEQ * (c + 1)])
        in_maps.append(m)
    res = run_bass_kernel_spmd(nc, in_maps, core_ids=list(range(NCORES)))
    out = np.concatenate([np.asarray(r["out"], np.float32).reshape(NSEQ, SEQ, DM) for r in res.results], axis=0)
    return out
```

```python
import math
from contextlib import ExitStack

import numpy as np
import concourse.bass as bass
import concourse.mybir as mybir
from concourse.bass_utils import run_bass_kernel_spmd

F32 = mybir.dt.float32
BF16 = mybir.dt.bfloat16
AF = mybir.ActivationFunctionType
ALU = mybir.AluOpType
AX = mybir.AxisListType

SEQ = 2048
DM = 1024
NTT = 16
NQ = 4
NSEQ = 2
NCORES = 8
BIG = -30000.0
ALPHA = 2.0 ** 0.25
LAMBDA_INIT = 0.8 - 0.6 * math.exp(0.0)
LN_EPS = 1e-5
RMS_EPS = 1e-5
FFN_H = 2816
NHC = 22
EPOCH = 12000


class Sched:
    def __init__(self, nc, es):
        self.nc = nc
        self.es = es
        self.eng = {"pe": nc.tensor, "act": nc.scalar, "dve": nc.vector,
                    "pool": nc.gpsimd, "sp": nc.sync}
        self.sems = {}
        self.cnt = {}
        self.epoch = {e: 0 for e in self.eng}
        self.waited = {}
        self.last_w = {}
        self.readers = {}
        self.n_wait = 0
        self.n_ins = 0

    def _sem(self, key):
        if key not in self.sems:
            self.sems[key] = self.es.enter_context(
                self.nc.semaphore("s_" + "_".join(str(k) for k in key)))
            self.cnt[key] = 0
        return self.sems[key]

    def _engkey(self, e):
        key = (e, self.epoch[e])
        self._sem(key)
        if self.cnt[key] >= EPOCH:
            self.epoch[e] += 1
            key = (e, self.epoch[e])
            self._sem(key)
        return key

    def _deps(self, reads, writes):
        deps = {}

        def add(tok):
            if tok is None:
                return
            k, v = tok
            if deps.get(k, 0) < v:
                deps[k] = v
        for k in reads:
            add(self.last_w.get(k))
        for k in writes:
            add(self.last_w.get(k))
            for t in self.readers.get(k, ()):
                add(t)
        return deps

    def _emit_waits(self, e, deps, skip_self=False):
        for k, v in deps.items():
            if skip_self and k[0] == e:
                continue
            if self.waited.get((e, k), 0) >= v:
                continue
            self.eng[e].wait_ge(self.sems[k], v)
            self.waited[(e, k)] = v
            self.n_wait += 1

    def _record(self, tok, reads, writes):
        for k in writes:
            self.last_w[k] = tok
            self.readers[k] = []
        for k in reads:
            self.readers.setdefault(k, []).append(tok)

    def op(self, e, fn, reads=(), writes=()):
        deps = self._deps(reads, writes)
        self._emit_waits(e, deps, skip_self=(e == "pe"))
        key = self._engkey(e)
        ins = fn()
        self.cnt[key] += 1
        ins.then_inc(self.sems[key], 1)
        self.n_ins += 1
        self._record((key, self.cnt[key]), reads, writes)

    def group(self, e, fns, reads=(), writes=()):
        deps = self._deps(reads, writes)
        self._emit_waits(e, deps, skip_self=(e == "pe"))
        key = self._engkey(e)
        ins = None
        for fn in fns:
            ins = fn()
            self.n_ins += 1
        self.cnt[key] += 1
        ins.then_inc(self.sems[key], 1)
        self._record((key, self.cnt[key]), reads, writes)

    def dma(self, q, out, in_, reads=(), writes=(), sem="d0", **kw):
        deps = self._deps(reads, writes)
        self._emit_waits(q, deps)
        key = ("dma", sem)
        self._sem(key)
        if q == "pool":
            kw.setdefault("max_dma_last_dim", 2048)
        ins = self.eng[q].dma_start(out=out, in_=in_, **kw)
        self.cnt[key] += 16
        ins.then_inc(self.sems[key], 16)
        self.n_ins += 1
        self._record((key, self.cnt[key]), reads, writes)

    def barrier(self, engines=("pe", "act", "dve", "pool", "sp")):
        deps = {k: v for k, v in self.cnt.items() if v > 0}
        for e in engines:
            self._emit_waits(e, deps)
        self.last_w = {}
        self.readers = {}


class Region:
    def __init__(self, ranges):
        self.ranges = [[a * 1024, b * 1024] for a, b in ranges]

    def alloc(self, nbytes):
        for r in self.ranges:
            off = (r[0] + 63) // 64 * 64
            if off + nbytes <= r[1]:
                r[0] = off + nbytes
                return off
        raise RuntimeError(f"region full: need {nbytes}, ranges {self.ranges}")

    def __enter__(self):
        return self

    def __exit__(self, *a):
        return False

    def close(self):
        pass


ARENA_BYTES = 203 * 1024


OFF = {}
_o = 0
for _n, _w in (("q_nsa", 512), ("k_c", 128), ("v_c", 128), ("k_s", 128), ("v_s", 128), ("k_w", 128),
               ("v_w", 128), ("g_nsa", 24), ("q_df", 512), ("k_df", 512), ("v_df", 512), ("g_m", 2048)):
    OFF[_n] = _o
    _o += _w


def _swap_cols(base, n):
    idx = []
    for c in range(n):
        h, d = divmod(c, 64)
        if d < 8:
            d2 = d + 8
        elif d < 16:
            d2 = d - 8
        else:
            d2 = d
        idx.append(base + h * 64 + d2)
    return idx


def _kmajor(w):
    K, N = w.shape
    return np.ascontiguousarray(w.reshape(K // 128, 128, N).transpose(1, 0, 2))


def prep_shared(inp):
    w_in = np.asarray(inp["w_in"], np.float32)[0]
    out = {}
    cols = []
    cols += list(range(OFF["q_nsa"], OFF["q_nsa"] + 512))
    cols += _swap_cols(OFF["q_nsa"], 512)
    cols += list(range(OFF["k_s"], OFF["k_s"] + 128))
    cols += _swap_cols(OFF["k_s"], 128)
    cols += list(range(OFF["k_w"], OFF["k_w"] + 128))
    cols += _swap_cols(OFF["k_w"], 128)
    cols += list(range(OFF["k_c"], OFF["k_c"] + 128))
    cols += list(range(OFF["v_c"], OFF["v_c"] + 128))
    out["wn"] = _kmajor(w_in[:, cols])
    cols = list(range(OFF["v_s"], OFF["v_s"] + 128)) + list(range(OFF["v_w"], OFF["v_w"] + 128))
    out["wnv"] = _kmajor(w_in[:, cols])
    out["wg"] = _kmajor(w_in[:, OFF["g_nsa"]:OFF["g_nsa"] + 24])
    cols = []
    cols += list(range(OFF["q_df"], OFF["q_df"] + 512))
    cols += _swap_cols(OFF["q_df"], 512)
    cols += list(range(OFF["k_df"], OFF["k_df"] + 512))
    cols += _swap_cols(OFF["k_df"], 512)
    out["wd"] = _kmajor(w_in[:, cols])
    out["wdv"] = _kmajor(w_in[:, OFF["v_df"]:OFF["v_df"] + 512])
    out["wgm"] = _kmajor(w_in[:, OFF["g_m"]:OFF["g_m"] + 2048])
    out["wa"] = _kmajor(np.asarray(inp["w_branch_a"], np.float32)[0])
    out["wb"] = _kmajor(np.asarray(inp["w_branch_b"], np.float32)[0])
    out["wo"] = _kmajor(np.asarray(inp["w_o"], np.float32)[0])
    wgu = np.asarray(inp["w_gate_up"], np.float32)[0]
    g = wgu[:, :FFN_H].reshape(DM, NHC, 128)
    u = wgu[:, FFN_H:].reshape(DM, NHC, 128)
    gu = np.concatenate([g, u], axis=2).reshape(DM, NHC * 256)
    out["wgu"] = _kmajor(gu)
    out["wdn"] = _kmajor(np.asarray(inp["w_down"], np.float32)[0])
    w1 = np.asarray(inp["cmp_w1"], np.float32)[0]
    w1r = w1.reshape(2, 32, 64, 256).transpose(0, 2, 1, 3)
    out["w1"] = np.ascontiguousarray(np.concatenate([w1r, w1r], axis=1))
    w2 = np.asarray(inp["cmp_w2"], np.float32)[0]
    w2k = np.concatenate([w2[0], w2[0]], axis=1)
    out["w2k"] = _kmajor(w2k)
    out["w2v"] = _kmajor(w2[1])
    b1 = np.asarray(inp["cmp_b1"], np.float32)[0]
    out["b1"] = np.ascontiguousarray(b1.reshape(2, 2, 128).transpose(2, 0, 1).reshape(128, 4))
    pe = np.asarray(inp["cmp_pe"], np.float32)[0]
    pet = pe.transpose(2, 0, 1)
    out["pet"] = np.ascontiguousarray(np.concatenate([pet, pet], axis=0).reshape(128, 64))
    dl = np.asarray(inp["diff_lambda"], np.float32)[0].reshape(1, 256)
    out["dl"] = np.ascontiguousarray(np.broadcast_to(dl, (128, 256)))
    out["ng"] = np.ascontiguousarray(np.asarray(inp["diff_norm_g"], np.float32)[0].reshape(128, 1))
    for n in ("ln1_g", "ln1_b", "ln2_g", "ln2_b"):
        v = np.asarray(inp[n], np.float32)[0].reshape(1, DM)
        out[n] = np.ascontiguousarray(np.broadcast_to(v, (128, DM)))
    out.update(make_consts())
    return out


_CONSTS = None


def make_consts():
    global _CONSTS
    if _CONSTS is not None:
        return _CONSTS
    c = {}
    c["ident"] = np.eye(128, dtype=np.float32)
    c["ones"] = np.ones((128, 128), np.float32)
    half = 8
    inv = (500000.0 ** (-(np.arange(half, dtype=np.float32) * 2.0 / 16.0))).astype(np.float32)
    ang = np.arange(SEQ, dtype=np.float32)[:, None] * inv[None, :]
    cos = np.cos(ang).astype(np.float32).T
    sin = np.sin(ang).astype(np.float32).T
    rc = np.ones((128, SEQ), np.float32)
    rs = np.zeros((128, SEQ), np.float32)
    for base in (0, 64):
        rc[base:base + 8] = cos
        rc[base + 8:base + 16] = cos
        rs[base:base + 8] = -sin
        rs[base + 8:base + 16] = sin
    c["ropec"] = rc
    c["ropes"] = rs
    k = np.arange(128)[:, None]
    q = np.arange(128)[None, :]
    tri = np.zeros((128, 256), np.float32)
    tri[:, 0:128] = np.where(k <= q, 0.0, BIG)
    tri[:, 128:256] = np.where(q < k, 0.0, BIG)
    c["tri"] = tri
    cm = np.full((128, SEQ), BIG, np.float32)
    cc = np.arange(127)[:, None]
    t = np.arange(SEQ)[None, :]
    cm[:127] = np.where(cc * 16 + 31 <= t, 0.0, BIG)
    c["cmask"] = cm
    e = np.zeros((32, SEQ), np.float32)
    e[np.arange(SEQ) // 64, np.arange(SEQ)] = 1.0
    c["efull"] = e
    ov = np.zeros((128, 33), np.float32)
    cs = np.arange(127)[:, None] * 16
    ss = np.arange(32)[None, :] * 64
    ovl = np.clip(np.minimum(cs + 32, ss + 64) - np.maximum(cs, ss), 0, None).astype(np.float32) / 32.0
    ov[:127, :32] = ovl
    ov[:127, 32] = 1.0
    c["ovaug"] = ov
    valid = np.zeros((128, 16, 32), np.float32)
    addc = np.zeros((128, 16, 32), np.float32)
    for n in range(16):
        tt = n * 128 + np.arange(128)
        tb = (tt // 64)[:, None]
        j = np.arange(32)[None, :]
        v = j <= tb
        f = (j == 0) | (j == tb) | (j == tb - 1)
        valid[:, n, :] = v
        addc[:, n, :] = np.where(v, np.where(f, 1.0e4, 0.0), -1.0)
    c["selvalid"] = valid.reshape(128, 512)
    c["seladdc"] = addc.reshape(128, 512)
    sg = np.zeros((128, 24, 128), np.float32)
    for i in range(24):
        sg[i, i, :] = 1.0
    c["selg"] = sg.reshape(128, 24 * 128)
    _CONSTS = c
    return c


def make_xT(x_core):
    xc = np.asarray(x_core, np.float32).reshape(NSEQ, SEQ, DM // 128, 128)
    return np.ascontiguousarray(xc.transpose(0, 3, 2, 1))


IN_SHAPES = {
    "x": [NSEQ * SEQ, DM], "xT": [NSEQ, 128, DM // 128, SEQ],
    "wn": [128, 8, 14 * 128], "wnv": [128, 8, 256], "wg": [128, 8, 24],
    "wd": [128, 8, 16 * 128], "wdv": [128, 8, 512], "wgm": [128, 8, 2048],
    "wa": [128, 4, 1024], "wb": [128, 4, 1024], "wo": [128, 8, 1024],
    "wgu": [128, 8, NHC * 256], "wdn": [128, NHC, 1024],
    "w1": [2, 128, 32, 256], "w2k": [128, 2, 128], "w2v": [128, 2, 64], "b1": [128, 4], "pet": [128, 64],
    "dl": [128, 256], "ng": [128, 1],
    "ln1_g": [128, DM], "ln1_b": [128, DM], "ln2_g": [128, DM], "ln2_b": [128, DM],
    "ident": [128, 128], "ones": [128, 128], "ropec": [128, SEQ], "ropes": [128, SEQ], "tri": [128, 256],
    "cmask": [128, SEQ], "efull": [32, SEQ], "ovaug": [128, 33], "selvalid": [128, 512],
    "seladdc": [128, 512], "selg": [128, 24 * 128],
}


def build_nc(debug=False, nseq=NSEQ):
    nc = bass.Bass("TRN2", target_bir_lowering=False)
    D = {n: nc.dram_tensor(n, shp, F32, kind="ExternalInput").ap() for n, shp in IN_SHAPES.items()}
    out_d = nc.dram_tensor("out", [NSEQ * SEQ, DM], F32, kind="ExternalOutput").ap()
    dbg = {}
    if debug:
        for n, shp in (("d_ya", [128, 4 * SEQ]), ("d_yb", [128, 4 * SEQ]), ("d_mt", [128, 8 * SEQ]),
                       ("d_z2", [128, NTT * DM]), ("d_t0", [128, SEQ]), ("d_qu0", [128, SEQ]),
                       ("d_ke0", [128, SEQ]), ("d_kcmp", [128, 128]), ("d_vcmp", [128, 128]),
                       ("d_t0b", [128, SEQ])):
            dbg[n] = nc.dram_tensor(n, shp, F32, kind="ExternalOutput").ap()

    with ExitStack() as es:
        S = Sched(nc, es)
        uid = [0]

        arena = es.enter_context(nc.sbuf_tensor("arena", [128, ARENA_BYTES // 2], BF16))

        def sbuf(region, shape, dt, name=None):
            nel = 1
            for d_ in shape[1:]:
                nel *= d_
            esz = 4 if dt == F32 else 2
            off = region.alloc(nel * esz)
            v = arena[0:shape[0], off // 2: off // 2 + nel * esz // 2]
            if dt == F32:
                v = v.bitcast(F32)
            if len(shape) == 3:
                v = v.rearrange("p (a b) -> p a b", a=shape[1])
            elif len(shape) == 4:
                v = v.rearrange("p (a b c) -> p a b c", a=shape[1], b=shape[2])
            return v

        GREG = Region([(0, 4)])

        ps = es.enter_context(nc.psum_tensor("ps", [128, 8, 512], F32))

        def pb(i):
            return ps[:, i, :]

        def pbf(i):
            return ps[:, i, :].bitcast(BF16)

        def pk(i):
            return ("ps", i)

        rot_state = {"banks": [0, 1, 2, 3], "i": 0, "pairs": [], "pi": 0}

        def rotpair():
            p = rot_state["pairs"][rot_state["pi"] % len(rot_state["pairs"])]
            rot_state["pi"] += 1
            return p

        def rot():
            if not rot_state["banks"]:
                return rotpair()
            b = rot_state["banks"][rot_state["i"] % len(rot_state["banks"])]
            rot_state["i"] += 1
            return b

        def set_rot(banks, pairs=()):
            rot_state["banks"] = list(banks)
            rot_state["i"] = 0
            rot_state["pairs"] = list(pairs)
            rot_state["pi"] = 0

        def V(fn, r=(), w=()):
            S.op("dve", fn, r, w)

        def A(fn, r=(), w=()):
            S.op("act", fn, r, w)

        def P(fn, r=(), w=()):
            S.op("pool", fn, r, w)

        def MM(fns, r=(), w=()):
            S.group("pe", fns, r, w)

        def mm(out, lhsT, rhs, start, stop):
            return lambda: nc.tensor.matmul(out, lhsT=lhsT, rhs=rhs, start=start, stop=stop)

        dcount = [0]

        def load(stack, name, shape, dt, src, q=None, key=None):
            t = sbuf(stack, shape, dt, name)
            dcount[0] += 1
            qq = q or ("pool" if dt == BF16 else "sp")
            S.dma(qq, t[:], src, writes=[key or name], sem=f"ld_{name}")
            return t

        def dump(name, ap_sb, key, shape, is_bf=True):
            if not debug or name == "d_mt":
                return
            S.barrier()
            stg = sbuf(Region([(195, 203)]), [128, 2048], F32, "dbgstage")
            ncol = shape[1]
            for c0 in range(0, ncol, 2048):
                c1 = min(ncol, c0 + 2048)
                V(lambda: nc.vector.tensor_copy(out=stg[0:shape[0], 0:c1 - c0], in_=ap_sb[:, c0:c1]), w=["dbgstg"])
                S.dma("sp", dbg[name][:, c0:c1], stg[0:shape[0], 0:c1 - c0], reads=["dbgstg"], sem="dbg")
            S.barrier()

        ident = load(GREG, "ident", [128, 128], BF16, D["ident"])
        ones = load(GREG, "ones", [128, 128], BF16, D["ones"])
        tri = load(GREG, "tri", [128, 256], BF16, D["tri"])
        eps_ln = sbuf(GREG, [128, 1], F32, "eps_ln")
        V(lambda: nc.vector.memset(eps_ln[:], LN_EPS), w=["eps_ln"])
        tiny_c = sbuf(GREG, [128, 1], F32, "tiny_c")
        V(lambda: nc.vector.memset(tiny_c[:], 1e-18), w=["tiny_c"])
        zero_c = sbuf(GREG, [128, 1], F32, "zero_c")
        V(lambda: nc.vector.memset(zero_c[:], 0.0), w=["tiny_c"])
        eps_rms = sbuf(GREG, [128, 1], F32, "eps_rms")
        V(lambda: nc.vector.memset(eps_rms[:], RMS_EPS), w=["eps_rms"])

        neglam = sbuf(GREG, [128, 1], F32, "neglam")
        gcol = sbuf(GREG, [128, 1], F32, "gcol")
        with Region([(153, 195)]) as st:
            dl = load(st, "dl", [128, 256], F32, D["dl"])
            ng = load(st, "ng", [128, 1], F32, D["ng"])
            tmp = sbuf(st, [128, 128], F32, "dltmp")
            s12 = sbuf(st, [128, 2], F32, "s12")
            e12 = sbuf(st, [128, 2], F32, "e12")
            V(lambda: nc.vector.tensor_tensor(out=tmp[:, 0:64], in0=dl[:, 0:64], in1=dl[:, 64:128], op=ALU.mult),
              r=["dl"], w=["dltmp"])
            V(lambda: nc.vector.tensor_tensor(out=tmp[:, 64:128], in0=dl[:, 128:192], in1=dl[:, 192:256],
                                              op=ALU.mult), r=["dl"], w=["dltmp"])
            V(lambda: nc.vector.reduce_sum(out=s12[:, 0:1], in_=tmp[:, 0:64], axis=AX.X), r=["dltmp"], w=["s12"])
            V(lambda: nc.vector.reduce_sum(out=s12[:, 1:2], in_=tmp[:, 64:128], axis=AX.X), r=["dltmp"], w=["s12"])
            A(lambda: nc.scalar.activation(out=e12[:], in_=s12[:], func=AF.Exp), r=["s12"], w=["e12"])
            V(lambda: nc.vector.tensor_tensor(out=neglam[:], in0=e12[:, 1:2], in1=e12[:, 0:1], op=ALU.subtract),
              r=["e12"], w=["neglam"])
            V(lambda: nc.vector.tensor_scalar(out=neglam[:], in0=neglam[:], scalar1=-LAMBDA_INIT, scalar2=None,
                                              op0=ALU.add), r=["neglam"], w=["neglam"])
            V(lambda: nc.vector.tensor_scalar(out=gcol[:], in0=ng[:], scalar1=1.0 - LAMBDA_INIT, scalar2=None,
                                              op0=ALU.mult), r=["ng"], w=["gcol"])
            S.barrier()

        class Pipe:
            def __init__(self, la):
                self.dq = []
                self.la = la

            def npv(self):
                return sum(1 for k, _ in self.dq if k == "pv")

            def push(self, qk, pv):
                qk()
                self.dq.append(("pv", pv))
                while self.npv() > self.la:
                    while True:
                        k, fn = self.dq.pop(0)
                        fn()
                        if k == "pv":
                            break

            def post(self, fn):
                self.dq.append(("post", fn))

            def drain(self):
                while self.dq:
                    k, fn = self.dq.pop(0)
                    fn()

        pipe = Pipe(1)
        filler = [0]

        def attn_pipe(blocks, kT, qT, vA, rkeys, acc, PT, pti, nrow=128, den_acc=None, LA=1):
            nb = len(blocks)
            steps = [list(range(i, min(i + 2, nb))) for i in range(0, nb, 2)]
            slots = {}
            sbank = {}

            def emit_qk(si):
                st_ = steps[si]
                bp = rotpair()
                fns = []
                for j, bi in enumerate(st_):
                    kb, c0, c1, mask, m0 = blocks[bi]
                    fns.append(mm(pb(bp + j)[0:nrow, c0:c1], kT(kb), qT(c0, c1), True, mask is None))
                    if mask == "causal":
                        fns.append(mm(pb(bp + j)[0:nrow, m0:m0 + 128], ident[:, :], tri[:, 0:128], False, True))
                    elif mask == "anti":
                        fns.append(mm(pb(bp + j)[0:nrow, m0:m0 + 128], ident[:, :], tri[:, 128:256], False, True))
                bkeys = [pk(bp + j) for j in range(len(st_))]
                MM(fns, r=list(rkeys), w=bkeys)
                i = pti[0] % len(PT)
                pti[0] += 1
                pt = PT[i]
                rngs = [blocks[bi][1:3] for bi in st_]
                if len(st_) == 2 and rngs[0] == rngs[1]:
                    c0, c1 = rngs[0]
                    A(lambda: nc.scalar.activation(out=pt[0:nrow, 0:2, c0:c1], in_=ps[0:nrow, bp:bp + 2, c0:c1],
                                                   func=AF.Exp, scale=0.125), w=bkeys + [("PT", i)])
                else:
                    for j, (c0, c1) in enumerate(rngs):
                        A(lambda: nc.scalar.activation(out=pt[0:nrow, j, c0:c1], in_=pb(bp + j)[0:nrow, c0:c1],
                                                       func=AF.Exp, scale=0.125), w=[pk(bp + j), ("PT", i)])
                slots[si] = i
                sbank[si] = bp

            def emit_pv(si):
                i = slots.pop(si)
                pt = PT[i]
                fns = []
                for j, bi in enumerate(steps[si]):
                    kb, c0, c1, mask, m0 = blocks[bi]
                    fns.append(mm(pb(acc)[:, c0:c1], vA(kb), pt[0:nrow, j, c0:c1], bi == 0, bi == nb - 1))
                MM(fns, r=[("PT", i)] + list(rkeys), w=[pk(acc)])
                if filler[0] and den_acc is None:
                    fb = sbank[si]
                    MM([mm(pb(fb)[:, :], ident[:, :], pt[:, 0, :], True, True) for _ in range(filler[0])],
                       r=[("PT", i), "ident"], w=[pk(fb)])
                if den_acc is not None:
                    fns = []
                    for j, bi in enumerate(steps[si]):
                        kb, c0, c1, mask, m0 = blocks[bi]
                        fns.append(mm(pb(den_acc)[:, c0:c1], ones[0:nrow, :], pt[0:nrow, j, c0:c1],
                                      bi == 0, bi == nb - 1))
                    MM(fns, r=[("PT", i), "ones"], w=[pk(den_acc)])

            for si in range(len(steps)):
                pipe.push(lambda si=si: emit_qk(si), lambda si=si: emit_pv(si))

        wb_i = [0]

        for s in range(nseq):
            x_s = D["x"][s * SEQ:(s + 1) * SEQ, :]
            out_s = out_d[s * SEQ:(s + 1) * SEQ, :]
            seq_stack = Region([(4, 36)])
            set_rot([0, 1, 2, 3, 4, 5, 6, 7])
            xT = sbuf(seq_stack, [128, 8, SEQ], BF16, "xT")
            xT_src = D["xT"][s]

            def xT_issue(tg_):
                S.dma("pool", xT[:, :, tg_ * 512:(tg_ + 1) * 512], xT_src[:, :, tg_ * 512:(tg_ + 1) * 512],
                      writes=[("xT", tg_ * 4 + i_) for i_ in range(4)], sem=f"xT{tg_}")

            xT_issue(0)

            def xkeys(tg):
                return [("xT", tg * 4 + i) for i in range(4)]

            def proj_bank(wt, wkey, tg):
                bk = rot()
                MM([mm(pb(bk), wt[:, k, :], xT[:, k, tg * 512:(tg + 1) * 512], k == 0, k == 7) for k in range(8)],
                   r=[wkey] + xkeys(tg), w=[pk(bk)])
                return bk

            ya_stack = Region([(36, 68)])
            YA = sbuf(ya_stack, [128, 4, SEQ], BF16, "YA")
            YB = sbuf(ya_stack, [128, 4, SEQ], BF16, "YB")

            nsa = Region([(68, 154)])
            Tq = [sbuf(nsa, [128, SEQ], BF16, f"Tq{h}") for h in range(8)]
            QU = [sbuf(nsa, [128, SEQ], BF16, f"QU{j}") for j in range(4)]
            KE = [sbuf(nsa, [128, SEQ], BF16, f"KE{g}") for g in range(2)]
            KW = [sbuf(nsa, [128, SEQ], BF16, f"KW{g}") for g in range(2)]
            VS = sbuf(nsa, [128, NTT, 2, 128], BF16, "VS")
            VW = sbuf(nsa, [128, NTT, 2, 128], BF16, "VW")
            GT = sbuf(nsa, [128, SEQ], BF16, "GT")
            kcmpL = [sbuf(nsa, [128, 128], BF16, f"kcmpL{g}") for g in range(2)]
            kcmpH = [sbuf(nsa, [128, 128], BF16, f"kcmpH{g}") for g in range(2)]
            vcmpA = [sbuf(nsa, [128, 128], BF16, f"vcmpA{g}") for g in range(2)]
            for g in range(2):
                S.dma("pool", KE[g][64:96, :], D["efull"], writes=[("KE", g)], sem=f"ke{g}")
                V(lambda: nc.vector.memset(vcmpA[g][:], 1.0), w=[("vcmpA", g)])
            for h_ in range(8):
                V(lambda: nc.vector.memset(Tq[h_][64:96, :], 0.0), w=[("TqN", h_, q_) for q_ in range(NQ)])
            V(lambda: nc.vector.memset(GT[:], 0.0), w=["GT"])
            for g in range(2):
                V(lambda: nc.vector.memset(KW[g][64:96, :], 0.0), w=[("KW", g)])
                V(lambda: nc.vector.memset(kcmpL[g][:], 0.0), w=[("kcmpT", g)])
                V(lambda: nc.vector.memset(kcmpH[g][:], 0.0), w=[("kcmpT", g)])
            V(lambda: nc.vector.memset(VS[:], 1.0), w=["VS"])
            V(lambda: nc.vector.memset(VW[:], 1.0), w=["VW"])

            set_rot([0, 1, 2, 3, 4, 5, 6, 7])
            with Region([(36, 68), (154, 195)]) as st:
                ropec = load(st, "ropec", [128, SEQ], F32, D["ropec"])
                ropes = load(st, "ropes", [128, SEQ], F32, D["ropes"])
                w1h = [sbuf(st, [128, 32, 128], BF16, f"w1h{i}") for i in range(2)]

                def w1_issue(p_):
                    kv_, hc_ = divmod(p_, 2)
                    S.dma("pool", w1h[p_ % 2][:], D["w1"][kv_][:, :, hc_ * 128:(hc_ + 1) * 128],
                          writes=[("w1h", p_ % 2)], sem=f"w1h{p_ % 2}")
                kcT = sbuf(st, [128, SEQ + 32], BF16, "kcT")
                vcT = sbuf(st, [128, SEQ + 32], BF16, "vcT")
                S.dma("pool", kcT[:, SEQ:SEQ + 32], D["pet"][:, 0:32], writes=["kcT_pe"], sem="pet0")
                S.dma("pool", vcT[:, SEQ:SEQ + 32], D["pet"][:, 32:64], writes=["vcT_pe"], sem="pet1")
                wbufs = [sbuf(st, [128, 8, 128], BF16, f"wbuf{i}") for i in range(4)]
                tmpA = [sbuf(st, [128, 512], F32, f"tmpA{i}") for i in range(2)]
                tmpB = [sbuf(st, [128, 512], F32, f"tmpB{i}") for i in range(2)]
                tcount = [0]

                def wload(src_ap):
                    i = wb_i[0] % 4
                    wb_i[0] += 1
                    S.dma("pool", wbufs[i][:], src_ap, writes=[("wbuf", i)], sem=f"wbuf{i}")
                    return wbufs[i], ("wbuf", i)

                def rope_evac(bA, bB, tg, dests, plain=None):
                    i = tcount[0] % 2
                    tcount[0] += 1
                    cs = slice(tg * 512, (tg + 1) * 512)
                    tA, tB = tmpA[i], tmpB[i]
                    if plain is not None:
                        dt_, dk = plain
                        A(lambda: nc.scalar.copy(out=dt_[:, cs], in_=pb(bA)), w=[pk(bA), dk])
                    V(lambda: nc.vector.tensor_tensor(out=tA[:], in0=pb(bA), in1=ropec[:, cs], op=ALU.mult),
                      r=["ropec"], w=[pk(bA), ("tmpA", i)])
                    V(lambda: nc.vector.tensor_tensor(out=tB[:], in0=pb(bB), in1=ropes[:, cs], op=ALU.mult),
                      r=["ropes"], w=[pk(bB), ("tmpB", i)])
                    for (r0, dtile, dkey, d0, nrow) in dests:
                        V(lambda: nc.vector.tensor_tensor(out=dtile[d0:d0 + nrow, cs], in0=tA[r0:r0 + nrow, :],
                                                          in1=tB[r0:r0 + nrow, :], op=ALU.add),
                          r=[("tmpA", i), ("tmpB", i)], w=[dkey])

                wn = D["wn"]
                for j in range(4):
                    wA, kA = wload(wn[:, :, j * 128:(j + 1) * 128])
                    wB, kB = wload(wn[:, :, (4 + j) * 128:(5 + j) * 128])
                    if j == 0:
                        for tg_ in range(1, NQ):
                            xT_issue(tg_)
                    for tg in range(NQ):
                        bA = proj_bank(wA, kA, tg)
                        bB = proj_bank(wB, kB, tg)
                        rope_evac(bA, bB, tg,
                                  [(0, Tq[2 * j], ("Tq", 2 * j, tg), 0, 64),
                                   (64, Tq[2 * j + 1], ("Tq", 2 * j + 1, tg), 0, 64)],
                                  plain=(QU[j], ("QU", j, tg)))
                    if j == 1:
                        w1_issue(0)
                        w1_issue(1)
                for (t0, dst, nm) in ((8, KE, "KE"), (10, KW, "KW")):
                    wA, kA = wload(wn[:, :, t0 * 128:(t0 + 1) * 128])
                    wB, kB = wload(wn[:, :, (t0 + 1) * 128:(t0 + 2) * 128])
                    for tg in range(NQ):
                        bA = proj_bank(wA, kA, tg)
                        bB = proj_bank(wB, kB, tg)
                        rope_evac(bA, bB, tg, [(0, dst[0], (nm, 0), 0, 64), (64, dst[1], (nm, 1), 0, 64)])
                for (t0, dst, nm) in ((12, kcT, "kcT"), (13, vcT, "vcT")):
                    wA, kA = wload(wn[:, :, t0 * 128:(t0 + 1) * 128])
                    for tg in range(NQ):
                        bA = proj_bank(wA, kA, tg)
                        if tg % 2 == 0:
                            A(lambda: nc.scalar.copy(out=dst[:, tg * 512:(tg + 1) * 512], in_=pb(bA)),
                              w=[pk(bA), nm])
                        else:
                            V(lambda: nc.vector.tensor_copy(out=dst[:, tg * 512:(tg + 1) * 512], in_=pb(bA)),
                              w=[pk(bA), nm])
                wgt = load(st, "wg", [128, 8, 24], BF16, D["wg"])
                for tg in range(NQ):
                    bk = rot()
                    MM([mm(pb(bk)[0:24, :], wgt[:, k, :], xT[:, k, tg * 512:(tg + 1) * 512], k == 0, k == 7)
                        for k in range(8)], r=["wg"] + xkeys(tg), w=[pk(bk)])
                    A(lambda: nc.scalar.activation(out=GT[0:24, tg * 512:(tg + 1) * 512], in_=pb(bk)[0:24, :],
                                                   func=AF.Sigmoid), w=[pk(bk), "GT"])
                wnv = load(st, "wnv", [128, 8, 256], BF16, D["wnv"])
                for tt in range(NTT):
                    bk = rot()
                    MM([mm(pb(bk)[:, 0:256], xT[:, k, tt * 128:(tt + 1) * 128], wnv[:, k, :], k == 0, k == 7)
                        for k in range(8)], r=["wnv", ("xT", tt)], w=[pk(bk)])
                    srcs = pb(bk)[:, 0:128].rearrange("p (g d) -> p g d", g=2)
                    srcw = pb(bk)[:, 128:256].rearrange("p (g d) -> p g d", g=2)
                    V(lambda: nc.vector.tensor_copy(out=VS[:, tt, :, 0:64], in_=srcs), w=[pk(bk), "VS"])
                    A(lambda: nc.scalar.copy(out=VW[:, tt, :, 0:64], in_=srcw), w=[pk(bk), "VW"])

                w2k = load(st, "w2k", [128, 2, 128], BF16, D["w2k"])
                w2v = load(st, "w2v", [128, 2, 64], BF16, D["w2v"])
                b1 = load(st, "b1", [128, 4], F32, D["b1"])
                hb = [sbuf(st, [128, 1], F32, f"hb{i}") for i in range(2)]
                xh = [sbuf(st, [128, 128], F32, f"xh{i}") for i in range(2)]
                x2 = [sbuf(st, [128, 128], F32, f"x2{i}") for i in range(2)]
                sgm = [sbuf(st, [128, 128], F32, f"sgm{i}") for i in range(2)]
                gT = sbuf(st, [128, 2, 2, 128], BF16, "gT")
                it = [0]
                for kv, srcT, srck in ((0, kcT, "kcT"), (1, vcT, "vcT")):
                    for hc in range(2):
                        p_ = kv * 2 + hc
                        wb_ = w1h[p_ % 2]
                        for g in range(2):
                            r0 = g * 64
                            i = it[0] % 2
                            it[0] += 1
                            bk = rot()
                            MM([mm(pb(bk)[:, 0:129], wb_[r0:r0 + 64, l, :],
                                   srcT[r0:r0 + 64, l:l + 16 * 128 + 1:16], l == 0, l == 31)
                                for l in range(32)], r=[("w1h", p_ % 2), srck, srck + "_pe"], w=[pk(bk)])
                            V(lambda: nc.vector.tensor_tensor(out=hb[i][:], in0=pb(bk)[:, 128:129],
                                                              in1=b1[:, kv * 2 + hc:kv * 2 + hc + 1], op=ALU.add),
                              r=["b1"], w=[pk(bk), ("hb", i)])
                            A(lambda: nc.scalar.activation(out=xh[i][:, 0:127], in_=pb(bk)[:, 0:127],
                                                           func=AF.Identity, bias=hb[i][:]),
                              r=[("hb", i)], w=[pk(bk), ("xh", i)])
                            V(lambda: nc.vector.tensor_tensor(out=x2[i][:, 0:127], in0=xh[i][:, 0:127],
                                                              in1=xh[i][:, 0:127], op=ALU.mult),
                              r=[("xh", i)], w=[("x2", i)])
                            V(lambda: nc.vector.tensor_scalar(out=x2[i][:, 0:127], in0=x2[i][:, 0:127],
                                                              scalar1=0.044715, scalar2=1.0, op0=ALU.mult,
                                                              op1=ALU.add), w=[("x2", i)])
                            V(lambda: nc.vector.tensor_tensor(out=x2[i][:, 0:127], in0=x2[i][:, 0:127],
                                                              in1=xh[i][:, 0:127], op=ALU.mult),
                              r=[("xh", i)], w=[("x2", i)])
                            A(lambda: nc.scalar.activation(out=sgm[i][:, 0:127], in_=x2[i][:, 0:127], func=AF.Sigmoid,
                                                           scale=2.0 * math.sqrt(2.0 / math.pi)),
                              r=[("x2", i)], w=[("sgm", i)])
                            V(lambda: nc.vector.tensor_tensor(out=gT[:, g, hc, 0:127], in0=xh[i][:, 0:127],
                                                              in1=sgm[i][:, 0:127], op=ALU.mult),
                              r=[("xh", i), ("sgm", i)], w=[("gT", g, hc)])
                        if p_ + 2 < 4:
                            w1_issue(p_ + 2)
                    for g in range(2):
                        bk = rot()
                        gk = [("gT", g, 0), ("gT", g, 1)]
                        if kv == 0:
                            MM([mm(pb(bk)[:, 0:127], w2k[:, hc, :], gT[:, g, hc, 0:127], hc == 0, hc == 1)
                                for hc in range(2)], r=["w2k"] + gk, w=[pk(bk)])
                            V(lambda: nc.vector.tensor_copy(out=kcmpL[g][0:64, 0:127], in_=pb(bk)[0:64, 0:127]),
                              w=[pk(bk), ("kcmpT", g)])
                            V(lambda: nc.vector.tensor_copy(out=kcmpH[g][64:128, 0:127], in_=pb(bk)[64:128, 0:127]),
                              w=[pk(bk), ("kcmpT", g)])
                        else:
                            MM([mm(pb(bk)[0:127, 0:64], gT[:, g, hc, 0:127], w2v[:, hc, :], hc == 0, hc == 1)
                                for hc in range(2)], r=["w2v"] + gk, w=[pk(bk)])
                            V(lambda: nc.vector.tensor_copy(out=vcmpA[g][0:127, 0:64], in_=pb(bk)[0:127, 0:64]),
                              w=[pk(bk), ("vcmpA", g)])
                if s == 0:
                    dump("d_t0", Tq[0][:], ("Tq", 0, 0), [128, SEQ])
                    dump("d_qu0", QU[0][:], ("QU", 0, 0), [128, SEQ])
                    dump("d_ke0", KE[0][:], ("KE", 0), [128, SEQ])
                    dump("d_kcmp", kcmpL[0][:], ("kcmpT", 0), [128, 128])
                    dump("d_vcmp", vcmpA[0][:], ("vcmpA", 0), [128, 128])
                S.barrier()

            set_rot([], pairs=[0, 2])
            ACC = [4, 5, 6, 7]
            ACCP = [4, 6]
            pipe.la = 2
            with Region([(52, 68), (154, 195)]) as st:
                cmask = load(st, "cmask", [128, SEQ], BF16, D["cmask"])
                ovaug = load(st, "ovaug", [128, 33], BF16, D["ovaug"])
                selvalid = load(st, "selvalid", [128, 512], F32, D["selvalid"])
                seladdc = load(st, "seladdc", [128, 512], F32, D["seladdc"])
                selg = load(st, "selg", [128, 24 * 128], BF16, D["selg"])
                PT = [sbuf(st, [128, 2, 512], BF16, f"PT{i}") for i in range(4)]
                PTc = sbuf(st, [128, 4, 512], BF16, "PTc")
                GB = sbuf(st, [64, 12, 512], BF16, "GB")
                yacc = sbuf(st, [64, 4, 512], F32, "yacc")
                rr_all = sbuf(st, [128, 4, 512], F32, "rr_all")
                rr = [rr_all[:, 2 * i:2 * i + 2, :] for i in range(2)]
                den16 = sbuf(st, [128, 16], F32, "den16")
                pn = sbuf(Region([(195, 203)]), [128, 512], F32, "pn")
                pri = sbuf(st, [128, 128], F32, "pri")
                top8 = sbuf(st, [128, 32], F32, "top8")
                nsel = sbuf(st, [128, 128], BF16, "nsel")
                pti = [0]
                acci = [0]
                nrm = [0]

                def attn(blocks, kT, qT, vA, rkeys, acc, nrow=128, den_acc=None):
                    attn_pipe(blocks, kT, qT, vA, rkeys, acc, PT, pti, nrow, den_acc)

                def combine_cmp(acc, h, Q, hh):
                    i = nrm[0] % 2
                    nrm[0] += 1
                    A(lambda: nc.scalar.activation(out=rr[i][64:128, 0, :], in_=pb(acc)[64:128, :], func=AF.Ln,
                                                   bias=tiny_c[64:128, :]), r=["tiny_c"], w=[pk(acc), ("rr", i)])
                    A(lambda: nc.scalar.activation(out=rr[i][64:128, 0, :], in_=rr[i][64:128, 0, :], func=AF.Exp,
                                                   scale=-1.0), w=[("rr", i)])
                    V(lambda: nc.vector.tensor_tensor(out=rr[i][0:64, 0, :], in0=pb(acc)[0:64, :],
                                                      in1=rr[i][64:128, 0, :], op=ALU.mult),
                      w=[pk(acc), ("rr", i)])
                    V(lambda: nc.vector.tensor_tensor(out=yacc[:, hh, :], in0=rr[i][0:64, 0, :], in1=GB[:, hh * 3, :],
                                                      op=ALU.mult), r=[("rr", i), ("GB", hh)], w=[("yacc", hh)])

                def combine2(accp, h, Q, hh):
                    i = nrm[0] % 2
                    nrm[0] += 1
                    cs = slice(Q * 512, (Q + 1) * 512)
                    A(lambda: nc.scalar.activation(out=rr[i][64:128, :, :], in_=ps[64:128, accp:accp + 2, :],
                                                   func=AF.Ln, bias=zero_c[64:128, :]),
                      r=["tiny_c"], w=[pk(accp), pk(accp + 1), ("rr", i)])
                    A(lambda: nc.scalar.activation(out=rr[i][64:128, :, :], in_=rr[i][64:128, :, :], func=AF.Exp,
                                                   scale=-1.0), w=[("rr", i)])
                    V(lambda: nc.vector.tensor_tensor(out=rr[i][0:64, :, :], in0=ps[0:64, accp:accp + 2, :],
                                                      in1=rr[i][64:128, :, :], op=ALU.mult),
                      w=[pk(accp), pk(accp + 1), ("rr", i)])
                    V(lambda: nc.vector.tensor_tensor(out=rr[i][0:64, :, :], in0=rr[i][0:64, :, :],
                                                      in1=GB[:, hh * 3 + 1:hh * 3 + 3, :], op=ALU.mult),
                      r=[("GB", hh)], w=[("rr", i)])
                    V(lambda: nc.vector.tensor_tensor(out=rr[i][0:64, 0, :], in0=rr[i][0:64, 0, :],
                                                      in1=rr[i][0:64, 1, :], op=ALU.add), w=[("rr", i)])
                    r0 = (h % 2) * 64
                    V(lambda: nc.vector.tensor_tensor(out=YA[r0:r0 + 64, h // 2, cs], in0=yacc[:, hh, :],
                                                      in1=rr[i][0:64, 0, :], op=ALU.add),
                      r=[("rr", i), ("yacc", hh)], w=[("YA", h // 2)])

                for g in range(2):
                    for Q in range(NQ):
                        cs = slice(Q * 512, (Q + 1) * 512)
                        for c12 in range(12):
                            gbk = c12 % 4
                            c = g * 12 + c12
                            MM([mm(pb(gbk)[:, :], selg[:, c * 128:(c + 1) * 128], GT[:, cs], True, True)],
                               r=["selg", "GT"], w=[pk(gbk)])
                            A(lambda: nc.scalar.copy(out=GB[:, c12, :], in_=pb(gbk)[0:64, :]),
                              w=[pk(gbk), ("GB", c12 // 3)])
                        bps = [rotpair(), rotpair()]
                        for hh in range(4):
                            h = g * 4 + hh
                            j, r0 = h // 2, (h % 2) * 64
                            sbk = bps[hh // 2] + hh % 2
                            kc_ = kcmpL[g] if r0 == 0 else kcmpH[g]
                            MM([mm(pb(sbk)[0:127, :], kc_[:, 0:127], QU[j][:, cs], True, False),
                                mm(pb(sbk)[0:127, :], ident[0:127, 0:127], cmask[0:127, cs], False, True)],
                               r=[("kcmpT", g), ("QU", j, Q), "cmask", "ident"], w=[pk(sbk)])
                        for pp in range(2):
                            bp = bps[pp]
                            A(lambda: nc.scalar.activation(out=PTc[0:127, 2 * pp:2 * pp + 2, :],
                                                           in_=ps[0:127, bp:bp + 2, :], func=AF.Exp, scale=0.125),
                              w=[pk(bp), pk(bp + 1), ("PTc", pp)])
                        for hh in range(4):
                            MM([mm(pb(4 + hh)[:, :], vcmpA[g][0:127, :], PTc[0:127, hh, :], True, True)],
                               r=[("PTc", hh // 2), ("vcmpA", g)], w=[pk(4 + hh)])
                        rkeys4 = [("rr", 0), ("rr", 1)]
                        akeys4 = [pk(4 + hh) for hh in range(4)]
                        A(lambda: nc.scalar.activation(out=rr_all[64:128, :, :], in_=ps[64:128, 4:8, :], func=AF.Ln,
                                                       bias=tiny_c[64:128, :]), r=["tiny_c"], w=akeys4 + rkeys4)
                        A(lambda: nc.scalar.activation(out=rr_all[64:128, :, :], in_=rr_all[64:128, :, :],
                                                       func=AF.Exp, scale=-1.0), w=rkeys4)
                        rp = rotpair()
                        fns = []
                        for n in range(4):
                            for hh in range(4):
                                c16 = n * 4 + hh
                                fns.append(mm(pb(rp)[:, c16 * 32:(c16 + 1) * 32], PTc[0:127, hh, n * 128:(n + 1) * 128],
                                              ovaug[0:127, 0:32], True, True))
                                fns.append(mm(pb(rp + 1)[:, c16:c16 + 1], PTc[0:127, hh, n * 128:(n + 1) * 128],
                                              ovaug[0:127, 32:33], True, True))
                        MM(fns, r=[("PTc", 0), ("PTc", 1), "ovaug"], w=[pk(rp), pk(rp + 1)])
                        V(lambda: nc.vector.tensor_scalar(out=den16[:], in0=pb(rp + 1)[:, 0:16], scalar1=1e-30,
                                                          scalar2=None, op0=ALU.max), w=[pk(rp + 1), "den16"])
                        V(lambda: nc.vector.reciprocal(out=den16[:], in_=den16[:]), w=["den16"])
                        V(lambda: nc.vector.tensor_tensor(
                            out=pn[:].rearrange("p (c j) -> p c j", j=32),
                            in0=pb(rp)[:, :].rearrange("p (c j) -> p c j", j=32),
                            in1=den16[:].unsqueeze(2).broadcast_to([128, 16, 32]), op=ALU.mult),
                          r=["den16"], w=[pk(rp), "pn"])
                        V(lambda: nc.vector.tensor_reduce(out=pri[:].rearrange("p (n j) -> p n j", n=4),
                                                          in_=pn[:].rearrange("p (n h j) -> p n j h", n=4, h=4),
                                                          axis=AX.X, op=ALU.add), r=["pn"], w=["pri"])
                        V(lambda: nc.vector.tensor_tensor(out=pri[:], in0=pri[:],
                                                          in1=selvalid[:, Q * 128:(Q + 1) * 128], op=ALU.mult),
                          r=["selvalid"], w=["pri"])
                        V(lambda: nc.vector.tensor_tensor(out=pri[:], in0=pri[:],
                                                          in1=seladdc[:, Q * 128:(Q + 1) * 128], op=ALU.add),
                          r=["seladdc"], w=["pri"])
                        for n in range(4):
                            V(lambda: nc.vector.max(out=top8[:, n * 8:(n + 1) * 8], in_=pri[:, n * 32:(n + 1) * 32]),
                              r=["pri"], w=["top8"])
                        V(lambda: nc.vector.tensor_tensor(
                            out=pn[:, 0:128].rearrange("p (n j) -> p n j", n=4),
                            in0=pri[:].rearrange("p (n j) -> p n j", n=4),
                            in1=top8[:].rearrange("p (n e) -> p n e", n=4)[:, :, 7:8].broadcast_to([128, 4, 32]),
                            op=ALU.is_lt), r=["pri", "top8"], w=["pn"])
                        V(lambda: nc.vector.tensor_scalar(out=nsel[:], in0=pn[:, 0:128], scalar1=BIG, scalar2=None,
                                                          op0=ALU.mult), r=["pn"], w=["nsel"])
                        V(lambda: nc.vector.tensor_tensor(out=rr_all[0:64, :, :], in0=ps[0:64, 4:8, :],
                                                          in1=rr_all[64:128, :, :], op=ALU.mult), w=akeys4 + rkeys4)
                        V(lambda: nc.vector.tensor_tensor(out=yacc[:, :, :], in0=rr_all[0:64, :, :],
                                                          in1=GB[:, 0:12:3, :], op=ALU.mult),
                          r=rkeys4 + [("GB", hh) for hh in range(4)], w=[("yacc", hh) for hh in range(4)])

                        def sel_finish(g=g, Q=Q, cs=cs):
                            tb = rot()
                            MM([(lambda n=n: nc.tensor.transpose(out=pbf(tb)[0:32, n * 128:(n + 1) * 128],
                                                                 in_=nsel[:, n * 32:(n + 1) * 32], identity=ident[:]))
                                for n in range(4)], r=["nsel", "ident"], w=[pk(tb)])
                            for hh_ in range(4):
                                h_ = g * 4 + hh_
                                dst = Tq[h_][64:96, cs]
                                if hh_ % 2 == 0:
                                    V(lambda: nc.vector.tensor_copy(out=dst, in_=pbf(tb)[0:32, 0:512]),
                                      w=[pk(tb), ("TqN", h_, Q)])
                                else:
                                    A(lambda: nc.scalar.copy(out=dst, in_=pbf(tb)[0:32, 0:512]),
                                      w=[pk(tb), ("TqN", h_, Q)])

                        for hh in range(4):
                            h = g * 4 + hh
                            accp = ACCP[acci[0] % 2]
                            acci[0] += 1
                            blocks = [(4 * Q + o, 128 * o, 512, "causal", 128 * o) for o in range(4)]
                            if Q > 0:
                                blocks += [(4 * Q - 4 + o, 0, 128 * (o + 1), "anti", 128 * o) for o in range(4)]
                            attn(blocks,
                                 lambda kb, g=g: KW[g][0:96, kb * 128:(kb + 1) * 128],
                                 lambda c0, c1, h=h, Q=Q: Tq[h][0:96, Q * 512 + c0:Q * 512 + c1],
                                 lambda kb, g=g: VW[:, kb, g, :],
                                 [("KW", g), ("Tq", h, Q), "VW", "ident", "tri"], accp + 1)
                            if hh == 0:
                                sel_finish()
                            blocks = [(kb, 0, 512, None, 0) for kb in range(4 * Q)]
                            blocks += [(4 * Q + o, 128 * o, 512, "causal", 128 * o) for o in range(4)]
                            attn(blocks,
                                 lambda kb, g=g: KE[g][0:96, kb * 128:(kb + 1) * 128],
                                 lambda c0, c1, h=h, Q=Q: Tq[h][0:96, Q * 512 + c0:Q * 512 + c1],
                                 lambda kb, g=g: VS[:, kb, g, :],
                                 [("KE", g), ("Tq", h, Q), ("TqN", h, Q), "VS", "ident", "tri"], accp)
                            pipe.post(lambda accp=accp, h=h, Q=Q, hh=hh: combine2(accp, h, Q, hh))
                        pipe.drain()
                if s == 0:
                    dump("d_t0b", Tq[0][:], ("Tq", 0, 0), [128, SEQ])
                S.barrier()
            nsa.close()

            set_rot([0, 1, 2, 3, 4, 5, 6, 7])
            with Region([(68, 132)]) as dst_:
                QD = sbuf(dst_, [128, 4, SEQ], BF16, "QD")
                KD = sbuf(dst_, [128, 4, SEQ], BF16, "KD")
                KD1 = sbuf(dst_, [128, 4, SEQ], BF16, "KD1")
                V(lambda: nc.vector.memset(KD[64:128, :, :], 0.0), w=[("KD", j) for j in range(4)])
                V(lambda: nc.vector.memset(KD1[0:64, :, :], 0.0), w=[("KD", j) for j in range(4)])
                VD = sbuf(dst_, [128, NTT, 512], BF16, "VD")
                with Region([(132, 195)]) as st:
                    ropec = load(st, "ropec", [128, SEQ], F32, D["ropec"])
                    ropes = load(st, "ropes", [128, SEQ], F32, D["ropes"])
                    wbufs = [sbuf(st, [128, 8, 128], BF16, f"wbuf{i}") for i in range(4)]
                    tmpA = [sbuf(st, [128, 512], F32, f"tmpA{i}") for i in range(2)]
                    tmpB = [sbuf(st, [128, 512], F32, f"tmpB{i}") for i in range(2)]
                    wd = D["wd"]
                    tci = [0]
                    for (t0, dstt, nm) in ((0, QD, "QD"), (8, KD, "KD")):
                        for j in range(4):
                            i0 = wb_i[0] % 4
                            wb_i[0] += 1
                            i1 = wb_i[0] % 4
                            wb_i[0] += 1
                            S.dma("pool", wbufs[i0][:], wd[:, :, (t0 + j) * 128:(t0 + j + 1) * 128],
                                  writes=[("wbuf", i0)], sem=f"wbuf{i0}")
                            S.dma("pool", wbufs[i1][:], wd[:, :, (t0 + 4 + j) * 128:(t0 + 5 + j) * 128],
                                  writes=[("wbuf", i1)], sem=f"wbuf{i1}")
                            for tg in range(NQ):
                                bA = proj_bank(wbufs[i0], ("wbuf", i0), tg)
                                bB = proj_bank(wbufs[i1], ("wbuf", i1), tg)
                                i = tci[0] % 2
                                tci[0] += 1
                                cs = slice(tg * 512, (tg + 1) * 512)
                                V(lambda: nc.vector.tensor_tensor(out=tmpA[i][:], in0=pb(bA), in1=ropec[:, cs],
                                                                  op=ALU.mult), r=["ropec"], w=[pk(bA), ("tmpA", i)])
                                V(lambda: nc.vector.tensor_tensor(out=tmpB[i][:], in0=pb(bB), in1=ropes[:, cs],
                                                                  op=ALU.mult), r=["ropes"], w=[pk(bB), ("tmpB", i)])
                                if nm == "QD":
                                    V(lambda: nc.vector.tensor_tensor(out=dstt[:, j, cs], in0=tmpA[i][:],
                                                                      in1=tmpB[i][:], op=ALU.add),
                                      r=[("tmpA", i), ("tmpB", i)], w=[(nm, j)])
                                else:
                                    V(lambda: nc.vector.tensor_tensor(out=KD[0:64, j, cs], in0=tmpA[i][0:64, :],
                                                                      in1=tmpB[i][0:64, :], op=ALU.add),
                                      r=[("tmpA", i), ("tmpB", i)], w=[(nm, j)])
                                    V(lambda: nc.vector.tensor_tensor(out=KD1[64:128, j, cs], in0=tmpA[i][64:128, :],
                                                                      in1=tmpB[i][64:128, :], op=ALU.add),
                                      r=[("tmpA", i), ("tmpB", i)], w=[(nm, j)])
                    wdv = load(st, "wdv", [128, 8, 512], BF16, D["wdv"])
                    for tt in range(NTT):
                        bk = rot()
                        MM([mm(pb(bk), xT[:, k, tt * 128:(tt + 1) * 128], wdv[:, k, :], k == 0, k == 7)
                            for k in range(8)], r=["wdv", ("xT", tt)], w=[pk(bk)])
                        if tt % 2 == 0:
                            V(lambda: nc.vector.tensor_copy(out=VD[:, tt, :], in_=pb(bk)), w=[pk(bk), "VD"])
                        else:
                            A(lambda: nc.scalar.copy(out=VD[:, tt, :], in_=pb(bk)), w=[pk(bk), "VD"])
                    S.barrier()
                set_rot([], pairs=[0, 2])
                pipe.la = 2
                with Region([(132, 195)]) as st:
                    PT = [sbuf(st, [128, 2, 512], BF16, f"PT{i}") for i in range(4)]
                    a0 = sbuf(st, [128, 512], F32, "a0")
                    a1 = sbuf(st, [128, 512], F32, "a1")
                    r0t = sbuf(st, [128, 512], F32, "r0t")
                    of2 = [sbuf(st, [128, 512], F32, f"of{i}") for i in range(2)]
                    sq2 = [sbuf(st, [128, 512], BF16, f"sq{i}") for i in range(2)]
                    sd = sbuf(st, [128, 512], F32, "sd")
                    pti = [0]

                    def diff_post(c, oacc, dacc):
                        dstA = a0 if c == 0 else a1
                        A(lambda: nc.scalar.activation(out=r0t[:], in_=pb(dacc), func=AF.Ln),
                          w=[pk(dacc), "r0t"])
                        A(lambda: nc.scalar.activation(out=r0t[:], in_=r0t[:], func=AF.Exp, scale=-1.0),
                          w=["r0t"])
                        V(lambda: nc.vector.tensor_tensor(out=dstA[:], in0=pb(oacc), in1=r0t[:], op=ALU.mult),
                          r=["r0t"], w=[pk(oacc), ("a", c)])

                    def diff_final_a(pb_):
                        V(lambda: nc.vector.scalar_tensor_tensor(out=of2[pb_][:], in0=a1[:], scalar=neglam[:, 0:1],
                                                                 in1=a0[:], op0=ALU.mult, op1=ALU.add),
                          r=[("a", 0), ("a", 1), "neglam"], w=[("of", pb_)])
                        A(lambda: nc.scalar.activation(out=sq2[pb_][:], in_=of2[pb_][:], func=AF.Square),
                          r=[("of", pb_)], w=[("sq", pb_)])

                    def diff_final_b(pb_, H, cs):
                        mb = rot()
                        MM([mm(pb(mb), ones[:, :], sq2[pb_][:], True, True)], r=[("sq", pb_), "ones"], w=[pk(mb)])
                        A(lambda: nc.scalar.activation(out=sd[:], in_=pb(mb), func=AF.Ln, scale=1.0 / 128.0,
                                                       bias=eps_rms[:]), r=["eps_rms"], w=[pk(mb), "sd"])
                        A(lambda: nc.scalar.activation(out=sd[:], in_=sd[:], func=AF.Exp, scale=-0.5), w=["sd"])
                        V(lambda: nc.vector.scalar_tensor_tensor(out=YB[:, H, cs], in0=of2[pb_][:],
                                                                 scalar=gcol[:, 0:1], in1=sd[:], op0=ALU.mult,
                                                                 op1=ALU.mult),
                          r=[("of", pb_), "sd", "gcol"], w=[("YB", H)])

                    fin_i = [0]
                    fin_prev = [None]
                    for H in range(4):
                        for Q in range(NQ):
                            cs = slice(Q * 512, (Q + 1) * 512)
                            for c in range(2):
                                r0 = c * 64
                                oacc, dacc = (4, 5) if c == 0 else (6, 7)
                                blocks = [(kb, 0, 512, None, 0) for kb in range(4 * Q)]
                                blocks += [(4 * Q + o, 128 * o, 512, "causal", 128 * o) for o in range(4)]
                                attn_pipe(blocks,
                                          lambda kb, c=c, H=H: (KD if c == 0 else KD1)[:, H, kb * 128:(kb + 1) * 128],
                                          lambda c0, c1, H=H, Q=Q: QD[:, H, Q * 512 + c0:Q * 512 + c1],
                                          lambda kb, H=H: VD[:, kb, H * 128:(H + 1) * 128],
                                          [("KD", H), ("QD", H), "VD", "ident", "tri"], oacc, PT, pti, 128, dacc)
                                pipe.post(lambda c=c, oacc=oacc, dacc=dacc: diff_post(c, oacc, dacc))
                            pb_ = fin_i[0] % 2
                            fin_i[0] += 1
                            if fin_prev[0] is not None:
                                pipe.post(lambda a=fin_prev[0]: diff_final_b(*a))
                            pipe.post(lambda pb_=pb_: diff_final_a(pb_))
                            fin_prev[0] = (pb_, H, cs)
                            if H == 3 and Q == 2:
                                R6a = Region([(154, 178)])
                                wbufs6 = [sbuf(R6a, [128, 8, 128], BF16, f"w6_{i}") for i in range(4)]
                                WA = sbuf(R6a, [128, 4, 1024], BF16, "wa")
                                WB = sbuf(R6a, [128, 4, 1024], BF16, "wb")

                                def ld6a(m_):
                                    j0, j1 = (2 * m_) % 4, (2 * m_ + 1) % 4
                                    S.dma("pool", wbufs6[j0][:], D["wgm"][:, :, m_ * 128:(m_ + 1) * 128],
                                          writes=[("w6", j0)], sem=f"w6_{j0}")
                                    S.dma("pool", wbufs6[j1][:], D["wgm"][:, :, (8 + m_) * 128:(9 + m_) * 128],
                                          writes=[("w6", j1)], sem=f"w6_{j1}")
                                    S.dma("pool", WA[:, :, m_ * 128:(m_ + 1) * 128],
                                          D["wa"][:, :, m_ * 128:(m_ + 1) * 128], writes=[("wa", m_)], sem=f"wa{m_}")
                                    S.dma("pool", WB[:, :, m_ * 128:(m_ + 1) * 128],
                                          D["wb"][:, :, m_ * 128:(m_ + 1) * 128], writes=[("wb", m_)], sem=f"wb{m_}")

                                ld6a(0)
                    pipe.drain()
                    diff_final_b(*fin_prev[0])
                    S.barrier()
            if s == 0:
                dump("d_ya", YA[:].rearrange("p a b -> p (a b)"), ("YA", 0), [128, 4 * SEQ])
                dump("d_yb", YB[:].rearrange("p a b -> p (a b)"), ("YB", 0), [128, 4 * SEQ])
                S.barrier()

            set_rot([0, 1, 2, 3, 4, 5, 6, 7])
            mt_stack = Region([(68, 100)])
            MT = sbuf(mt_stack, [128, 8, SEQ], BF16, "MT")
            with Region([(100, 112)]) as st:
                wbufs = wbufs6
                sg0 = [sbuf(st, [128, 512], BF16, f"sg0{i}") for i in range(2)]
                sg1 = [sbuf(st, [128, 512], BF16, f"sg1{i}") for i in range(2)]
                t0_ = [sbuf(st, [128, 512], F32, f"t0{i}") for i in range(2)]
                t1_ = [sbuf(st, [128, 512], F32, f"t1{i}") for i in range(2)]
                ci = [0]
                for m in range(8):
                    i0, i1 = (2 * m) % 4, (2 * m + 1) % 4
                    if m + 1 < 8:
                        ld6a(m + 1)
                    if m == 4:
                        WO = sbuf(Region([(178, 194)]), [128, 8, 1024], BF16, "wo")
                        for hf in range(2):
                            S.dma("pool", WO[:, :, hf * 512:(hf + 1) * 512], D["wo"][:, :, hf * 512:(hf + 1) * 512],
                                  writes=[("wo", hf)], sem=f"wo{hf}")
                        ln1r = Region([(194, 202)])
                        lng1 = load(ln1r, "lng", [128, DM], F32, D["ln1_g"])
                        lnb1 = load(ln1r, "lnb", [128, DM], F32, D["ln1_b"])
                    for tg in range(NQ):
                        cs = slice(tg * 512, (tg + 1) * 512)
                        i = ci[0] % 2
                        ci[0] += 1
                        b0 = proj_bank(wbufs[i0], ("w6", i0), tg)
                        A(lambda: nc.scalar.activation(out=sg0[i][:], in_=pb(b0), func=AF.Sigmoid),
                          w=[pk(b0), ("sg0", i)])
                        b1_ = proj_bank(wbufs[i1], ("w6", i1), tg)
                        A(lambda: nc.scalar.activation(out=sg1[i][:], in_=pb(b1_), func=AF.Sigmoid),
                          w=[pk(b1_), ("sg1", i)])
                        ba = rot()
                        MM([mm(pb(ba), WA[:, kc, m * 128:(m + 1) * 128], YA[:, kc, cs], kc == 0, kc == 3)
                            for kc in range(4)], r=[("wa", m)] + [("YA", kc) for kc in range(4)], w=[pk(ba)])
                        V(lambda: nc.vector.tensor_tensor(out=t0_[i][:], in0=pb(ba), in1=sg0[i][:], op=ALU.mult),
                          r=[("sg0", i)], w=[pk(ba), ("t0", i)])
                        bb = rot()
                        MM([mm(pb(bb), WB[:, kc, m * 128:(m + 1) * 128], YB[:, kc, cs], kc == 0, kc == 3)
                            for kc in range(4)], r=[("wb", m)] + [("YB", kc) for kc in range(4)], w=[pk(bb)])
                        V(lambda: nc.vector.tensor_tensor(out=t1_[i][:], in0=pb(bb), in1=sg1[i][:], op=ALU.mult),
                          r=[("sg1", i)], w=[pk(bb), ("t1", i)])
                        V(lambda: nc.vector.tensor_tensor(out=MT[:, m, cs], in0=t0_[i][:], in1=t1_[i][:], op=ALU.add),
                          r=[("t0", i), ("t1", i)], w=[("MT", m)])
                S.barrier()
            ya_stack.close()
            seq_stack.close()
            if s == 0:
                dump("d_mt", MT[:].rearrange("p a b -> p (a b)"), ("MT", 0), [128, 8 * SEQ])
                S.barrier()

            ffn_stack = Region([(100, 164)])
            z2acc = sbuf(ffn_stack, [128, NTT, DM], F32, "z2acc")
            h1T = sbuf(Region([(4, 36)]), [128, 8, SEQ], BF16, "h1T")

            def layer_norm(zt, zkey, gam, bet, outt, okey, mv, rstd, st6):
                V(lambda: nc.vector.bn_stats(out=st6[:, 0, :], in_=zt[:, 0:512]), r=[zkey], w=["st6"])
                V(lambda: nc.vector.bn_stats(out=st6[:, 1, :], in_=zt[:, 512:1024]), r=[zkey], w=["st6"])
                V(lambda: nc.vector.bn_aggr(out=mv[:], in_=st6[:]), r=["st6"], w=["mv"])
                A(lambda: nc.scalar.activation(out=rstd[:], in_=mv[:, 1:2], func=AF.Sqrt, bias=eps_ln[:]),
                  r=["mv", "eps_ln"], w=["rstd"])
                V(lambda: nc.vector.reciprocal(out=rstd[:], in_=rstd[:]), w=["rstd"])
                V(lambda: nc.vector.tensor_scalar(out=outt[:], in0=zt[:], scalar1=mv[:, 0:1], scalar2=rstd[:, 0:1],
                                                  op0=ALU.subtract, op1=ALU.mult), r=[zkey, "mv", "rstd"], w=[okey])
                V(lambda: nc.vector.tensor_tensor(out=outt[:], in0=outt[:], in1=gam[:], op=ALU.mult),
                  r=["lng"], w=[okey])
                V(lambda: nc.vector.tensor_tensor(out=outt[:], in0=outt[:], in1=bet[:], op=ALU.add),
                  r=["lnb"], w=[okey])

            wgub = [sbuf(Region([(164 + 4 * i, 168 + 4 * i)]), [128, 8, 256], BF16, f"wgub{i}") for i in range(3)]
            with Region([(36, 68), (176, 178)]) as st:
                lng, lnb = lng1, lnb1
                for j_ in range(3):
                    S.dma("pool", wgub[j_][:], D["wgu"][:, :, j_ * 256:(j_ + 1) * 256], writes=[("wgub", j_)],
                          sem=f"wgub{j_}")
                xt = [sbuf(st, [128, DM], F32, f"xt{i}") for i in range(2)]
                zt = [sbuf(st, [128, DM], F32, f"zt{i}") for i in range(2)]
                ht = [sbuf(st, [128, DM], F32, f"ht{i}") for i in range(2)]
                hbf = [sbuf(st, [128, DM], BF16, f"hbf{i}") for i in range(2)]
                st6 = sbuf(st, [128, 2, 6], F32, "st6")
                mv = sbuf(st, [128, 2], F32, "mv")
                rstd = sbuf(st, [128, 1], F32, "rstd")
                pairs = [(0, 1), (2, 3), (4, 5)]

                def tail6b(t_):
                    i_ = t_ % 2
                    A(lambda: nc.scalar.mul(out=z2acc[:, t_, :], in_=ht[i_][:], mul=ALPHA), r=[("ht", i_)],
                      w=[("z2", t_)])
                    A(lambda: nc.scalar.copy(out=hbf[i_][:], in_=ht[i_][:]), r=[("ht", i_)], w=[("hbf", i_)])
                    bk_ = 6 + (t_ % 2)
                    MM([(lambda k=k: nc.tensor.transpose(out=pbf(bk_)[:, k * 128:(k + 1) * 128],
                                                          in_=hbf[i_][:, k * 128:(k + 1) * 128], identity=ident[:]))
                        for k in range(8)], r=[("hbf", i_), "ident"], w=[pk(bk_)])
                    A(lambda: nc.scalar.copy(out=h1T[:, :, t_ * 128:(t_ + 1) * 128],
                                             in_=pbf(bk_).rearrange("p (k t) -> p k t", k=8)),
                      w=[pk(bk_), ("h1T", t_)])
                for tt in range(NTT):
                    i = tt % 2
                    S.dma("sp", xt[i][:], x_s[tt * 128:(tt + 1) * 128, :], writes=[("xt", i)], sem=f"xt{i}")
                    b0, b1_ = pairs[tt % 3]
                    MM([mm(pb(b0), MT[:, kc, tt * 128:(tt + 1) * 128], WO[:, kc, 0:512], kc == 0, kc == 7)
                        for kc in range(8)], r=[("wo", 0)] + [("MT", m) for m in range(8)], w=[pk(b0)])
                    MM([mm(pb(b1_), MT[:, kc, tt * 128:(tt + 1) * 128], WO[:, kc, 512:1024], kc == 0, kc == 7)
                        for kc in range(8)], r=[("wo", 1)] + [("MT", m) for m in range(8)], w=[pk(b1_)])
                    V(lambda: nc.vector.scalar_tensor_tensor(out=zt[i][:].rearrange("p (a b) -> p a b", a=2),
                                                             in0=xt[i][:].rearrange("p (a b) -> p a b", a=2),
                                                             scalar=ALPHA, in1=ps[:, b0:b0 + 2, :],
                                                             op0=ALU.mult, op1=ALU.add),
                      r=[("xt", i)], w=[pk(b0), pk(b1_), ("zt", i)])
                    if tt > 0:
                        tail6b(tt - 1)
                    layer_norm(zt[i], ("zt", i), lng, lnb, ht[i], ("ht", i), mv, rstd, st6)
                tail6b(NTT - 1)
                S.barrier()
            mt_stack.close()
            if s == 0:
                dump("d_z2", z2acc[:].rearrange("p a b -> p (a b)"), ("z2", 0), [128, NTT * DM], is_bf=False)
                S.barrier()

            set_rot([0, 1, 2, 3, 4, 5])
            with Region([(36, 100), (176, 203)]) as st:
                lng = load(st, "lng2", [128, DM], F32, D["ln2_g"], key="lng")
                lnb = load(st, "lnb2", [128, DM], F32, D["ln2_b"], key="lnb")
                AT = sbuf(st, [128, 11, SEQ], BF16, "AT")
                WDN = sbuf(st, [128, 11, 1024], BF16, "WDN")
                sgt = [sbuf(st, [128, 512], BF16, f"sgt{i}") for i in range(2)]
                st6 = sbuf(st, [128, 2, 6], F32, "st6")
                mv = sbuf(st, [128, 2], F32, "mv")
                rstd = sbuf(st, [128, 1], F32, "rstd")
                st6b = [sbuf(st, [128, 2, 6], F32, f"st6b{i}") for i in range(2)]
                mvb = [sbuf(st, [128, 4], F32, f"mvb{i}") for i in range(2)]
                ln2_pending = []
                gi = [0]
                si = [0]
                for half in range(2):
                    for jj in range(11):
                        j = half * 11 + jj
                        wi = gi[0] % 3
                        gi[0] += 1
                        if j >= 3:
                            S.dma("pool", wgub[wi][:], D["wgu"][:, :, j * 256:(j + 1) * 256], writes=[("wgub", wi)],
                                  sem=f"wgub{wi}")
                        if jj == 2:
                            for j2 in range(11):
                                S.dma("pool", WDN[:, j2, :], D["wdn"][:, half * 11 + j2, :],
                                      reads=[], writes=[("WDN", j2)], sem=f"wdn{j2}")
                        for tg in range(NQ):
                            cs = slice(tg * 512, (tg + 1) * 512)
                            hk = [("h1T", tg * 4 + q) for q in range(4)]
                            bg = rot()
                            MM([mm(pb(bg), wgub[wi][:, k, 0:128], h1T[:, k, cs], k == 0, k == 7) for k in range(8)],
                               r=[("wgub", wi)] + hk, w=[pk(bg)])
                            bu = rot()
                            MM([mm(pb(bu), wgub[wi][:, k, 128:256], h1T[:, k, cs], k == 0, k == 7) for k in range(8)],
                               r=[("wgub", wi)] + hk, w=[pk(bu)])
                            i = si[0] % 2
                            si[0] += 1
                            A(lambda: nc.scalar.activation(out=sgt[i][:], in_=pb(bg), func=AF.Silu),
                              w=[pk(bg), ("sgt", i)])
                            V(lambda: nc.vector.tensor_tensor(out=AT[:, jj, cs], in0=pb(bu), in1=sgt[i][:],
                                                              op=ALU.mult), r=[("sgt", i)], w=[pk(bu), ("AT", jj, tg)])
                    for tt in range(NTT):
                        b0 = [0, 2, 4, 6][tt % 4]
                        ak = [("AT", jj, tt // 4) for jj in range(11)]
                        wk = [("WDN", jj) for jj in range(11)]
                        MM([mm(pb(b0), AT[:, jj, tt * 128:(tt + 1) * 128], WDN[:, jj, 0:512], jj == 0, jj == 10)
                            for jj in range(11)], r=ak + wk, w=[pk(b0)])
                        MM([mm(pb(b0 + 1), AT[:, jj, tt * 128:(tt + 1) * 128], WDN[:, jj, 512:1024], jj == 0, jj == 10)
                            for jj in range(11)], r=ak + wk, w=[pk(b0 + 1)])
                        V(lambda: nc.vector.tensor_tensor(out=z2acc[:, tt, :].rearrange("p (a b) -> p a b", a=2),
                                                          in0=ps[:, b0:b0 + 2, :],
                                                          in1=z2acc[:, tt, :].rearrange("p (a b) -> p a b", a=2),
                                                          op=ALU.add), w=[pk(b0), pk(b0 + 1), ("z2", tt)])
                        if half == 1:
                            i = tt % 2
                            zt_ = z2acc[:, tt, :]
                            zk = ("z2", tt)
                            V(lambda: nc.vector.bn_stats(out=st6b[i][:, 0, :], in_=zt_[:, 0:512]), r=[zk],
                              w=[("st6b", i)])
                            V(lambda: nc.vector.bn_stats(out=st6b[i][:, 1, :], in_=zt_[:, 512:1024]), r=[zk],
                              w=[("st6b", i)])
                            V(lambda: nc.vector.bn_aggr(out=mvb[i][:, 0:2], in_=st6b[i][:]), r=[("st6b", i)],
                              w=[("mvb", i)])
                            A(lambda: nc.scalar.activation(out=mvb[i][:, 2:3], in_=mvb[i][:, 1:2], func=AF.Sqrt,
                                                           bias=eps_ln[:]), r=["eps_ln"], w=[("mvb", i)])
                            V(lambda: nc.vector.reciprocal(out=mvb[i][:, 2:3], in_=mvb[i][:, 2:3]), w=[("mvb", i)])
                            V(lambda: nc.vector.tensor_scalar(out=mvb[i][:, 3:4], in0=mvb[i][:, 0:1], scalar1=-1.0,
                                                              scalar2=mvb[i][:, 2:3], op0=ALU.mult, op1=ALU.mult),
                              w=[("mvb", i)])
                            A(lambda: nc.scalar.activation(out=zt_, in_=zt_, func=AF.Identity, scale=mvb[i][:, 2:3],
                                                           bias=mvb[i][:, 3:4]), r=[("mvb", i)], w=[zk])
                            P(lambda: nc.gpsimd.tensor_tensor(out=zt_, in0=zt_, in1=lng[:], op=ALU.mult),
                              r=["lng"], w=[zk])

                            def ln2_tail(t_=tt):
                                V(lambda: nc.vector.tensor_tensor(out=z2acc[:, t_, :], in0=z2acc[:, t_, :], in1=lnb[:],
                                                                  op=ALU.add), r=["lnb"], w=[("z2", t_)])
                                S.dma("sp", out_s[t_ * 128:(t_ + 1) * 128, :], z2acc[:, t_, :], reads=[("z2", t_)],
                                      writes=[("outd", t_ % 2)], sem=f"out{t_ % 2}")
                            if ln2_pending:
                                ln2_pending.pop(0)()
                            ln2_pending.append(ln2_tail)
                while ln2_pending:
                    ln2_pending.pop(0)()
                S.barrier()
            ffn_stack.close()
        S.barrier(engines=("sp",))
        print("kernel build: n_ins", S.n_ins, "n_wait", S.n_wait, "sems", len(S.sems))
    return nc


_NC_CACHE = {}


def kernel(**inputs):
    shared = prep_shared(inputs)
    x = np.asarray(inputs["x"], np.float32)
    if "nc" not in _NC_CACHE:
        _NC_CACHE["nc"] = build_nc()
    nc = _NC_CACHE["nc"]
    in_maps = []
    for c in range(NCORES):
        m = dict(shared)
        m["x"] = np.ascontiguousarray(x[NSEQ * c:NSEQ * (c + 1)].reshape(NSEQ * SEQ, DM))
        m["xT"] = make_xT(x[NSEQ * c:NSEQ * (c + 1)])
        in_maps.append(m)
    res = run_bass_kernel_spmd(nc, in_maps, core_ids=list(range(NCORES)))
    out = np.concatenate([np.asarray(r["out"], np.float32).reshape(NSEQ, SEQ, DM) for r in res.results], axis=0)
    return out
```

```python
import math
from contextlib import ExitStack

import numpy as np
import concourse.bass as bass
import concourse.mybir as mybir
from concourse.bass_utils import run_bass_kernel_spmd

F32 = mybir.dt.float32
BF16 = mybir.dt.bfloat16
AF = mybir.ActivationFunctionType
ALU = mybir.AluOpType
AX = mybir.AxisListType

SEQ = 2048
DM = 1024
NTT = 16
NQ = 4
NSEQ = 2
NCORES = 8
BIG = -30000.0
ALPHA = 2.0 ** 0.25
LAMBDA_INIT = 0.8 - 0.6 * math.exp(0.0)
LN_EPS = 1e-5
RMS_EPS = 1e-5
FFN_H = 2816
NHC = 22
EPOCH = 12000


class Sched:
    def __init__(self, nc, es):
        self.nc = nc
        self.es = es
        self.eng = {"pe": nc.tensor, "act": nc.scalar, "dve": nc.vector,
                    "pool": nc.gpsimd, "sp": nc.sync}
        self.sems = {}
        self.cnt = {}
        self.epoch = {e: 0 for e in self.eng}
        self.waited = {}
        self.last_w = {}
        self.readers = {}
        self.n_wait = 0
        self.n_ins = 0

    def _sem(self, key):
        if key not in self.sems:
            self.sems[key] = self.es.enter_context(
                self.nc.semaphore("s_" + "_".join(str(k) for k in key)))
            self.cnt[key] = 0
        return self.sems[key]

    def _engkey(self, e):
        key = (e, self.epoch[e])
        self._sem(key)
        if self.cnt[key] >= EPOCH:
            self.epoch[e] += 1
            key = (e, self.epoch[e])
            self._sem(key)
        return key

    def _deps(self, reads, writes):
        deps = {}

        def add(tok):
            if tok is None:
                return
            k, v = tok
            if deps.get(k, 0) < v:
                deps[k] = v
        for k in reads:
            add(self.last_w.get(k))
        for k in writes:
            add(self.last_w.get(k))
            for t in self.readers.get(k, ()):
                add(t)
        return deps

    def _emit_waits(self, e, deps, skip_self=False):
        for k, v in deps.items():
            if skip_self and k[0] == e:
                continue
            if self.waited.get((e, k), 0) >= v:
                continue
            self.eng[e].wait_ge(self.sems[k], v)
            self.waited[(e, k)] = v
            self.n_wait += 1

    def _record(self, tok, reads, writes):
        for k in writes:
            self.last_w[k] = tok
            self.readers[k] = []
        for k in reads:
            self.readers.setdefault(k, []).append(tok)

    def op(self, e, fn, reads=(), writes=()):
        deps = self._deps(reads, writes)
        self._emit_waits(e, deps, skip_self=(e == "pe"))
        key = self._engkey(e)
        ins = fn()
        self.cnt[key] += 1
        ins.then_inc(self.sems[key], 1)
        self.n_ins += 1
        self._record((key, self.cnt[key]), reads, writes)

    def group(self, e, fns, reads=(), writes=()):
        deps = self._deps(reads, writes)
        self._emit_waits(e, deps, skip_self=(e == "pe"))
        key = self._engkey(e)
        ins = None
        for fn in fns:
            ins = fn()
            self.n_ins += 1
        self.cnt[key] += 1
        ins.then_inc(self.sems[key], 1)
        self._record((key, self.cnt[key]), reads, writes)

    def dma(self, q, out, in_, reads=(), writes=(), sem="d0", **kw):
        deps = self._deps(reads, writes)
        self._emit_waits(q, deps)
        key = ("dma", sem)
        self._sem(key)
        if q == "pool":
            kw.setdefault("max_dma_last_dim", 2048)
        ins = self.eng[q].dma_start(out=out, in_=in_, **kw)
        self.cnt[key] += 16
        ins.then_inc(self.sems[key], 16)
        self.n_ins += 1
        self._record((key, self.cnt[key]), reads, writes)

    def barrier(self, engines=("pe", "act", "dve", "pool", "sp")):
        deps = {k: v for k, v in self.cnt.items() if v > 0}
        for e in engines:
            self._emit_waits(e, deps)
        self.last_w = {}
        self.readers = {}


class Region:
    def __init__(self, ranges):
        self.ranges = [[a * 1024, b * 1024] for a, b in ranges]

    def alloc(self, nbytes):
        for r in self.ranges:
            off = (r[0] + 63) // 64 * 64
            if off + nbytes <= r[1]:
                r[0] = off + nbytes
                return off
        raise RuntimeError(f"region full: need {nbytes}, ranges {self.ranges}")

    def __enter__(self):
        return self

    def __exit__(self, *a):
        return False

    def close(self):
        pass


ARENA_BYTES = 203 * 1024


OFF = {}
_o = 0
for _n, _w in (("q_nsa", 512), ("k_c", 128), ("v_c", 128), ("k_s", 128), ("v_s", 128), ("k_w", 128),
               ("v_w", 128), ("g_nsa", 24), ("q_df", 512), ("k_df", 512), ("v_df", 512), ("g_m", 2048)):
    OFF[_n] = _o
    _o += _w


def _swap_cols(base, n):
    idx = []
    for c in range(n):
        h, d = divmod(c, 64)
        if d < 8:
            d2 = d + 8
        elif d < 16:
            d2 = d - 8
        else:
            d2 = d
        idx.append(base + h * 64 + d2)
    return idx


def _kmajor(w):
    K, N = w.shape
    return np.ascontiguousarray(w.reshape(K // 128, 128, N).transpose(1, 0, 2))


def prep_shared(inp):
    w_in = np.asarray(inp["w_in"], np.float32)[0]
    out = {}
    cols = []
    cols += list(range(OFF["q_nsa"], OFF["q_nsa"] + 512))
    cols += _swap_cols(OFF["q_nsa"], 512)
    cols += list(range(OFF["k_s"], OFF["k_s"] + 128))
    cols += _swap_cols(OFF["k_s"], 128)
    cols += list(range(OFF["k_w"], OFF["k_w"] + 128))
    cols += _swap_cols(OFF["k_w"], 128)
    cols += list(range(OFF["k_c"], OFF["k_c"] + 128))
    cols += list(range(OFF["v_c"], OFF["v_c"] + 128))
    out["wn"] = _kmajor(w_in[:, cols])
    cols = list(range(OFF["v_s"], OFF["v_s"] + 128)) + list(range(OFF["v_w"], OFF["v_w"] + 128))
    out["wnv"] = _kmajor(w_in[:, cols])
    out["wg"] = _kmajor(w_in[:, OFF["g_nsa"]:OFF["g_nsa"] + 24])
    cols = []
    cols += list(range(OFF["q_df"], OFF["q_df"] + 512))
    cols += _swap_cols(OFF["q_df"], 512)
    cols += list(range(OFF["k_df"], OFF["k_df"] + 512))
    cols += _swap_cols(OFF["k_df"], 512)
    out["wd"] = _kmajor(w_in[:, cols])
    out["wdv"] = _kmajor(w_in[:, OFF["v_df"]:OFF["v_df"] + 512])
    out["wgm"] = _kmajor(w_in[:, OFF["g_m"]:OFF["g_m"] + 2048])
    out["wa"] = _kmajor(np.asarray(inp["w_branch_a"], np.float32)[0])
    out["wb"] = _kmajor(np.asarray(inp["w_branch_b"], np.float32)[0])
    out["wo"] = _kmajor(np.asarray(inp["w_o"], np.float32)[0])
    wgu = np.asarray(inp["w_gate_up"], np.float32)[0]
    g = wgu[:, :FFN_H].reshape(DM, NHC, 128)
    u = wgu[:, FFN_H:].reshape(DM, NHC, 128)
    gu = np.concatenate([g, u], axis=2).reshape(DM, NHC * 256)
    out["wgu"] = _kmajor(gu)
    out["wdn"] = _kmajor(np.asarray(inp["w_down"], np.float32)[0])
    w1 = np.asarray(inp["cmp_w1"], np.float32)[0]
    w1r = w1.reshape(2, 32, 64, 256).transpose(0, 2, 1, 3)
    out["w1"] = np.ascontiguousarray(np.concatenate([w1r, w1r], axis=1))
    w2 = np.asarray(inp["cmp_w2"], np.float32)[0]
    w2k = np.concatenate([w2[0], w2[0]], axis=1)
    out["w2k"] = _kmajor(w2k)
    out["w2v"] = _kmajor(w2[1])
    b1 = np.asarray(inp["cmp_b1"], np.float32)[0]
    out["b1"] = np.ascontiguousarray(b1.reshape(2, 2, 128).transpose(2, 0, 1).reshape(128, 4))
    pe = np.asarray(inp["cmp_pe"], np.float32)[0]
    pet = pe.transpose(2, 0, 1)
    out["pet"] = np.ascontiguousarray(np.concatenate([pet, pet], axis=0).reshape(128, 64))
    dl = np.asarray(inp["diff_lambda"], np.float32)[0].reshape(1, 256)
    out["dl"] = np.ascontiguousarray(np.broadcast_to(dl, (128, 256)))
    out["ng"] = np.ascontiguousarray(np.asarray(inp["diff_norm_g"], np.float32)[0].reshape(128, 1))
    for n in ("ln1_g", "ln1_b", "ln2_g", "ln2_b"):
        v = np.asarray(inp[n], np.float32)[0].reshape(1, DM)
        out[n] = np.ascontiguousarray(np.broadcast_to(v, (128, DM)))
    out.update(make_consts())
    return out


_CONSTS = None


def make_consts():
    global _CONSTS
    if _CONSTS is not None:
        return _CONSTS
    c = {}
    c["ident"] = np.eye(128, dtype=np.float32)
    c["ones"] = np.ones((128, 128), np.float32)
    half = 8
    inv = (500000.0 ** (-(np.arange(half, dtype=np.float32) * 2.0 / 16.0))).astype(np.float32)
    ang = np.arange(SEQ, dtype=np.float32)[:, None] * inv[None, :]
    cos = np.cos(ang).astype(np.float32).T
    sin = np.sin(ang).astype(np.float32).T
    rc = np.ones((128, SEQ), np.float32)
    rs = np.zeros((128, SEQ), np.float32)
    for base in (0, 64):
        rc[base:base + 8] = cos
        rc[base + 8:base + 16] = cos
        rs[base:base + 8] = -sin
        rs[base + 8:base + 16] = sin
    c["ropec"] = rc
    c["ropes"] = rs
    k = np.arange(128)[:, None]
    q = np.arange(128)[None, :]
    tri = np.zeros((128, 256), np.float32)
    tri[:, 0:128] = np.where(k <= q, 0.0, BIG)
    tri[:, 128:256] = np.where(q < k, 0.0, BIG)
    c["tri"] = tri
    cm = np.full((128, SEQ), BIG, np.float32)
    cc = np.arange(127)[:, None]
    t = np.arange(SEQ)[None, :]
    cm[:127] = np.where(cc * 16 + 31 <= t, 0.0, BIG)
    c["cmask"] = cm
    e = np.zeros((32, SEQ), np.float32)
    e[np.arange(SEQ) // 64, np.arange(SEQ)] = 1.0
    c["efull"] = e
    ov = np.zeros((128, 33), np.float32)
    cs = np.arange(127)[:, None] * 16
    ss = np.arange(32)[None, :] * 64
    ovl = np.clip(np.minimum(cs + 32, ss + 64) - np.maximum(cs, ss), 0, None).astype(np.float32) / 32.0
    ov[:127, :32] = ovl
    ov[:127, 32] = 1.0
    c["ovaug"] = ov
    valid = np.zeros((128, 16, 32), np.float32)
    addc = np.zeros((128, 16, 32), np.float32)
    for n in range(16):
        tt = n * 128 + np.arange(128)
        tb = (tt // 64)[:, None]
        j = np.arange(32)[None, :]
        v = j <= tb
        f = (j == 0) | (j == tb) | (j == tb - 1)
        valid[:, n, :] = v
        addc[:, n, :] = np.where(v, np.where(f, 1.0e4, 0.0), -1.0)
    c["selvalid"] = valid.reshape(128, 512)
    c["seladdc"] = addc.reshape(128, 512)
    sg = np.zeros((128, 24, 128), np.float32)
    for i in range(24):
        sg[i, i, :] = 1.0
    c["selg"] = sg.reshape(128, 24 * 128)
    _CONSTS = c
    return c


def make_xT(x_core):
    xc = np.asarray(x_core, np.float32).reshape(NSEQ, SEQ, DM // 128, 128)
    return np.ascontiguousarray(xc.transpose(0, 3, 2, 1))


IN_SHAPES = {
    "x": [NSEQ * SEQ, DM], "xT": [NSEQ, 128, DM // 128, SEQ],
    "wn": [128, 8, 14 * 128], "wnv": [128, 8, 256], "wg": [128, 8, 24],
    "wd": [128, 8, 16 * 128], "wdv": [128, 8, 512], "wgm": [128, 8, 2048],
    "wa": [128, 4, 1024], "wb": [128, 4, 1024], "wo": [128, 8, 1024],
    "wgu": [128, 8, NHC * 256], "wdn": [128, NHC, 1024],
    "w1": [2, 128, 32, 256], "w2k": [128, 2, 128], "w2v": [128, 2, 64], "b1": [128, 4], "pet": [128, 64],
    "dl": [128, 256], "ng": [128, 1],
    "ln1_g": [128, DM], "ln1_b": [128, DM], "ln2_g": [128, DM], "ln2_b": [128, DM],
    "ident": [128, 128], "ones": [128, 128], "ropec": [128, SEQ], "ropes": [128, SEQ], "tri": [128, 256],
    "cmask": [128, SEQ], "efull": [32, SEQ], "ovaug": [128, 33], "selvalid": [128, 512],
    "seladdc": [128, 512], "selg": [128, 24 * 128],
}


def build_nc(debug=False, nseq=NSEQ):
    nc = bass.Bass("TRN2", target_bir_lowering=False)
    D = {n: nc.dram_tensor(n, shp, F32, kind="ExternalInput").ap() for n, shp in IN_SHAPES.items()}
    out_d = nc.dram_tensor("out", [NSEQ * SEQ, DM], F32, kind="ExternalOutput").ap()
    dbg = {}
    if debug:
        for n, shp in (("d_ya", [128, 4 * SEQ]), ("d_yb", [128, 4 * SEQ]), ("d_mt", [128, 8 * SEQ]),
                       ("d_z2", [128, NTT * DM]), ("d_t0", [128, SEQ]), ("d_qu0", [128, SEQ]),
                       ("d_ke0", [128, SEQ]), ("d_kcmp", [128, 128]), ("d_vcmp", [128, 128]),
                       ("d_t0b", [128, SEQ])):
            dbg[n] = nc.dram_tensor(n, shp, F32, kind="ExternalOutput").ap()

    with ExitStack() as es:
        S = Sched(nc, es)
        uid = [0]

        arena = es.enter_context(nc.sbuf_tensor("arena", [128, ARENA_BYTES // 2], BF16))

        def sbuf(region, shape, dt, name=None):
            nel = 1
            for d_ in shape[1:]:
                nel *= d_
            esz = 4 if dt == F32 else 2
            off = region.alloc(nel * esz)
            v = arena[0:shape[0], off // 2: off // 2 + nel * esz // 2]
            if dt == F32:
                v = v.bitcast(F32)
            if len(shape) == 3:
                v = v.rearrange("p (a b) -> p a b", a=shape[1])
            elif len(shape) == 4:
                v = v.rearrange("p (a b c) -> p a b c", a=shape[1], b=shape[2])
            return v

        GREG = Region([(0, 4)])

        ps = es.enter_context(nc.psum_tensor("ps", [128, 8, 512], F32))

        def pb(i):
            return ps[:, i, :]

        def pbf(i):
            return ps[:, i, :].bitcast(BF16)

        def pk(i):
            return ("ps", i)

        rot_state = {"banks": [0, 1, 2, 3], "i": 0, "pairs": [], "pi": 0}

        def rotpair():
            p = rot_state["pairs"][rot_state["pi"] % len(rot_state["pairs"])]
            rot_state["pi"] += 1
            return p

        def rot():
            if not rot_state["banks"]:
                return rotpair()
            b = rot_state["banks"][rot_state["i"] % len(rot_state["banks"])]
            rot_state["i"] += 1
            return b

        def set_rot(banks, pairs=()):
            rot_state["banks"] = list(banks)
            rot_state["i"] = 0
            rot_state["pairs"] = list(pairs)
            rot_state["pi"] = 0

        def V(fn, r=(), w=()):
            S.op("dve", fn, r, w)

        def A(fn, r=(), w=()):
            S.op("act", fn, r, w)

        def P(fn, r=(), w=()):
            S.op("pool", fn, r, w)

        def MM(fns, r=(), w=()):
            S.group("pe", fns, r, w)

        def mm(out, lhsT, rhs, start, stop):
            return lambda: nc.tensor.matmul(out, lhsT=lhsT, rhs=rhs, start=start, stop=stop)

        dcount = [0]

        def load(stack, name, shape, dt, src, q=None, key=None):
            t = sbuf(stack, shape, dt, name)
            dcount[0] += 1
            qq = q or ("pool" if dt == BF16 else "sp")
            S.dma(qq, t[:], src, writes=[key or name], sem=f"ld_{name}")
            return t

        def dump(name, ap_sb, key, shape, is_bf=True):
            if not debug or name == "d_mt":
                return
            S.barrier()
            stg = sbuf(Region([(195, 203)]), [128, 2048], F32, "dbgstage")
            ncol = shape[1]
            for c0 in range(0, ncol, 2048):
                c1 = min(ncol, c0 + 2048)
                V(lambda: nc.vector.tensor_copy(out=stg[0:shape[0], 0:c1 - c0], in_=ap_sb[:, c0:c1]), w=["dbgstg"])
                S.dma("sp", dbg[name][:, c0:c1], stg[0:shape[0], 0:c1 - c0], reads=["dbgstg"], sem="dbg")
            S.barrier()

        ident = load(GREG, "ident", [128, 128], BF16, D["ident"])
        ones = load(GREG, "ones", [128, 128], BF16, D["ones"])
        tri = load(GREG, "tri", [128, 256], BF16, D["tri"])
        eps_ln = sbuf(GREG, [128, 1], F32, "eps_ln")
        V(lambda: nc.vector.memset(eps_ln[:], LN_EPS), w=["eps_ln"])
        tiny_c = sbuf(GREG, [128, 1], F32, "tiny_c")
        V(lambda: nc.vector.memset(tiny_c[:], 1e-18), w=["tiny_c"])
        zero_c = sbuf(GREG, [128, 1], F32, "zero_c")
        V(lambda: nc.vector.memset(zero_c[:], 0.0), w=["tiny_c"])
        eps_rms = sbuf(GREG, [128, 1], F32, "eps_rms")
        V(lambda: nc.vector.memset(eps_rms[:], RMS_EPS), w=["eps_rms"])

        neglam = sbuf(GREG, [128, 1], F32, "neglam")
        gcol = sbuf(GREG, [128, 1], F32, "gcol")
        with Region([(153, 195)]) as st:
            dl = load(st, "dl", [128, 256], F32, D["dl"])
            ng = load(st, "ng", [128, 1], F32, D["ng"])
            tmp = sbuf(st, [128, 128], F32, "dltmp")
            s12 = sbuf(st, [128, 2], F32, "s12")
            e12 = sbuf(st, [128, 2], F32, "e12")
            V(lambda: nc.vector.tensor_tensor(out=tmp[:, 0:64], in0=dl[:, 0:64], in1=dl[:, 64:128], op=ALU.mult),
              r=["dl"], w=["dltmp"])
            V(lambda: nc.vector.tensor_tensor(out=tmp[:, 64:128], in0=dl[:, 128:192], in1=dl[:, 192:256],
                                              op=ALU.mult), r=["dl"], w=["dltmp"])
            V(lambda: nc.vector.reduce_sum(out=s12[:, 0:1], in_=tmp[:, 0:64], axis=AX.X), r=["dltmp"], w=["s12"])
            V(lambda: nc.vector.reduce_sum(out=s12[:, 1:2], in_=tmp[:, 64:128], axis=AX.X), r=["dltmp"], w=["s12"])
            A(lambda: nc.scalar.activation(out=e12[:], in_=s12[:], func=AF.Exp), r=["s12"], w=["e12"])
            V(lambda: nc.vector.tensor_tensor(out=neglam[:], in0=e12[:, 1:2], in1=e12[:, 0:1], op=ALU.subtract),
              r=["e12"], w=["neglam"])
            V(lambda: nc.vector.tensor_scalar(out=neglam[:], in0=neglam[:], scalar1=-LAMBDA_INIT, scalar2=None,
                                              op0=ALU.add), r=["neglam"], w=["neglam"])
            V(lambda: nc.vector.tensor_scalar(out=gcol[:], in0=ng[:], scalar1=1.0 - LAMBDA_INIT, scalar2=None,
                                              op0=ALU.mult), r=["ng"], w=["gcol"])
            S.barrier()

        class Pipe:
            def __init__(self, la):
                self.dq = []
                self.la = la

            def npv(self):
                return sum(1 for k, _ in self.dq if k == "pv")

            def push(self, qk, pv):
                qk()
                self.dq.append(("pv", pv))
                while self.npv() > self.la:
                    while True:
                        k, fn = self.dq.pop(0)
                        fn()
                        if k == "pv":
                            break

            def post(self, fn):
                self.dq.append(("post", fn))

            def drain(self):
                while self.dq:
                    k, fn = self.dq.pop(0)
                    fn()

        pipe = Pipe(1)
        filler = [0]

        def attn_pipe(blocks, kT, qT, vA, rkeys, acc, PT, pti, nrow=128, den_acc=None, LA=1):
            items = [b if isinstance(b, list) else [b] for b in blocks]
            nb = sum(len(it) for it in items)
            flat_idx = {}
            n_ = 0
            for ii, it in enumerate(items):
                for bj in range(len(it)):
                    flat_idx[(ii, bj)] = n_
                    n_ += 1
            steps = [list(range(i, min(i + 2, len(items)))) for i in range(0, len(items), 2)]
            slots = {}

            def emit_qk(si):
                st_ = steps[si]
                bp = rotpair()
                fns = []
                for j, ii in enumerate(st_):
                    for (kb, c0, c1, mask, m0) in items[ii]:
                        fns.append(mm(pb(bp + j)[0:nrow, c0:c1], kT(kb), qT(c0, c1), True, mask is None))
                        if mask == "causal":
                            fns.append(mm(pb(bp + j)[0:nrow, m0:m0 + 128], ident[:, :], tri[:, 0:128], False, True))
                        elif mask == "anti":
                            fns.append(mm(pb(bp + j)[0:nrow, m0:m0 + 128], ident[:, :], tri[:, 128:256], False, True))
                bkeys = [pk(bp + j) for j in range(len(st_))]
                MM(fns, r=list(rkeys), w=bkeys)
                i = pti[0] % len(PT)
                pti[0] += 1
                pt = PT[i]
                rngs = [(min(b[1] for b in items[ii]), max(b[2] for b in items[ii])) for ii in st_]
                if len(st_) == 2 and rngs[0] == rngs[1]:
                    c0, c1 = rngs[0]
                    A(lambda: nc.scalar.activation(out=pt[0:nrow, 0:2, c0:c1], in_=ps[0:nrow, bp:bp + 2, c0:c1],
                                                   func=AF.Exp, scale=0.125), w=bkeys + [("PT", i)])
                else:
                    for j, (c0, c1) in enumerate(rngs):
                        A(lambda: nc.scalar.activation(out=pt[0:nrow, j, c0:c1], in_=pb(bp + j)[0:nrow, c0:c1],
                                                       func=AF.Exp, scale=0.125), w=[pk(bp + j), ("PT", i)])
                slots[si] = i

            def emit_pv(si):
                i = slots.pop(si)
                pt = PT[i]
                fns = []
                dfns = []
                for j, ii in enumerate(steps[si]):
                    for bj, (kb, c0, c1, mask, m0) in enumerate(items[ii]):
                        n0 = flat_idx[(ii, bj)]
                        fns.append(mm(pb(acc)[:, c0:c1], vA(kb), pt[0:nrow, j, c0:c1], n0 == 0, n0 == nb - 1))
                        if den_acc is not None:
                            dfns.append(mm(pb(den_acc)[:, c0:c1], ones[0:nrow, :], pt[0:nrow, j, c0:c1],
                                           n0 == 0, n0 == nb - 1))
                MM(fns, r=[("PT", i)] + list(rkeys), w=[pk(acc)])
                if dfns:
                    MM(dfns, r=[("PT", i), "ones"], w=[pk(den_acc)])

            for si in range(len(steps)):
                pipe.push(lambda si=si: emit_qk(si), lambda si=si: emit_pv(si))

        wb_i = [0]

        for s in range(nseq):
            x_s = D["x"][s * SEQ:(s + 1) * SEQ, :]
            out_s = out_d[s * SEQ:(s + 1) * SEQ, :]
            seq_stack = Region([(4, 36)])
            set_rot([0, 1, 2, 3, 4, 5, 6, 7])
            xT = sbuf(seq_stack, [128, 8, SEQ], BF16, "xT")
            xT_src = D["xT"][s]

            def xT_issue(tg_):
                S.dma("pool", xT[:, :, tg_ * 512:(tg_ + 1) * 512], xT_src[:, :, tg_ * 512:(tg_ + 1) * 512],
                      writes=[("xT", tg_ * 4 + i_) for i_ in range(4)], sem=f"xT{tg_}")

            xT_issue(0)

            def xkeys(tg):
                return [("xT", tg * 4 + i) for i in range(4)]

            def proj_bank(wt, wkey, tg):
                bk = rot()
                MM([mm(pb(bk), wt[:, k, :], xT[:, k, tg * 512:(tg + 1) * 512], k == 0, k == 7) for k in range(8)],
                   r=[wkey] + xkeys(tg), w=[pk(bk)])
                return bk

            ya_stack = Region([(36, 68)])
            YA = sbuf(ya_stack, [128, 4, SEQ], BF16, "YA")
            YB = sbuf(ya_stack, [128, 4, SEQ], BF16, "YB")

            nsa = Region([(68, 154)])
            Tq = [sbuf(nsa, [128, SEQ], BF16, f"Tq{h}") for h in range(8)]
            QU = [sbuf(nsa, [128, SEQ], BF16, f"QU{j}") for j in range(4)]
            KE = [sbuf(nsa, [128, SEQ], BF16, f"KE{g}") for g in range(2)]
            KW = [sbuf(nsa, [128, SEQ], BF16, f"KW{g}") for g in range(2)]
            VS = sbuf(nsa, [128, NTT, 2, 128], BF16, "VS")
            VW = sbuf(nsa, [128, NTT, 2, 128], BF16, "VW")
            GT = sbuf(nsa, [128, SEQ], BF16, "GT")
            kcmpL = [sbuf(nsa, [128, 128], BF16, f"kcmpL{g}") for g in range(2)]
            kcmpH = [sbuf(nsa, [128, 128], BF16, f"kcmpH{g}") for g in range(2)]
            vcmpA = [sbuf(nsa, [128, 128], BF16, f"vcmpA{g}") for g in range(2)]
            for g in range(2):
                S.dma("pool", KE[g][64:96, :], D["efull"], writes=[("KE", g)], sem=f"ke{g}")
                V(lambda: nc.vector.memset(vcmpA[g][:], 1.0), w=[("vcmpA", g)])
            for h_ in range(8):
                V(lambda: nc.vector.memset(Tq[h_][64:96, :], 0.0), w=[("TqN", h_, q_) for q_ in range(NQ)])
            V(lambda: nc.vector.memset(GT[:], 0.0), w=["GT"])
            for g in range(2):
                V(lambda: nc.vector.memset(KW[g][64:96, :], 0.0), w=[("KW", g)])
                V(lambda: nc.vector.memset(kcmpL[g][:], 0.0), w=[("kcmpT", g)])
                V(lambda: nc.vector.memset(kcmpH[g][:], 0.0), w=[("kcmpT", g)])
            V(lambda: nc.vector.memset(VS[:], 1.0), w=["VS"])
            V(lambda: nc.vector.memset(VW[:], 1.0), w=["VW"])

            set_rot([0, 1, 2, 3, 4, 5, 6, 7])
            with Region([(36, 68), (154, 195)]) as st:
                ropec = load(st, "ropec", [128, SEQ], F32, D["ropec"])
                ropes = load(st, "ropes", [128, SEQ], F32, D["ropes"])
                w1h = [sbuf(st, [128, 32, 128], BF16, f"w1h{i}") for i in range(2)]

                def w1_issue(p_):
                    kv_, hc_ = divmod(p_, 2)
                    S.dma("pool", w1h[p_ % 2][:], D["w1"][kv_][:, :, hc_ * 128:(hc_ + 1) * 128],
                          writes=[("w1h", p_ % 2)], sem=f"w1h{p_ % 2}")
                kcT = sbuf(st, [128, SEQ + 32], BF16, "kcT")
                vcT = sbuf(st, [128, SEQ + 32], BF16, "vcT")
                S.dma("pool", kcT[:, SEQ:SEQ + 32], D["pet"][:, 0:32], writes=["kcT_pe"], sem="pet0")
                S.dma("pool", vcT[:, SEQ:SEQ + 32], D["pet"][:, 32:64], writes=["vcT_pe"], sem="pet1")
                wbufs = [sbuf(st, [128, 8, 128], BF16, f"wbuf{i}") for i in range(4)]
                tmpA = [sbuf(st, [128, 512], F32, f"tmpA{i}") for i in range(2)]
                tmpB = [sbuf(st, [128, 512], F32, f"tmpB{i}") for i in range(2)]
                tcount = [0]

                def wload(src_ap):
                    i = wb_i[0] % 4
                    wb_i[0] += 1
                    S.dma("pool", wbufs[i][:], src_ap, writes=[("wbuf", i)], sem=f"wbuf{i}")
                    return wbufs[i], ("wbuf", i)

                def rope_evac(bA, bB, tg, dests, plain=None):
                    i = tcount[0] % 2
                    tcount[0] += 1
                    cs = slice(tg * 512, (tg + 1) * 512)
                    tA, tB = tmpA[i], tmpB[i]
                    if plain is not None:
                        dt_, dk = plain
                        A(lambda: nc.scalar.copy(out=dt_[:, cs], in_=pb(bA)), w=[pk(bA), dk])
                    V(lambda: nc.vector.tensor_tensor(out=tA[:], in0=pb(bA), in1=ropec[:, cs], op=ALU.mult),
                      r=["ropec"], w=[pk(bA), ("tmpA", i)])
                    V(lambda: nc.vector.tensor_tensor(out=tB[:], in0=pb(bB), in1=ropes[:, cs], op=ALU.mult),
                      r=["ropes"], w=[pk(bB), ("tmpB", i)])
                    for (r0, dtile, dkey, d0, nrow) in dests:
                        V(lambda: nc.vector.tensor_tensor(out=dtile[d0:d0 + nrow, cs], in0=tA[r0:r0 + nrow, :],
                                                          in1=tB[r0:r0 + nrow, :], op=ALU.add),
                          r=[("tmpA", i), ("tmpB", i)], w=[dkey])

                wn = D["wn"]
                for j in range(4):
                    wA, kA = wload(wn[:, :, j * 128:(j + 1) * 128])
                    wB, kB = wload(wn[:, :, (4 + j) * 128:(5 + j) * 128])
                    if j == 0:
                        for tg_ in range(1, NQ):
                            xT_issue(tg_)
                    for tg in range(NQ):
                        bA = proj_bank(wA, kA, tg)
                        bB = proj_bank(wB, kB, tg)
                        rope_evac(bA, bB, tg,
                                  [(0, Tq[2 * j], ("Tq", 2 * j, tg), 0, 64),
                                   (64, Tq[2 * j + 1], ("Tq", 2 * j + 1, tg), 0, 64)],
                                  plain=(QU[j], ("QU", j, tg)))
                    if j == 1:
                        w1_issue(0)
                        w1_issue(1)
                for (t0, dst, nm) in ((8, KE, "KE"), (10, KW, "KW")):
                    wA, kA = wload(wn[:, :, t0 * 128:(t0 + 1) * 128])
                    wB, kB = wload(wn[:, :, (t0 + 1) * 128:(t0 + 2) * 128])
                    for tg in range(NQ):
                        bA = proj_bank(wA, kA, tg)
                        bB = proj_bank(wB, kB, tg)
                        rope_evac(bA, bB, tg, [(0, dst[0], (nm, 0), 0, 64), (64, dst[1], (nm, 1), 0, 64)])
                for (t0, dst, nm) in ((12, kcT, "kcT"), (13, vcT, "vcT")):
                    wA, kA = wload(wn[:, :, t0 * 128:(t0 + 1) * 128])
                    for tg in range(NQ):
                        bA = proj_bank(wA, kA, tg)
                        if tg % 2 == 0:
                            A(lambda: nc.scalar.copy(out=dst[:, tg * 512:(tg + 1) * 512], in_=pb(bA)),
                              w=[pk(bA), nm])
                        else:
                            V(lambda: nc.vector.tensor_copy(out=dst[:, tg * 512:(tg + 1) * 512], in_=pb(bA)),
                              w=[pk(bA), nm])
                wgt = load(st, "wg", [128, 8, 24], BF16, D["wg"])
                for tg in range(NQ):
                    bk = rot()
                    MM([mm(pb(bk)[0:24, :], wgt[:, k, :], xT[:, k, tg * 512:(tg + 1) * 512], k == 0, k == 7)
                        for k in range(8)], r=["wg"] + xkeys(tg), w=[pk(bk)])
                    A(lambda: nc.scalar.activation(out=GT[0:24, tg * 512:(tg + 1) * 512], in_=pb(bk)[0:24, :],
                                                   func=AF.Sigmoid), w=[pk(bk), "GT"])
                wnv = load(st, "wnv", [128, 8, 256], BF16, D["wnv"])
                for tt in range(NTT):
                    bk = rot()
                    MM([mm(pb(bk)[:, 0:256], xT[:, k, tt * 128:(tt + 1) * 128], wnv[:, k, :], k == 0, k == 7)
                        for k in range(8)], r=["wnv", ("xT", tt)], w=[pk(bk)])
                    srcs = pb(bk)[:, 0:128].rearrange("p (g d) -> p g d", g=2)
                    srcw = pb(bk)[:, 128:256].rearrange("p (g d) -> p g d", g=2)
                    V(lambda: nc.vector.tensor_copy(out=VS[:, tt, :, 0:64], in_=srcs), w=[pk(bk), "VS"])
                    A(lambda: nc.scalar.copy(out=VW[:, tt, :, 0:64], in_=srcw), w=[pk(bk), "VW"])

                w2k = load(st, "w2k", [128, 2, 128], BF16, D["w2k"])
                w2v = load(st, "w2v", [128, 2, 64], BF16, D["w2v"])
                b1 = load(st, "b1", [128, 4], F32, D["b1"])
                hb = [sbuf(st, [128, 1], F32, f"hb{i}") for i in range(2)]
                xh = [sbuf(st, [128, 128], F32, f"xh{i}") for i in range(2)]
                x2 = [sbuf(st, [128, 128], F32, f"x2{i}") for i in range(2)]
                sgm = [sbuf(st, [128, 128], F32, f"sgm{i}") for i in range(2)]
                gT = sbuf(st, [128, 2, 2, 128], BF16, "gT")
                it = [0]
                for kv, srcT, srck in ((0, kcT, "kcT"), (1, vcT, "vcT")):
                    for hc in range(2):
                        p_ = kv * 2 + hc
                        wb_ = w1h[p_ % 2]
                        for g in range(2):
                            r0 = g * 64
                            i = it[0] % 2
                            it[0] += 1
                            bk = rot()
                            MM([mm(pb(bk)[:, 0:129], wb_[r0:r0 + 64, l, :],
                                   srcT[r0:r0 + 64, l:l + 16 * 128 + 1:16], l == 0, l == 31)
                                for l in range(32)], r=[("w1h", p_ % 2), srck, srck + "_pe"], w=[pk(bk)])
                            V(lambda: nc.vector.tensor_tensor(out=hb[i][:], in0=pb(bk)[:, 128:129],
                                                              in1=b1[:, kv * 2 + hc:kv * 2 + hc + 1], op=ALU.add),
                              r=["b1"], w=[pk(bk), ("hb", i)])
                            A(lambda: nc.scalar.activation(out=xh[i][:, 0:127], in_=pb(bk)[:, 0:127],
                                                           func=AF.Identity, bias=hb[i][:]),
                              r=[("hb", i)], w=[pk(bk), ("xh", i)])
                            V(lambda: nc.vector.tensor_tensor(out=x2[i][:, 0:127], in0=xh[i][:, 0:127],
                                                              in1=xh[i][:, 0:127], op=ALU.mult),
                              r=[("xh", i)], w=[("x2", i)])
                            V(lambda: nc.vector.tensor_scalar(out=x2[i][:, 0:127], in0=x2[i][:, 0:127],
                                                              scalar1=0.044715, scalar2=1.0, op0=ALU.mult,
                                                              op1=ALU.add), w=[("x2", i)])
                            V(lambda: nc.vector.tensor_tensor(out=x2[i][:, 0:127], in0=x2[i][:, 0:127],
                                                              in1=xh[i][:, 0:127], op=ALU.mult),
                              r=[("xh", i)], w=[("x2", i)])
                            A(lambda: nc.scalar.activation(out=sgm[i][:, 0:127], in_=x2[i][:, 0:127], func=AF.Sigmoid,
                                                           scale=2.0 * math.sqrt(2.0 / math.pi)),
                              r=[("x2", i)], w=[("sgm", i)])
                            V(lambda: nc.vector.tensor_tensor(out=gT[:, g, hc, 0:127], in0=xh[i][:, 0:127],
                                                              in1=sgm[i][:, 0:127], op=ALU.mult),
                              r=[("xh", i), ("sgm", i)], w=[("gT", g, hc)])
                        if p_ + 2 < 4:
                            w1_issue(p_ + 2)
                    for g in range(2):
                        bk = rot()
                        gk = [("gT", g, 0), ("gT", g, 1)]
                        if kv == 0:
                            MM([mm(pb(bk)[:, 0:127], w2k[:, hc, :], gT[:, g, hc, 0:127], hc == 0, hc == 1)
                                for hc in range(2)], r=["w2k"] + gk, w=[pk(bk)])
                            V(lambda: nc.vector.tensor_copy(out=kcmpL[g][0:64, 0:127], in_=pb(bk)[0:64, 0:127]),
                              w=[pk(bk), ("kcmpT", g)])
                            V(lambda: nc.vector.tensor_copy(out=kcmpH[g][64:128, 0:127], in_=pb(bk)[64:128, 0:127]),
                              w=[pk(bk), ("kcmpT", g)])
                        else:
                            MM([mm(pb(bk)[0:127, 0:64], gT[:, g, hc, 0:127], w2v[:, hc, :], hc == 0, hc == 1)
                                for hc in range(2)], r=["w2v"] + gk, w=[pk(bk)])
                            V(lambda: nc.vector.tensor_copy(out=vcmpA[g][0:127, 0:64], in_=pb(bk)[0:127, 0:64]),
                              w=[pk(bk), ("vcmpA", g)])
                if s == 0:
                    dump("d_t0", Tq[0][:], ("Tq", 0, 0), [128, SEQ])
                    dump("d_qu0", QU[0][:], ("QU", 0, 0), [128, SEQ])
                    dump("d_ke0", KE[0][:], ("KE", 0), [128, SEQ])
                    dump("d_kcmp", kcmpL[0][:], ("kcmpT", 0), [128, 128])
                    dump("d_vcmp", vcmpA[0][:], ("vcmpA", 0), [128, 128])
                S.barrier()

            set_rot([], pairs=[0, 2])
            ACC = [4, 5, 6, 7]
            ACCP = [4, 6]
            pipe.la = 2
            with Region([(52, 68), (154, 195)]) as st:
                cmask = load(st, "cmask", [128, SEQ], BF16, D["cmask"])
                ovaug = load(st, "ovaug", [128, 33], BF16, D["ovaug"])
                selvalid = load(st, "selvalid", [128, 512], F32, D["selvalid"])
                seladdc = load(st, "seladdc", [128, 512], F32, D["seladdc"])
                selg = load(st, "selg", [128, 24 * 128], BF16, D["selg"])
                PT = [sbuf(st, [128, 2, 512], BF16, f"PT{i}") for i in range(4)]
                PTc = sbuf(st, [128, 4, 512], BF16, "PTc")
                GB = sbuf(st, [64, 12, 512], BF16, "GB")
                yacc = sbuf(st, [64, 4, 512], F32, "yacc")
                rr_all = sbuf(st, [128, 4, 512], F32, "rr_all")
                rr = [rr_all[:, 2 * i:2 * i + 2, :] for i in range(2)]
                den16 = sbuf(st, [128, 16], F32, "den16")
                pn = sbuf(Region([(195, 203)]), [128, 512], F32, "pn")
                pri = sbuf(st, [128, 128], F32, "pri")
                top8 = sbuf(st, [128, 32], F32, "top8")
                nsel = sbuf(st, [128, 128], BF16, "nsel")
                pti = [0]
                acci = [0]
                nrm = [0]

                def attn(blocks, kT, qT, vA, rkeys, acc, nrow=128, den_acc=None):
                    attn_pipe(blocks, kT, qT, vA, rkeys, acc, PT, pti, nrow, den_acc)

                def combine_cmp(acc, h, Q, hh):
                    i = nrm[0] % 2
                    nrm[0] += 1
                    A(lambda: nc.scalar.activation(out=rr[i][64:128, 0, :], in_=pb(acc)[64:128, :], func=AF.Ln,
                                                   bias=tiny_c[64:128, :]), r=["tiny_c"], w=[pk(acc), ("rr", i)])
                    A(lambda: nc.scalar.activation(out=rr[i][64:128, 0, :], in_=rr[i][64:128, 0, :], func=AF.Exp,
                                                   scale=-1.0), w=[("rr", i)])
                    V(lambda: nc.vector.tensor_tensor(out=rr[i][0:64, 0, :], in0=pb(acc)[0:64, :],
                                                      in1=rr[i][64:128, 0, :], op=ALU.mult),
                      w=[pk(acc), ("rr", i)])
                    V(lambda: nc.vector.tensor_tensor(out=yacc[:, hh, :], in0=rr[i][0:64, 0, :], in1=GB[:, hh * 3, :],
                                                      op=ALU.mult), r=[("rr", i), ("GB", hh)], w=[("yacc", hh)])

                def combine2(accp, h, Q, hh):
                    i = nrm[0] % 2
                    nrm[0] += 1
                    cs = slice(Q * 512, (Q + 1) * 512)
                    A(lambda: nc.scalar.activation(out=rr[i][64:128, :, :], in_=ps[64:128, accp:accp + 2, :],
                                                   func=AF.Ln, bias=zero_c[64:128, :]),
                      r=["tiny_c"], w=[pk(accp), pk(accp + 1), ("rr", i)])
                    A(lambda: nc.scalar.activation(out=rr[i][64:128, :, :], in_=rr[i][64:128, :, :], func=AF.Exp,
                                                   scale=-1.0), w=[("rr", i)])
                    V(lambda: nc.vector.tensor_tensor(out=rr[i][0:64, :, :], in0=ps[0:64, accp:accp + 2, :],
                                                      in1=rr[i][64:128, :, :], op=ALU.mult),
                      w=[pk(accp), pk(accp + 1), ("rr", i)])
                    V(lambda: nc.vector.tensor_tensor(out=rr[i][0:64, :, :], in0=rr[i][0:64, :, :],
                                                      in1=GB[:, hh * 3 + 1:hh * 3 + 3, :], op=ALU.mult),
                      r=[("GB", hh)], w=[("rr", i)])
                    V(lambda: nc.vector.tensor_tensor(out=rr[i][0:64, 0, :], in0=rr[i][0:64, 0, :],
                                                      in1=rr[i][0:64, 1, :], op=ALU.add), w=[("rr", i)])
                    r0 = (h % 2) * 64
                    V(lambda: nc.vector.tensor_tensor(out=YA[r0:r0 + 64, h // 2, cs], in0=yacc[:, hh, :],
                                                      in1=rr[i][0:64, 0, :], op=ALU.add),
                      r=[("rr", i), ("yacc", hh)], w=[("YA", h // 2)])

                for g in range(2):
                    for Q in range(NQ):
                        cs = slice(Q * 512, (Q + 1) * 512)
                        for c12 in range(12):
                            gbk = c12 % 4
                            c = g * 12 + c12
                            MM([mm(pb(gbk)[:, :], selg[:, c * 128:(c + 1) * 128], GT[:, cs], True, True)],
                               r=["selg", "GT"], w=[pk(gbk)])
                            A(lambda: nc.scalar.copy(out=GB[:, c12, :], in_=pb(gbk)[0:64, :]),
                              w=[pk(gbk), ("GB", c12 // 3)])
                        bps = [rotpair(), rotpair()]
                        for hh in range(4):
                            h = g * 4 + hh
                            j, r0 = h // 2, (h % 2) * 64
                            sbk = bps[hh // 2] + hh % 2
                            kc_ = kcmpL[g] if r0 == 0 else kcmpH[g]
                            MM([mm(pb(sbk)[0:127, :], kc_[:, 0:127], QU[j][:, cs], True, False),
                                mm(pb(sbk)[0:127, :], ident[0:127, 0:127], cmask[0:127, cs], False, True)],
                               r=[("kcmpT", g), ("QU", j, Q), "cmask", "ident"], w=[pk(sbk)])
                        for pp in range(2):
                            bp = bps[pp]
                            A(lambda: nc.scalar.activation(out=PTc[0:127, 2 * pp:2 * pp + 2, :],
                                                           in_=ps[0:127, bp:bp + 2, :], func=AF.Exp, scale=0.125),
                              w=[pk(bp), pk(bp + 1), ("PTc", pp)])
                        for hh in range(4):
                            MM([mm(pb(4 + hh)[:, :], vcmpA[g][0:127, :], PTc[0:127, hh, :], True, True)],
                               r=[("PTc", hh // 2), ("vcmpA", g)], w=[pk(4 + hh)])
                        rkeys4 = [("rr", 0), ("rr", 1)]
                        akeys4 = [pk(4 + hh) for hh in range(4)]
                        A(lambda: nc.scalar.activation(out=rr_all[64:128, :, :], in_=ps[64:128, 4:8, :], func=AF.Ln,
                                                       bias=tiny_c[64:128, :]), r=["tiny_c"], w=akeys4 + rkeys4)
                        A(lambda: nc.scalar.activation(out=rr_all[64:128, :, :], in_=rr_all[64:128, :, :],
                                                       func=AF.Exp, scale=-1.0), w=rkeys4)
                        rp = rotpair()
                        fns = []
                        for n in range(4):
                            for hh in range(4):
                                c16 = n * 4 + hh
                                fns.append(mm(pb(rp)[:, c16 * 32:(c16 + 1) * 32], PTc[0:127, hh, n * 128:(n + 1) * 128],
                                              ovaug[0:127, 0:32], True, True))
                                fns.append(mm(pb(rp + 1)[:, c16:c16 + 1], PTc[0:127, hh, n * 128:(n + 1) * 128],
                                              ovaug[0:127, 32:33], True, True))
                        MM(fns, r=[("PTc", 0), ("PTc", 1), "ovaug"], w=[pk(rp), pk(rp + 1)])
                        V(lambda: nc.vector.tensor_scalar(out=den16[:], in0=pb(rp + 1)[:, 0:16], scalar1=1e-30,
                                                          scalar2=None, op0=ALU.max), w=[pk(rp + 1), "den16"])
                        V(lambda: nc.vector.reciprocal(out=den16[:], in_=den16[:]), w=["den16"])
                        V(lambda: nc.vector.tensor_tensor(
                            out=pn[:].rearrange("p (c j) -> p c j", j=32),
                            in0=pb(rp)[:, :].rearrange("p (c j) -> p c j", j=32),
                            in1=den16[:].unsqueeze(2).broadcast_to([128, 16, 32]), op=ALU.mult),
                          r=["den16"], w=[pk(rp), "pn"])
                        V(lambda: nc.vector.tensor_reduce(out=pri[:].rearrange("p (n j) -> p n j", n=4),
                                                          in_=pn[:].rearrange("p (n h j) -> p n j h", n=4, h=4),
                                                          axis=AX.X, op=ALU.add), r=["pn"], w=["pri"])
                        V(lambda: nc.vector.tensor_tensor(out=pri[:], in0=pri[:],
                                                          in1=selvalid[:, Q * 128:(Q + 1) * 128], op=ALU.mult),
                          r=["selvalid"], w=["pri"])
                        V(lambda: nc.vector.tensor_tensor(out=pri[:], in0=pri[:],
                                                          in1=seladdc[:, Q * 128:(Q + 1) * 128], op=ALU.add),
                          r=["seladdc"], w=["pri"])
                        for n in range(4):
                            V(lambda: nc.vector.max(out=top8[:, n * 8:(n + 1) * 8], in_=pri[:, n * 32:(n + 1) * 32]),
                              r=["pri"], w=["top8"])
                        V(lambda: nc.vector.tensor_tensor(
                            out=pn[:, 0:128].rearrange("p (n j) -> p n j", n=4),
                            in0=pri[:].rearrange("p (n j) -> p n j", n=4),
                            in1=top8[:].rearrange("p (n e) -> p n e", n=4)[:, :, 7:8].broadcast_to([128, 4, 32]),
                            op=ALU.is_lt), r=["pri", "top8"], w=["pn"])
                        V(lambda: nc.vector.tensor_scalar(out=nsel[:], in0=pn[:, 0:128], scalar1=BIG, scalar2=None,
                                                          op0=ALU.mult), r=["pn"], w=["nsel"])
                        V(lambda: nc.vector.tensor_tensor(out=rr_all[0:64, :, :], in0=ps[0:64, 4:8, :],
                                                          in1=rr_all[64:128, :, :], op=ALU.mult), w=akeys4 + rkeys4)
                        V(lambda: nc.vector.tensor_tensor(out=yacc[:, :, :], in0=rr_all[0:64, :, :],
                                                          in1=GB[:, 0:12:3, :], op=ALU.mult),
                          r=rkeys4 + [("GB", hh) for hh in range(4)], w=[("yacc", hh) for hh in range(4)])

                        def sel_finish(g=g, Q=Q, cs=cs):
                            tb = rot()
                            MM([(lambda n=n: nc.tensor.transpose(out=pbf(tb)[0:32, n * 128:(n + 1) * 128],
                                                                 in_=nsel[:, n * 32:(n + 1) * 32], identity=ident[:]))
                                for n in range(4)], r=["nsel", "ident"], w=[pk(tb)])
                            for hh_ in range(4):
                                h_ = g * 4 + hh_
                                dst = Tq[h_][64:96, cs]
                                if hh_ % 2 == 0:
                                    V(lambda: nc.vector.tensor_copy(out=dst, in_=pbf(tb)[0:32, 0:512]),
                                      w=[pk(tb), ("TqN", h_, Q)])
                                else:
                                    A(lambda: nc.scalar.copy(out=dst, in_=pbf(tb)[0:32, 0:512]),
                                      w=[pk(tb), ("TqN", h_, Q)])

                        for hh in range(4):
                            h = g * 4 + hh
                            accp = ACCP[acci[0] % 2]
                            acci[0] += 1
                            diag = [(4 * Q + o, 128 * o, 512, "causal", 128 * o) for o in range(4)]
                            if Q > 0:
                                low = [(4 * Q - 4 + o, 0, 128 * (o + 1), "anti", 128 * o) for o in range(4)]
                                blocks = [diag[0], low[3], [low[0], diag[1]], [low[1], diag[2]], [low[2], diag[3]]]
                            else:
                                blocks = diag
                            attn(blocks,
                                 lambda kb, g=g: KW[g][0:96, kb * 128:(kb + 1) * 128],
                                 lambda c0, c1, h=h, Q=Q: Tq[h][0:96, Q * 512 + c0:Q * 512 + c1],
                                 lambda kb, g=g: VW[:, kb, g, :],
                                 [("KW", g), ("Tq", h, Q), "VW", "ident", "tri"], accp + 1)
                            if hh == 0:
                                sel_finish()
                            blocks = [(kb, 0, 512, None, 0) for kb in range(4 * Q)]
                            blocks += [(4 * Q + o, 128 * o, 512, "causal", 128 * o) for o in range(4)]
                            attn(blocks,
                                 lambda kb, g=g: KE[g][0:96, kb * 128:(kb + 1) * 128],
                                 lambda c0, c1, h=h, Q=Q: Tq[h][0:96, Q * 512 + c0:Q * 512 + c1],
                                 lambda kb, g=g: VS[:, kb, g, :],
                                 [("KE", g), ("Tq", h, Q), ("TqN", h, Q), "VS", "ident", "tri"], accp)
                            pipe.post(lambda accp=accp, h=h, Q=Q, hh=hh: combine2(accp, h, Q, hh))
                        pipe.drain()
                if s == 0:
                    dump("d_t0b", Tq[0][:], ("Tq", 0, 0), [128, SEQ])
                S.barrier()
            nsa.close()

            set_rot([0, 1, 2, 3, 4, 5, 6, 7])
            with Region([(68, 132)]) as dst_:
                QD = sbuf(dst_, [128, 4, SEQ], BF16, "QD")
                KD = sbuf(dst_, [128, 4, SEQ], BF16, "KD")
                KD1 = sbuf(dst_, [128, 4, SEQ], BF16, "KD1")
                V(lambda: nc.vector.memset(KD[64:128, :, :], 0.0), w=[("KD", j) for j in range(4)])
                V(lambda: nc.vector.memset(KD1[0:64, :, :], 0.0), w=[("KD", j) for j in range(4)])
                VD = sbuf(dst_, [128, NTT, 512], BF16, "VD")
                with Region([(132, 195)]) as st:
                    ropec = load(st, "ropec", [128, SEQ], F32, D["ropec"])
                    ropes = load(st, "ropes", [128, SEQ], F32, D["ropes"])
                    wbufs = [sbuf(st, [128, 8, 128], BF16, f"wbuf{i}") for i in range(4)]
                    tmpA = [sbuf(st, [128, 512], F32, f"tmpA{i}") for i in range(2)]
                    tmpB = [sbuf(st, [128, 512], F32, f"tmpB{i}") for i in range(2)]
                    wd = D["wd"]
                    tci = [0]
                    for (t0, dstt, nm) in ((0, QD, "QD"), (8, KD, "KD")):
                        for j in range(4):
                            i0 = wb_i[0] % 4
                            wb_i[0] += 1
                            i1 = wb_i[0] % 4
                            wb_i[0] += 1
                            S.dma("pool", wbufs[i0][:], wd[:, :, (t0 + j) * 128:(t0 + j + 1) * 128],
                                  writes=[("wbuf", i0)], sem=f"wbuf{i0}")
                            S.dma("pool", wbufs[i1][:], wd[:, :, (t0 + 4 + j) * 128:(t0 + 5 + j) * 128],
                                  writes=[("wbuf", i1)], sem=f"wbuf{i1}")
                            for tg in range(NQ):
                                bA = proj_bank(wbufs[i0], ("wbuf", i0), tg)
                                bB = proj_bank(wbufs[i1], ("wbuf", i1), tg)
                                i = tci[0] % 2
                                tci[0] += 1
                                cs = slice(tg * 512, (tg + 1) * 512)
                                V(lambda: nc.vector.tensor_tensor(out=tmpA[i][:], in0=pb(bA), in1=ropec[:, cs],
                                                                  op=ALU.mult), r=["ropec"], w=[pk(bA), ("tmpA", i)])
                                V(lambda: nc.vector.tensor_tensor(out=tmpB[i][:], in0=pb(bB), in1=ropes[:, cs],
                                                                  op=ALU.mult), r=["ropes"], w=[pk(bB), ("tmpB", i)])
                                if nm == "QD":
                                    V(lambda: nc.vector.tensor_tensor(out=dstt[:, j, cs], in0=tmpA[i][:],
                                                                      in1=tmpB[i][:], op=ALU.add),
                                      r=[("tmpA", i), ("tmpB", i)], w=[(nm, j)])
                                else:
                                    V(lambda: nc.vector.tensor_tensor(out=KD[0:64, j, cs], in0=tmpA[i][0:64, :],
                                                                      in1=tmpB[i][0:64, :], op=ALU.add),
                                      r=[("tmpA", i), ("tmpB", i)], w=[(nm, j)])
                                    V(lambda: nc.vector.tensor_tensor(out=KD1[64:128, j, cs], in0=tmpA[i][64:128, :],
                                                                      in1=tmpB[i][64:128, :], op=ALU.add),
                                      r=[("tmpA", i), ("tmpB", i)], w=[(nm, j)])
                    wdv = load(st, "wdv", [128, 8, 512], BF16, D["wdv"])
                    for tt in range(NTT):
                        bk = rot()
                        MM([mm(pb(bk), xT[:, k, tt * 128:(tt + 1) * 128], wdv[:, k, :], k == 0, k == 7)
                            for k in range(8)], r=["wdv", ("xT", tt)], w=[pk(bk)])
                        if tt % 2 == 0:
                            V(lambda: nc.vector.tensor_copy(out=VD[:, tt, :], in_=pb(bk)), w=[pk(bk), "VD"])
                        else:
                            A(lambda: nc.scalar.copy(out=VD[:, tt, :], in_=pb(bk)), w=[pk(bk), "VD"])
                    S.barrier()
                set_rot([], pairs=[0, 2])
                pipe.la = 2
                with Region([(132, 195)]) as st:
                    PT = [sbuf(st, [128, 2, 512], BF16, f"PT{i}") for i in range(4)]
                    a0 = sbuf(st, [128, 512], F32, "a0")
                    a1 = sbuf(st, [128, 512], F32, "a1")
                    r0t = sbuf(st, [128, 512], F32, "r0t")
                    of2 = [sbuf(st, [128, 512], F32, f"of{i}") for i in range(2)]
                    sq2 = [sbuf(st, [128, 512], BF16, f"sq{i}") for i in range(2)]
                    sd = sbuf(st, [128, 512], F32, "sd")
                    pti = [0]

                    def diff_post(c, oacc, dacc):
                        dstA = a0 if c == 0 else a1
                        A(lambda: nc.scalar.activation(out=r0t[:], in_=pb(dacc), func=AF.Ln),
                          w=[pk(dacc), "r0t"])
                        A(lambda: nc.scalar.activation(out=r0t[:], in_=r0t[:], func=AF.Exp, scale=-1.0),
                          w=["r0t"])
                        V(lambda: nc.vector.tensor_tensor(out=dstA[:], in0=pb(oacc), in1=r0t[:], op=ALU.mult),
                          r=["r0t"], w=[pk(oacc), ("a", c)])

                    def diff_final_a(pb_):
                        V(lambda: nc.vector.scalar_tensor_tensor(out=of2[pb_][:], in0=a1[:], scalar=neglam[:, 0:1],
                                                                 in1=a0[:], op0=ALU.mult, op1=ALU.add),
                          r=[("a", 0), ("a", 1), "neglam"], w=[("of", pb_)])
                        A(lambda: nc.scalar.activation(out=sq2[pb_][:], in_=of2[pb_][:], func=AF.Square),
                          r=[("of", pb_)], w=[("sq", pb_)])

                    def diff_final_b(pb_, H, cs):
                        mb = rot()
                        MM([mm(pb(mb), ones[:, :], sq2[pb_][:], True, True)], r=[("sq", pb_), "ones"], w=[pk(mb)])
                        A(lambda: nc.scalar.activation(out=sd[:], in_=pb(mb), func=AF.Ln, scale=1.0 / 128.0,
                                                       bias=eps_rms[:]), r=["eps_rms"], w=[pk(mb), "sd"])
                        A(lambda: nc.scalar.activation(out=sd[:], in_=sd[:], func=AF.Exp, scale=-0.5), w=["sd"])
                        V(lambda: nc.vector.scalar_tensor_tensor(out=YB[:, H, cs], in0=of2[pb_][:],
                                                                 scalar=gcol[:, 0:1], in1=sd[:], op0=ALU.mult,
                                                                 op1=ALU.mult),
                          r=[("of", pb_), "sd", "gcol"], w=[("YB", H)])

                    fin_i = [0]
                    fin_prev = [None]
                    for H in range(4):
                        for Q in range(NQ):
                            cs = slice(Q * 512, (Q + 1) * 512)
                            for c in range(2):
                                r0 = c * 64
                                oacc, dacc = (4, 5) if c == 0 else (6, 7)
                                blocks = [(kb, 0, 512, None, 0) for kb in range(4 * Q)]
                                blocks += [(4 * Q + o, 128 * o, 512, "causal", 128 * o) for o in range(4)]
                                attn_pipe(blocks,
                                          lambda kb, c=c, H=H: (KD if c == 0 else KD1)[:, H, kb * 128:(kb + 1) * 128],
                                          lambda c0, c1, H=H, Q=Q: QD[:, H, Q * 512 + c0:Q * 512 + c1],
                                          lambda kb, H=H: VD[:, kb, H * 128:(H + 1) * 128],
                                          [("KD", H), ("QD", H), "VD", "ident", "tri"], oacc, PT, pti, 128, dacc)
                                pipe.post(lambda c=c, oacc=oacc, dacc=dacc: diff_post(c, oacc, dacc))
                            pb_ = fin_i[0] % 2
                            fin_i[0] += 1
                            if fin_prev[0] is not None:
                                pipe.post(lambda a=fin_prev[0]: diff_final_b(*a))
                            pipe.post(lambda pb_=pb_: diff_final_a(pb_))
                            fin_prev[0] = (pb_, H, cs)
                            if H == 3 and Q == 2:
                                R6a = Region([(154, 178)])
                                wbufs6 = [sbuf(R6a, [128, 8, 128], BF16, f"w6_{i}") for i in range(4)]
                                WA = sbuf(R6a, [128, 4, 1024], BF16, "wa")
                                WB = sbuf(R6a, [128, 4, 1024], BF16, "wb")

                                def ld6a(m_):
                                    j0, j1 = (2 * m_) % 4, (2 * m_ + 1) % 4
                                    S.dma("pool", wbufs6[j0][:], D["wgm"][:, :, m_ * 128:(m_ + 1) * 128],
                                          writes=[("w6", j0)], sem=f"w6_{j0}")
                                    S.dma("pool", wbufs6[j1][:], D["wgm"][:, :, (8 + m_) * 128:(9 + m_) * 128],
                                          writes=[("w6", j1)], sem=f"w6_{j1}")
                                    S.dma("pool", WA[:, :, m_ * 128:(m_ + 1) * 128],
                                          D["wa"][:, :, m_ * 128:(m_ + 1) * 128], writes=[("wa", m_)], sem=f"wa{m_}")
                                    S.dma("pool", WB[:, :, m_ * 128:(m_ + 1) * 128],
                                          D["wb"][:, :, m_ * 128:(m_ + 1) * 128], writes=[("wb", m_)], sem=f"wb{m_}")

                                ld6a(0)
                    pipe.drain()
                    diff_final_b(*fin_prev[0])
                    S.barrier()
            if s == 0:
                dump("d_ya", YA[:].rearrange("p a b -> p (a b)"), ("YA", 0), [128, 4 * SEQ])
                dump("d_yb", YB[:].rearrange("p a b -> p (a b)"), ("YB", 0), [128, 4 * SEQ])
                S.barrier()

            set_rot([0, 1, 2, 3, 4, 5, 6, 7])
            mt_stack = Region([(68, 100)])
            MT = sbuf(mt_stack, [128, 8, SEQ], BF16, "MT")
            with Region([(100, 112)]) as st:
                wbufs = wbufs6
                sg0 = [sbuf(st, [128, 512], BF16, f"sg0{i}") for i in range(2)]
                sg1 = [sbuf(st, [128, 512], BF16, f"sg1{i}") for i in range(2)]
                t0_ = [sbuf(st, [128, 512], F32, f"t0{i}") for i in range(2)]
                t1_ = [sbuf(st, [128, 512], F32, f"t1{i}") for i in range(2)]
                ci = [0]
                for m in range(8):
                    i0, i1 = (2 * m) % 4, (2 * m + 1) % 4
                    if m + 1 < 8:
                        ld6a(m + 1)
                    if m == 4:
                        WO = sbuf(Region([(178, 194)]), [128, 8, 1024], BF16, "wo")
                        for hf in range(2):
                            S.dma("pool", WO[:, :, hf * 512:(hf + 1) * 512], D["wo"][:, :, hf * 512:(hf + 1) * 512],
                                  writes=[("wo", hf)], sem=f"wo{hf}")
                        ln1r = Region([(194, 202)])
                        lng1 = load(ln1r, "lng", [128, DM], F32, D["ln1_g"])
                        lnb1 = load(ln1r, "lnb", [128, DM], F32, D["ln1_b"])
                    for tg in range(NQ):
                        cs = slice(tg * 512, (tg + 1) * 512)
                        i = ci[0] % 2
                        ci[0] += 1
                        b0 = proj_bank(wbufs[i0], ("w6", i0), tg)
                        A(lambda: nc.scalar.activation(out=sg0[i][:], in_=pb(b0), func=AF.Sigmoid),
                          w=[pk(b0), ("sg0", i)])
                        b1_ = proj_bank(wbufs[i1], ("w6", i1), tg)
                        A(lambda: nc.scalar.activation(out=sg1[i][:], in_=pb(b1_), func=AF.Sigmoid),
                          w=[pk(b1_), ("sg1", i)])
                        ba = rot()
                        MM([mm(pb(ba), WA[:, kc, m * 128:(m + 1) * 128], YA[:, kc, cs], kc == 0, kc == 3)
                            for kc in range(4)], r=[("wa", m)] + [("YA", kc) for kc in range(4)], w=[pk(ba)])
                        V(lambda: nc.vector.tensor_tensor(out=t0_[i][:], in0=pb(ba), in1=sg0[i][:], op=ALU.mult),
                          r=[("sg0", i)], w=[pk(ba), ("t0", i)])
                        bb = rot()
                        MM([mm(pb(bb), WB[:, kc, m * 128:(m + 1) * 128], YB[:, kc, cs], kc == 0, kc == 3)
                            for kc in range(4)], r=[("wb", m)] + [("YB", kc) for kc in range(4)], w=[pk(bb)])
                        V(lambda: nc.vector.tensor_tensor(out=t1_[i][:], in0=pb(bb), in1=sg1[i][:], op=ALU.mult),
                          r=[("sg1", i)], w=[pk(bb), ("t1", i)])
                        V(lambda: nc.vector.tensor_tensor(out=MT[:, m, cs], in0=t0_[i][:], in1=t1_[i][:], op=ALU.add),
                          r=[("t0", i), ("t1", i)], w=[("MT", m)])
                S.barrier()
            ya_stack.close()
            seq_stack.close()
            if s == 0:
                dump("d_mt", MT[:].rearrange("p a b -> p (a b)"), ("MT", 0), [128, 8 * SEQ])
                S.barrier()

            ffn_stack = Region([(100, 164)])
            z2acc = sbuf(ffn_stack, [128, NTT, DM], F32, "z2acc")
            h1T = sbuf(Region([(4, 36)]), [128, 8, SEQ], BF16, "h1T")

            def layer_norm(zt, zkey, gam, bet, outt, okey, mv, rstd, st6):
                V(lambda: nc.vector.bn_stats(out=st6[:, 0, :], in_=zt[:, 0:512]), r=[zkey], w=["st6"])
                V(lambda: nc.vector.bn_stats(out=st6[:, 1, :], in_=zt[:, 512:1024]), r=[zkey], w=["st6"])
                V(lambda: nc.vector.bn_aggr(out=mv[:], in_=st6[:]), r=["st6"], w=["mv"])
                A(lambda: nc.scalar.activation(out=rstd[:], in_=mv[:, 1:2], func=AF.Sqrt, bias=eps_ln[:]),
                  r=["mv", "eps_ln"], w=["rstd"])
                V(lambda: nc.vector.reciprocal(out=rstd[:], in_=rstd[:]), w=["rstd"])
                V(lambda: nc.vector.tensor_scalar(out=outt[:], in0=zt[:], scalar1=mv[:, 0:1], scalar2=rstd[:, 0:1],
                                                  op0=ALU.subtract, op1=ALU.mult), r=[zkey, "mv", "rstd"], w=[okey])
                V(lambda: nc.vector.tensor_tensor(out=outt[:], in0=outt[:], in1=gam[:], op=ALU.mult),
                  r=["lng"], w=[okey])
                V(lambda: nc.vector.tensor_tensor(out=outt[:], in0=outt[:], in1=bet[:], op=ALU.add),
                  r=["lnb"], w=[okey])

            wgub = [sbuf(Region([(164 + 4 * i, 168 + 4 * i)]), [128, 8, 256], BF16, f"wgub{i}") for i in range(3)]
            with Region([(36, 68), (176, 178)]) as st:
                lng, lnb = lng1, lnb1
                for j_ in range(3):
                    S.dma("pool", wgub[j_][:], D["wgu"][:, :, j_ * 256:(j_ + 1) * 256], writes=[("wgub", j_)],
                          sem=f"wgub{j_}")
                xt = [sbuf(st, [128, DM], F32, f"xt{i}") for i in range(2)]
                zt = [sbuf(st, [128, DM], F32, f"zt{i}") for i in range(2)]
                ht = [sbuf(st, [128, DM], F32, f"ht{i}") for i in range(2)]
                hbf = [sbuf(st, [128, DM], BF16, f"hbf{i}") for i in range(2)]
                st6 = sbuf(st, [128, 2, 6], F32, "st6")
                mv = sbuf(st, [128, 2], F32, "mv")
                rstd = sbuf(st, [128, 1], F32, "rstd")
                pairs = [(0, 1), (2, 3), (4, 5)]

                def tail6b(t_):
                    i_ = t_ % 2
                    A(lambda: nc.scalar.mul(out=z2acc[:, t_, :], in_=ht[i_][:], mul=ALPHA), r=[("ht", i_)],
                      w=[("z2", t_)])
                    A(lambda: nc.scalar.copy(out=hbf[i_][:], in_=ht[i_][:]), r=[("ht", i_)], w=[("hbf", i_)])
                    bk_ = 6 + (t_ % 2)
                    MM([(lambda k=k: nc.tensor.transpose(out=pbf(bk_)[:, k * 128:(k + 1) * 128],
                                                          in_=hbf[i_][:, k * 128:(k + 1) * 128], identity=ident[:]))
                        for k in range(8)], r=[("hbf", i_), "ident"], w=[pk(bk_)])
                    A(lambda: nc.scalar.copy(out=h1T[:, :, t_ * 128:(t_ + 1) * 128],
                                             in_=pbf(bk_).rearrange("p (k t) -> p k t", k=8)),
                      w=[pk(bk_), ("h1T", t_)])
                for tt in range(NTT):
                    i = tt % 2
                    S.dma("sp", xt[i][:], x_s[tt * 128:(tt + 1) * 128, :], writes=[("xt", i)], sem=f"xt{i}")
                    b0, b1_ = pairs[tt % 3]
                    MM([mm(pb(b0), MT[:, kc, tt * 128:(tt + 1) * 128], WO[:, kc, 0:512], kc == 0, kc == 7)
                        for kc in range(8)], r=[("wo", 0)] + [("MT", m) for m in range(8)], w=[pk(b0)])
                    MM([mm(pb(b1_), MT[:, kc, tt * 128:(tt + 1) * 128], WO[:, kc, 512:1024], kc == 0, kc == 7)
                        for kc in range(8)], r=[("wo", 1)] + [("MT", m) for m in range(8)], w=[pk(b1_)])
                    V(lambda: nc.vector.scalar_tensor_tensor(out=zt[i][:].rearrange("p (a b) -> p a b", a=2),
                                                             in0=xt[i][:].rearrange("p (a b) -> p a b", a=2),
                                                             scalar=ALPHA, in1=ps[:, b0:b0 + 2, :],
                                                             op0=ALU.mult, op1=ALU.add),
                      r=[("xt", i)], w=[pk(b0), pk(b1_), ("zt", i)])
                    if tt > 0:
                        tail6b(tt - 1)
                    layer_norm(zt[i], ("zt", i), lng, lnb, ht[i], ("ht", i), mv, rstd, st6)
                tail6b(NTT - 1)
                S.barrier()
            mt_stack.close()
            if s == 0:
                dump("d_z2", z2acc[:].rearrange("p a b -> p (a b)"), ("z2", 0), [128, NTT * DM], is_bf=False)
                S.barrier()

            set_rot([0, 1, 2, 3, 4, 5])
            with Region([(36, 100), (176, 203)]) as st:
                lng = load(st, "lng2", [128, DM], F32, D["ln2_g"], key="lng")
                lnb = load(st, "lnb2", [128, DM], F32, D["ln2_b"], key="lnb")
                AT = sbuf(st, [128, 11, SEQ], BF16, "AT")
                WDN = sbuf(st, [128, 11, 1024], BF16, "WDN")
                sgt = [sbuf(st, [128, 512], BF16, f"sgt{i}") for i in range(2)]
                st6 = sbuf(st, [128, 2, 6], F32, "st6")
                mv = sbuf(st, [128, 2], F32, "mv")
                rstd = sbuf(st, [128, 1], F32, "rstd")
                st6b = [sbuf(st, [128, 2, 6], F32, f"st6b{i}") for i in range(2)]
                mvb = [sbuf(st, [128, 4], F32, f"mvb{i}") for i in range(2)]
                ln2_pending = []
                gi = [0]
                si = [0]
                for half in range(2):
                    for jj in range(11):
                        j = half * 11 + jj
                        wi = gi[0] % 3
                        gi[0] += 1
                        if j >= 3:
                            S.dma("pool", wgub[wi][:], D["wgu"][:, :, j * 256:(j + 1) * 256], writes=[("wgub", wi)],
                                  sem=f"wgub{wi}")
                        if jj == 2:
                            for j2 in range(11):
                                S.dma("pool", WDN[:, j2, :], D["wdn"][:, half * 11 + j2, :],
                                      reads=[], writes=[("WDN", j2)], sem=f"wdn{j2}")
                        for tg in range(NQ):
                            cs = slice(tg * 512, (tg + 1) * 512)
                            hk = [("h1T", tg * 4 + q) for q in range(4)]
                            bg = rot()
                            MM([mm(pb(bg), wgub[wi][:, k, 0:128], h1T[:, k, cs], k == 0, k == 7) for k in range(8)],
                               r=[("wgub", wi)] + hk, w=[pk(bg)])
                            bu = rot()
                            MM([mm(pb(bu), wgub[wi][:, k, 128:256], h1T[:, k, cs], k == 0, k == 7) for k in range(8)],
                               r=[("wgub", wi)] + hk, w=[pk(bu)])
                            i = si[0] % 2
                            si[0] += 1
                            A(lambda: nc.scalar.activation(out=sgt[i][:], in_=pb(bg), func=AF.Silu),
                              w=[pk(bg), ("sgt", i)])
                            V(lambda: nc.vector.tensor_tensor(out=AT[:, jj, cs], in0=pb(bu), in1=sgt[i][:],
                                                              op=ALU.mult), r=[("sgt", i)], w=[pk(bu), ("AT", jj, tg)])
                    for tt in range(NTT):
                        b0 = [0, 2, 4, 6][tt % 4]
                        ak = [("AT", jj, tt // 4) for jj in range(11)]
                        wk = [("WDN", jj) for jj in range(11)]
                        MM([mm(pb(b0), AT[:, jj, tt * 128:(tt + 1) * 128], WDN[:, jj, 0:512], jj == 0, jj == 10)
                            for jj in range(11)], r=ak + wk, w=[pk(b0)])
                        MM([mm(pb(b0 + 1), AT[:, jj, tt * 128:(tt + 1) * 128], WDN[:, jj, 512:1024], jj == 0, jj == 10)
                            for jj in range(11)], r=ak + wk, w=[pk(b0 + 1)])
                        V(lambda: nc.vector.tensor_tensor(out=z2acc[:, tt, :].rearrange("p (a b) -> p a b", a=2),
                                                          in0=ps[:, b0:b0 + 2, :],
                                                          in1=z2acc[:, tt, :].rearrange("p (a b) -> p a b", a=2),
                                                          op=ALU.add), w=[pk(b0), pk(b0 + 1), ("z2", tt)])
                        if half == 1:
                            i = tt % 2
                            zt_ = z2acc[:, tt, :]
                            zk = ("z2", tt)
                            V(lambda: nc.vector.bn_stats(out=st6b[i][:, 0, :], in_=zt_[:, 0:512]), r=[zk],
                              w=[("st6b", i)])
                            V(lambda: nc.vector.bn_stats(out=st6b[i][:, 1, :], in_=zt_[:, 512:1024]), r=[zk],
                              w=[("st6b", i)])
                            V(lambda: nc.vector.bn_aggr(out=mvb[i][:, 0:2], in_=st6b[i][:]), r=[("st6b", i)],
                              w=[("mvb", i)])
                            A(lambda: nc.scalar.activation(out=mvb[i][:, 2:3], in_=mvb[i][:, 1:2], func=AF.Sqrt,
                                                           bias=eps_ln[:]), r=["eps_ln"], w=[("mvb", i)])
                            V(lambda: nc.vector.reciprocal(out=mvb[i][:, 2:3], in_=mvb[i][:, 2:3]), w=[("mvb", i)])
                            V(lambda: nc.vector.tensor_scalar(out=mvb[i][:, 3:4], in0=mvb[i][:, 0:1], scalar1=-1.0,
                                                              scalar2=mvb[i][:, 2:3], op0=ALU.mult, op1=ALU.mult),
                              w=[("mvb", i)])
                            A(lambda: nc.scalar.activation(out=zt_, in_=zt_, func=AF.Identity, scale=mvb[i][:, 2:3],
                                                           bias=mvb[i][:, 3:4]), r=[("mvb", i)], w=[zk])
                            P(lambda: nc.gpsimd.tensor_tensor(out=zt_, in0=zt_, in1=lng[:], op=ALU.mult),
                              r=["lng"], w=[zk])

                            def ln2_tail(t_=tt):
                                V(lambda: nc.vector.tensor_tensor(out=z2acc[:, t_, :], in0=z2acc[:, t_, :], in1=lnb[:],
                                                                  op=ALU.add), r=["lnb"], w=[("z2", t_)])
                                S.dma("sp", out_s[t_ * 128:(t_ + 1) * 128, :], z2acc[:, t_, :], reads=[("z2", t_)],
                                      writes=[("outd", t_ % 2)], sem=f"out{t_ % 2}")
                            if ln2_pending:
                                ln2_pending.pop(0)()
                            ln2_pending.append(ln2_tail)
                while ln2_pending:
                    ln2_pending.pop(0)()
                S.barrier()
            ffn_stack.close()
        S.barrier(engines=("sp",))
        print("kernel build: n_ins", S.n_ins, "n_wait", S.n_wait, "sems", len(S.sems))
    return nc


_NC_CACHE = {}


def kernel(**inputs):
    shared = prep_shared(inputs)
    x = np.asarray(inputs["x"], np.float32)
    if "nc" not in _NC_CACHE:
        _NC_CACHE["nc"] = build_nc()
    nc = _NC_CACHE["nc"]
    in_maps = []
    for c in range(NCORES):
        m = dict(shared)
        m["x"] = np.ascontiguousarray(x[NSEQ * c:NSEQ * (c + 1)].reshape(NSEQ * SEQ, DM))
        m["xT"] = make_xT(x[NSEQ * c:NSEQ * (c + 1)])
        in_maps.append(m)
    res = run_bass_kernel_spmd(nc, in_maps, core_ids=list(range(NCORES)))
    out = np.concatenate([np.asarray(r["out"], np.float32).reshape(NSEQ, SEQ, DM) for r in res.results], axis=0)
    return out
```
